# Optimizing a Trainium2 kernel written in Bass

```python
import jax, jax.numpy as jnp
from jax import lax
import numpy as np

D_MODEL = 1024
BATCH = 8
SEQ = 2048
DEPTH = 1

CTX_LEN = 256
GRID_W = 64
D_CONV = D_MODEL
CONV_K = 3
D_LSTM = D_MODEL
N_HEADS = 8
HEAD_DIM = D_LSTM // N_HEADS
CHUNK = 64
EPS = 1e-6
F_BIAS_LO = 3.0
F_BIAS_HI = 6.0
STATE_COLS = 3 * D_LSTM + 4 * N_HEADS
N_IN = STATE_COLS + 2 * D_LSTM + 4 * D_CONV + 2 * D_MODEL

kernel_name = "hybrid_conv_mlstm_prefix_block"


def _split_sizes(a, sizes):
    idx, acc = [], 0
    for s in sizes[:-1]:
        acc += s
        idx.append(acc)
    return jnp.split(a, idx, axis=-1)


def rmsnorm(a, g):
    af = a.astype(jnp.float32)
    y = af * lax.rsqrt(jnp.mean(af * af, axis=-1, keepdims=True) + EPS) * g.astype(jnp.float32)
    return y.astype(a.dtype)


def dwconv3(u, w, b):
    up = jnp.pad(u, [(0, 0)] * (u.ndim - 2) + [(1, 1), (0, 0)])
    return w[0] * up[..., :-2, :] + w[1] * up[..., 1:-1, :] + w[2] * up[..., 2:, :] + b


def mlstm_scan(q, k, v, i_pre, logf, state, emit):
    bsz, nh, length, dh = q.shape
    nc = length // CHUNK

    def chunks(a):
        return jnp.moveaxis(a.reshape(bsz, nh, nc, CHUNK, *a.shape[3:]), 2, 0)

    tri = jnp.tril(jnp.ones((CHUNK, CHUNK), dtype=bool))

    def step(carry, inp):
        C, n, m = carry
        qc, kc, vc, ic, fc = inp
        b = jnp.cumsum(fc, axis=-1)
        b_last = b[..., -1]
        w_log = b_last[..., None] - b + ic
        m_new = jnp.maximum(b_last + m, w_log.max(-1))
        wk = jnp.exp(w_log - m_new[..., None])
        decay = jnp.exp(b_last + m - m_new)
        C_new = decay[..., None, None] * C + jnp.einsum('bhj,bhjd,bhje->bhde', wk, kc, vc)
        n_new = decay[..., None] * n + jnp.einsum('bhj,bhjd->bhd', wk, kc)
        if not emit:
            return (C_new, n_new, m_new), None
        log_w = jnp.where(tri, b[..., :, None] - b[..., None, :] + ic[..., None, :], -jnp.inf)
        log_inter = b + m[..., None]
        m_row = jnp.maximum(log_w.max(-1), log_inter)
        s = jnp.einsum('bhid,bhjd->bhij', qc, kc) * jnp.exp(log_w - m_row[..., None])
        w_inter = jnp.exp(log_inter - m_row)
        num = jnp.einsum('bhij,bhje->bhie', s, vc) + w_inter[..., None] * jnp.einsum('bhid,bhde->bhie', qc, C)
        den = s.sum(-1) + w_inter * jnp.einsum('bhid,bhd->bhi', qc, n)
        den = jnp.maximum(jnp.abs(den), jnp.exp(-m_row))
        return (C_new, n_new, m_new), num / den[..., None]

    state, hs = lax.scan(step, state, tuple(chunks(a) for a in (q, k, v, i_pre, logf)))
    if not emit:
        return None, state
    return jnp.moveaxis(hs, 0, 2).reshape(bsz, nh, length, dh), state


def zero_state(bsz):
    return (jnp.zeros((bsz, N_HEADS, HEAD_DIM, HEAD_DIM), jnp.float32),
            jnp.zeros((bsz, N_HEADS, HEAD_DIM), jnp.float32),
            jnp.zeros((bsz, N_HEADS), jnp.float32))


def mlstm_bidir(p_state, states, emit):
    q, k, v, gates = _split_sizes(p_state, [D_LSTM, D_LSTM, D_LSTM, 4 * N_HEADS])

    def heads(a):
        return a.reshape(a.shape[0], a.shape[1], N_HEADS, HEAD_DIM).transpose(0, 2, 1, 3).astype(jnp.float32)

    qh = heads(q) * HEAD_DIM ** -0.5
    kh, vh = heads(k), heads(v)
    g = gates.astype(jnp.float32).transpose(0, 2, 1)
    i_f, f_f, i_b, f_b = jnp.split(g, 4, axis=1)
    h_f, st_f = mlstm_scan(qh, kh, vh, i_f, jax.nn.log_sigmoid(f_f), states[0], emit)
    rev = lambda a: jnp.flip(a, axis=2)
    h_b, st_b = mlstm_scan(rev(qh), rev(kh), rev(vh), rev(i_b), rev(jax.nn.log_sigmoid(f_b)), states[1], emit)
    if not emit:
        return None, (st_f, st_b)
    return h_f + rev(h_b), (st_f, st_b)


def mixer_output(h_lstm, p_rest, grid, w_conv, b_conv, g_head, w_pa, w_pb, w_out, b_out):
    o, z_b, b_a, c_a, x_a, z_a, g_a, g_b = _split_sizes(
        p_rest, [D_LSTM, D_LSTM, D_CONV, D_CONV, D_CONV, D_CONV, D_MODEL, D_MODEL])
    u = c_a * x_a
    if grid:
        bsz, length, ch = u.shape
        conv = dwconv3(u.reshape(bsz, length // GRID_W, GRID_W, ch), w_conv, b_conv).reshape(bsz, length, ch)
    else:
        conv = dwconv3(u, w_conv, b_conv)
    y_a = b_a * conv * jax.nn.silu(z_a)
    hn = h_lstm * lax.rsqrt(jnp.mean(h_lstm * h_lstm, axis=-1, keepdims=True) + EPS)
    bsz, _, length, _ = hn.shape
    hn = (hn.transpose(0, 2, 1, 3).reshape(bsz, length, D_LSTM) * g_head.astype(jnp.float32)).astype(p_rest.dtype)
    y_b = jax.nn.sigmoid(o) * hn * jax.nn.silu(z_b)
    merged = jax.nn.sigmoid(g_a) * (y_a @ w_pa) + jax.nn.sigmoid(g_b) * (y_b @ w_pb)
    return merged @ w_out + b_out


def setup_inputs(seed: int = 0) -> dict:
    key = jax.random.key(seed)
    ks = jax.random.split(key, 17)
    nrm = lambda k, shape, s: jax.random.normal(k, shape, jnp.float32) * s
    b_in = nrm(ks[8], (DEPTH, N_IN), 0.02)
    f_bias = jnp.linspace(F_BIAS_LO, F_BIAS_HI, N_HEADS, dtype=jnp.float32)
    og = 3 * D_LSTM
    b_in = b_in.at[:, og + N_HEADS:og + 2 * N_HEADS].add(f_bias)
    b_in = b_in.at[:, og + 3 * N_HEADS:og + 4 * N_HEADS].add(f_bias)
    return {
        "x": nrm(ks[0], (BATCH, SEQ, D_MODEL), 1.0),
        "c": nrm(ks[1], (BATCH, D_MODEL), 1.0),
        "ctx": nrm(ks[2], (BATCH, CTX_LEN, D_MODEL), 1.0),
        "c_ctx": nrm(ks[3], (D_MODEL,), 1.0),
        "w_ada": nrm(ks[4], (DEPTH, D_MODEL, 3 * D_MODEL), 0.5 * D_MODEL ** -0.5),
        "b_ada": nrm(ks[5], (DEPTH, 3 * D_MODEL), 0.02),
        "g_norm": 1.0 + nrm(ks[6], (DEPTH, D_MODEL), 0.02),
        "w_in": nrm(ks[7], (DEPTH, D_MODEL, N_IN), D_MODEL ** -0.5),
        "b_in": b_in,
        "w_conv": nrm(ks[9], (DEPTH, CONV_K, D_CONV), CONV_K ** -0.5),
        "b_conv": nrm(ks[10], (DEPTH, D_CONV), 0.02),
        "g_head": 1.0 + nrm(ks[11], (DEPTH, D_LSTM), 0.02),
        "w_pa": nrm(ks[12], (DEPTH, D_CONV, D_MODEL), D_CONV ** -0.5),
        "w_pb": nrm(ks[13], (DEPTH, D_LSTM, D_MODEL), D_LSTM ** -0.5),
        "w_out": nrm(ks[14], (DEPTH, D_MODEL, D_MODEL), D_MODEL ** -0.5),
        "b_out": nrm(ks[15], (DEPTH, D_MODEL), 0.02),
        "g_final": 1.0 + nrm(ks[16], (D_MODEL,), 0.02),
    }


def reference(x, c, ctx, c_ctx, w_ada, b_ada, g_norm, w_in, b_in, w_conv, b_conv, g_head,
              w_pa, w_pb, w_out, b_out, g_final):
    bsz = x.shape[0]
    for l in range(DEPTH):
        last = l == DEPTH - 1
        shift, scale, gate = jnp.split(jax.nn.silu(c) @ w_ada[l] + b_ada[l], 3, axis=-1)
        shift_c, scale_c, gate_c = jnp.split(jax.nn.silu(c_ctx) @ w_ada[l] + b_ada[l], 3, axis=-1)
        hx = rmsnorm(x, g_norm[l]) * (1 + scale[:, None, :]) + shift[:, None, :]
        hc = rmsnorm(ctx, g_norm[l]) * (1 + scale_c) + shift_c
        px = hx @ w_in[l] + b_in[l]
        n_ctx_cols = STATE_COLS if last else N_IN
        pc = hc @ w_in[l, :, :n_ctx_cols] + b_in[l, :n_ctx_cols]
        h_ctx, ctx_states = mlstm_bidir(pc[..., :STATE_COLS], (zero_state(bsz), zero_state(bsz)), not last)
        h_x, _ = mlstm_bidir(px[..., :STATE_COLS], ctx_states, True)
        out_x = mixer_output(h_x, px[..., STATE_COLS:], True, w_conv[l], b_conv[l], g_head[l],
                             w_pa[l], w_pb[l], w_out[l], b_out[l])
        if not last:
            out_c = mixer_output(h_ctx, pc[..., STATE_COLS:], False, w_conv[l], b_conv[l], g_head[l],
                                 w_pa[l], w_pb[l], w_out[l], b_out[l])
            ctx = ctx + gate_c * out_c
        x = x + gate[:, None, :] * out_x
    return rmsnorm(x, g_final)
```

```python
import contextlib
import numpy as np
import concourse.bass as bass
import concourse.mybir as mybir
from concourse.bass_utils import run_bass_kernel_spmd

F32 = mybir.dt.float32
BF16 = mybir.dt.bfloat16
AF = mybir.ActivationFunctionType
ALU = mybir.AluOpType

D = 1024
SEQ = 2048
CTX = 256
NT = 18
NLT = 16
TOK = NT * 128
NH = 8
N_IN = 11296
EPS = 1e-6
QS = 128 ** -0.5


class _Probe:
    def __init__(self):
        self.rec = None

    def __getattr__(self, name):
        def f(*a, **k):
            self.rec = (name, a, k)
            return None
        return f


def _free(ap):
    n = 1
    for v in ap.shape[1:]:
        n *= v
    return n


def _cost(eng, fn, dma):
    pr = _Probe()
    fn(pr)
    name, a, k = pr.rec
    if dma:
        out = k.get("out", a[0] if a else None)
        nbytes = _free(out) * out.shape[0] * (2 if out.dtype == BF16 else 4)
        issue = 1100.0 if eng == "pool" else 100.0
        return issue, issue + 2000.0 + nbytes / 250.0
    if eng == "pe":
        if name == "transpose":
            t = 128 / 2.4 + 3
        else:
            rhs = k.get("rhs", a[2] if len(a) > 2 else None)
            n = _free(rhs)
            t = max(n, 56) / 2.4 + 3
            if rhs.dtype == F32:
                t *= 4
        return t, t + 170.0
    out = k.get("out", a[0] if a else None)
    n = _free(out) if out is not None else 128
    if eng == "act":
        t = (170.0 + n * 1.8) if n <= 128 else (300.0 + n * 0.74)
        if k.get("accum_out") is not None:
            t += 190
    elif eng == "dve":
        t = 200.0 + n * 1.05
    else:
        if name == "tensor_tensor":
            t = 180.0 + n * 2.0
            if k.get("op") == ALU.pow:
                t += 2700
        elif name == "tensor_copy":
            t = 340.0 + n * 2.0
        else:
            t = 250.0 + n * 1.0
    return t, t + 60.0


class _Op:
    __slots__ = ("eng", "fn", "deps", "dma", "idx", "pos", "phase", "busy", "lat", "stt", "vc", "waits")


class Prog:
    ENGS = ("pe", "act", "dve", "pool", "sp")
    KDMA = 16
    SYNC = 180.0

    def __init__(self, nc, reorder=True):
        self.nc = nc
        self.ops = []
        self.last_w = {}
        self.readers = {}
        self.phase = 0
        self.reorder = reorder
        self.after = set()

    def _add(self, eng, fn, reads, writes, dma):
        norm = lambda k: k[:2] if (isinstance(k, tuple) and k[0] == "ps") else k
        reads = tuple(dict.fromkeys(norm(k) for k in reads))
        writes = tuple(dict.fromkeys(norm(k) for k in writes))
        writes = writes + tuple(k for k in reads if isinstance(k, tuple) and k[0] == "ps" and k not in writes)
        op = _Op()
        op.eng, op.fn, op.dma, op.idx, op.phase, op.pos = eng, fn, dma, len(self.ops), self.phase, None
        op.busy, op.lat = _cost(eng, fn, dma)
        deps = set()
        for k in reads:
            if k in self.last_w:
                deps.add(self.last_w[k])
        for k in writes:
            if k in self.last_w:
                deps.add(self.last_w[k])
            for r in self.readers.get(k, ()):
                deps.add(r)
        deps |= self.after
        deps.discard(op)
        op.deps = deps
        for k in reads:
            self.readers.setdefault(k, []).append(op)
        for k in writes:
            self.last_w[k] = op
            self.readers[k] = []
        self.ops.append(op)
        return op

    def op(self, eng, fn, reads=(), writes=()):
        return self._add(eng, fn, tuple(reads), tuple(writes), False)

    def dma(self, eng, fn, reads=(), writes=()):
        return self._add(eng, fn, tuple(reads), tuple(writes), True)

    def keys_named(self, names):
        ks = set(self.last_w) | set(self.readers)
        return [k for k in ks if (k if isinstance(k, str) else k[0]) in names]

    def fence_all_later(self, op):
        self.after.add(op)

    def barrier(self):
        self.after = set()
        self.phase += 1
        self.last_w = {}
        self.readers = {}

    def _schedule(self, ops):
        import heapq
        order = {e: [] for e in self.ENGS}
        if not self.reorder:
            for o in ops:
                o.stt = float(o.idx)
                order[o.eng].append(o)
            return order
        inphase = set(ops)
        succs = {o: [] for o in ops}
        indeg = {}
        for o in ops:
            ds = [d for d in o.deps if d in inphase]
            indeg[o] = len(ds)
            for d in ds:
                succs[d].append(o)
        tail = {}
        for o in reversed(ops):
            t_ = 0.0
            for su in succs[o]:
                t_ = max(t_, tail[su] + (30.0 if su.eng == o.eng else self.SYNC))
            tail[o] = o.lat + t_
        pk = lambda o: (-tail[o], o.idx)
        ready = {o: 0.0 for o in ops}
        fut = {e: [] for e in self.ENGS}
        now = {e: [] for e in self.ENGS}
        for o in ops:
            if indeg[o] == 0:
                heapq.heappush(fut[o.eng], (0.0, o.idx, o))
        free = {e: 0.0 for e in self.ENGS}
        dfin = {e: [] for e in self.ENGS}
        KD = self.KDMA
        left = len(ops)
        while left:
            best = None
            for e in self.ENGS:
                while fut[e] and fut[e][0][0] <= free[e]:
                    rt, idx, o = heapq.heappop(fut[e])
                    heapq.heappush(now[e], (pk(o), o.idx, o))
                if now[e]:
                    _, idx, o = now[e][0]
                    stt = free[e]
                elif fut[e]:
                    rt, idx, o = fut[e][0]
                    stt = max(rt, free[e])
                else:
                    continue
                if o.dma and len(dfin[e]) >= KD:
                    stt = max(stt, dfin[e][-KD])
                if best is None or (stt, pk(o)) < (best[0], pk(best[3])):
                    best = (stt, idx, e, o)
            stt, idx, e, o = best
            if now[e] and now[e][0][2] is o:
                heapq.heappop(now[e])
            else:
                heapq.heappop(fut[e])
            free[e] = stt + o.busy
            o.stt = stt
            fin = stt + o.lat
            if o.dma:
                dfin[e].append(fin)
            order[e].append(o)
            left -= 1
            for su in succs[o]:
                lat = 30.0 if (su.eng == o.eng) else self.SYNC
                ready[su] = max(ready[su], fin + lat)
                indeg[su] -= 1
                if indeg[su] == 0:
                    heapq.heappush(fut[su.eng], (ready[su], su.idx, su))
        return order

    def emit(self, final_wait_eng="sp"):
        nc = self.nc
        nph = self.phase + 1
        phases = [[] for _ in range(nph)]
        for o in self.ops:
            phases[o.phase].append(o)
        streams = {e: [] for e in self.ENGS}
        for ph in range(nph):
            od = self._schedule(phases[ph])
            for e in self.ENGS:
                streams[e].extend(od[e])
        cnt = {e: 0 for e in self.ENGS}
        dcnt = {e: 0 for e in self.ENGS}
        last_c = [dict() for _ in range(nph)]
        last_d = [dict() for _ in range(nph)]
        for e in self.ENGS:
            for o in streams[e]:
                if o.dma:
                    dcnt[e] += 1
                    o.pos = dcnt[e]
                    last_d[o.phase][e] = dcnt[e]
                else:
                    cnt[e] += 1
                    o.pos = cnt[e]
                    last_c[o.phase][e] = cnt[e]
        cum_c = [dict() for _ in range(nph + 1)]
        cum_d = [dict() for _ in range(nph + 1)]
        for ph in range(nph):
            cum_c[ph + 1] = dict(cum_c[ph]); cum_c[ph + 1].update(last_c[ph])
            cum_d[ph + 1] = dict(cum_d[ph]); cum_d[ph + 1].update(last_d[ph])
        self.cnt, self.dcnt = cnt, dcnt
        KD = self.KDMA

        def dkey(p, n):
            return ("d", p, (n - 1) % KD), 16 * ((n - 1) // KD + 1)

        know = {e: {} for e in self.ENGS}
        cur_ph = {e: 0 for e in self.ENGS}

        def need(kn, waits, key, val, vc):
            if kn.get(key, 0) >= val:
                return
            waits.append((key, val))
            kn[key] = val
            if vc:
                for k2, v2 in vc.items():
                    if kn.get(k2, 0) < v2:
                        kn[k2] = v2

        def barrier_waits(E, kn, waits, ph):
            for p, n in cum_c[ph].items():
                if not (p == E == "pe"):
                    need(kn, waits, ("c", p), n, None)
            for p, n in cum_d[ph].items():
                for m in range(max(1, n - KD + 1), n + 1):
                    k_, v_ = dkey(p, m)
                    need(kn, waits, k_, v_, None)

        for o in sorted(self.ops, key=lambda o: (o.phase, o.stt, o.idx)):
            E = o.eng
            kn = know[E]
            waits = []
            if o.phase != cur_ph[E]:
                cur_ph[E] = o.phase
                barrier_waits(E, kn, waits, o.phase)
            deps = [d for d in o.deps if d.phase == o.phase]
            deps.sort(key=lambda d: (-d.stt, d.idx))
            for d in deps:
                if d.dma:
                    k_, v_ = dkey(d.eng, d.pos)
                elif d.eng == E == "pe":
                    continue
                else:
                    k_, v_ = ("c", d.eng), d.pos
                need(kn, waits, k_, v_, d.vc)
            if o.dma and o.pos > KD:
                k_, v_ = dkey(E, o.pos - KD)
                need(kn, waits, k_, v_, None)
            o.waits = waits
            o.vc = dict(kn)
        final_waits = []
        barrier_waits(final_wait_eng, know[final_wait_eng], final_waits, nph)

        with contextlib.ExitStack() as st:
            csem = {e: st.enter_context(nc.semaphore("c_" + e)) for e in self.ENGS if cnt[e]}
            dsem = {e: [st.enter_context(nc.semaphore("d_%s%d" % (e, i))) for i in range(KD)]
                    for e in self.ENGS if dcnt[e]}
            block = st.enter_context(nc.Block())

            def body(ename, eng):
                def do_wait(key, val):
                    if key[0] == "c":
                        eng.wait_ge(csem[key[1]], val)
                    else:
                        eng.wait_ge(dsem[key[1]][key[2]], val)

                for o in streams[ename]:
                    for (key, val) in o.waits:
                        do_wait(key, val)
                    ins = o.fn(eng)
                    if o.dma:
                        ins.then_inc(dsem[ename][(o.pos - 1) % KD], 16)
                    else:
                        ins.then_inc(csem[ename], 1)
                if ename == final_wait_eng:
                    for (key, val) in final_waits:
                        do_wait(key, val)

            @block.tensor
            def _(e):
                body("pe", e)

            @block.scalar
            def _(e):
                body("act", e)

            @block.vector
            def _(e):
                body("dve", e)

            @block.gpsimd
            def _(e):
                body("pool", e)

            @block.sync
            def _(e):
                body("sp", e)


def _prod(s):
    r = 1
    for v in s:
        r *= v
    return r


def build_program(stop_after=99, dbg=None):
    nc = bass.Bass("TRN2", target_bir_lowering=False)

    def din(name, shape):
        return nc.dram_tensor(name, list(shape), F32, kind="ExternalInput").ap()

    x = din("x", [SEQ, D])
    ctx = din("ctx", [CTX, D])
    c_in = din("c", [D])
    cctx_in = din("c_ctx", [D])
    w_ada = din("w_ada", [D, 3 * D])
    b_ada = din("b_ada", [3 * D])
    g_norm = din("g_norm", [D])
    w_in = din("w_in", [D, N_IN])
    b_in = din("b_in", [N_IN])
    w_conv = din("w_conv", [3, D])
    b_conv = din("b_conv", [D])
    g_head = din("g_head", [D])
    w_pa = din("w_pa", [D, D])
    w_pb = din("w_pb", [D, D])
    w_out = din("w_out", [D, D])
    b_out = din("b_out", [D])
    g_final = din("g_final", [D])
    y = nc.dram_tensor("y", [SEQ, D], F32, kind="ExternalOutput").ap()
    dbg_out = {}
    if dbg:
        for name, shape in dbg.items():
            dbg_out[name] = nc.dram_tensor("dbg_" + name, [128, _prod(shape)], F32, kind="ExternalOutput").ap()

    ARENA = 210944
    with contextlib.ExitStack() as st:
        arena = st.enter_context(nc.sbuf_tensor("arena", [128, ARENA // 2], BF16))
        psum = st.enter_context(nc.psum_tensor("psum", [128, 4096], F32))
        P = Prog(nc)

        cur = [0]

        def V(off, shape, dt):
            n = _prod(shape)
            sz = 2 if dt == BF16 else 4
            assert off % 4 == 0 and off + n * sz <= ARENA, (off, shape)
            a = arena[:, off // 2: off // 2 + n * sz // 2]
            if dt != BF16:
                a = a.bitcast(dt)
            if len(shape) == 2:
                a = a.rearrange("p (a b) -> p a b", a=shape[0])
            elif len(shape) == 3:
                a = a.rearrange("p (a b c) -> p a b c", a=shape[0], b=shape[1])
            return a

        def alloc(shape, dt):
            n = _prod(shape) * (2 if dt == BF16 else 4)
            n = (n + 31) // 32 * 32
            off = cur[0]
            cur[0] += n
            return V(off, shape, dt)

        def PS(bank, off_f32, n_f32):
            return psum[:, bank * 512 + off_f32: bank * 512 + off_f32 + n_f32]

        def PSB(bank, off_bf, n_bf):
            return psum[:, bank * 512: (bank + 1) * 512].bitcast(BF16)[:, off_bf: off_bf + n_bf]

        ident_bf = alloc([128], BF16)
        ident_f = alloc([128], F32)
        M_le = alloc([128], F32)
        M_lt = alloc([128], F32)
        M_ge = alloc([128], F32)
        M_gt = alloc([128], F32)
        ones_f = alloc([128], F32)
        M_ge_bf = alloc([128], BF16)
        pp = alloc([128], F32)
        pp2 = alloc([40], F32)
        dp = alloc([64], F32)
        s_f = alloc([16], F32)
        s_t = alloc([16], F32)
        s2 = alloc([16], BF16)
        gts = alloc([NT, 32], F32)
        lf = alloc([NT, 16], F32)
        ea = alloc([NT, 16], F32)
        wk = alloc([NT, 16], F32)
        ern = alloc([NT, 16], F32)
        eB = alloc([NT, 16], F32)
        tg1 = alloc([NT, 16], F32)
        tg2 = alloc([NT, 16], F32)
        hxT = alloc([8, TOK], BF16)
        ybT = alloc([8, SEQ], BF16)
        bg_bc = alloc([32], F32)
        wg = alloc([8, 32], BF16)
        fsc = alloc([16], F32)
        PH = cur[0]
        assert PH % 32 == 0
        whd = [alloc([8, 640], BF16) for _ in range(2)]
        PH2 = cur[0]
        wv_in = w_in.rearrange("(kc p) n -> p kc n", p=128)

        BQ, BK, BREST, WCV, BCV, GHD = 8, 16, 24, 88, 112, 120
        B_O, B_ZB, B_BA, B_CA, B_XA, B_ZA, B_GA, B_GB = [BREST + 8 * i for i in range(8)]
        AX, BX, AC, BC, BOH, BGAH, BGBH, GHH = [8 * i for i in range(8)]

        cur[0] = PH2
        pst = alloc([128], F32)
        pst2 = alloc([128], F32)
        wada = [alloc([8, 1024], BF16) for _ in range(2)]
        modx = alloc([16], F32)
        modc = alloc([16], F32)
        PRO_LATE = cur[0]
        xst = [alloc([D], F32) for _ in range(6)]
        xs = [alloc([D], BF16) for _ in range(4)]
        sqj = alloc([D], BF16)
        ssx = alloc([NT], F32)
        rsx = alloc([NT], F32)
        lnx = alloc([NT], F32)
        assert cur[0] <= ARENA, cur[0]

        def mk_mask(t, pattern_step, cm, op, fill_in, fill_out):
            P.op("pool", lambda e: e.memset(t, fill_in), writes=[("c", id(t))])
            P.op("pool", lambda e: e.affine_select(out=t, in_=t, pattern=[[pattern_step, 128]], compare_op=op,
                                                   fill=fill_out, base=0, channel_multiplier=cm),
                 reads=[("c", id(t))], writes=[("c", id(t))])

        mk_mask(ident_f, -1, 1, ALU.not_equal, 0.0, 1.0)
        mk_mask(M_le, 1, -1, ALU.is_ge, 1.0, 0.0)
        mk_mask(M_lt, 1, -1, ALU.is_gt, 1.0, 0.0)
        mk_mask(M_ge, -1, 1, ALU.is_ge, 1.0, 0.0)
        mk_mask(M_gt, -1, 1, ALU.is_gt, 1.0, 0.0)
        P.op("pool", lambda e: e.memset(ones_f, 1.0), writes=["ones_f"])
        P.op("pool", lambda e: e.tensor_copy(out=ident_bf, in_=ident_f), reads=[("c", id(ident_f))], writes=["ident_bf"])
        P.op("pool", lambda e: e.tensor_copy(out=M_ge_bf, in_=M_ge), reads=[("c", id(M_ge))], writes=["M_ge_bf"])
        P.op("pool", lambda e: e.memset(pst2, 0.0), writes=["pst2", ("pst2", 1), ("pst2", 2), ("pst2", 3)])

        def row(ap1d, n):
            return ap1d.rearrange("(c p) -> c p", p=128)

        P.dma("sp", lambda e: e.dma_start(out=pst[0:8, :], in_=row(g_norm, 8)), writes=[("pst", 0)])
        P.dma("sp", lambda e: e.dma_start(out=pst[8:24, :], in_=row(b_in[0:2048], 16)), writes=[("pst", 1)])
        P.dma("sp", lambda e: e.dma_start(out=pst[24:88, :], in_=row(b_in[3104:3104 + 8192], 64)), writes=[("pst", 2)])
        P.dma("sp", lambda e: e.dma_start(out=pst[88:112, :], in_=w_conv.rearrange("t (c p) -> (t c) p", p=128)),
              writes=[("pst", 3)])
        P.dma("sp", lambda e: e.dma_start(out=pst[112:120, :], in_=row(b_conv, 8)), writes=[("pst", 4)])
        P.dma("sp", lambda e: e.dma_start(out=pst[120:128, :], in_=row(g_head, 8)), writes=[("pst", 5)])
        P.dma("sp", lambda e: e.dma_start(out=pst2[0:16, :], in_=row(b_ada[0:2048], 16)), reads=[], writes=["pst2"])
        P.dma("sp", lambda e: e.dma_start(out=pst2[16:24, :], in_=row(c_in, 8)), writes=[("pst2", 1)])
        P.dma("sp", lambda e: e.dma_start(out=pst2[24:32, :], in_=row(cctx_in, 8)), writes=[("pst2", 2)])
        P.dma("sp", lambda e: e.dma_start(out=pst2[32:40, :], in_=row(b_in[2048:3072], 8)), writes=[("pst2", 3)])
        for (dst, src, nm) in ((bg_bc, b_in[3072:3104], "bg"),):
            P.dma("sp", lambda e, dst=dst, src=src: e.dma_start(out=dst, in_=src.partition_broadcast(128)), writes=[nm])
        wv_ada = w_ada.rearrange("(kc p) n -> p kc n", p=128)
        for j in range(2):
            P.dma("pool", lambda e, j=j: e.dma_start(out=wada[j], in_=wv_ada[:, :, j * 1024:(j + 1) * 1024]),
                  writes=[("wada", j)])

        P.dma("pool", lambda e: e.dma_start(out=wg, in_=wv_in[:, :, 3072:3104]), writes=["wg"])

        def head_cols(h):
            return [h * 128, 1024 + h * 128, 2048 + h * 128, 3104 + h * 128, 3104 + 1024 + h * 128]

        def load_head_w(h, extra=()):
            for j, c0 in enumerate(head_cols(h)):
                P.dma("pool", lambda e, h=h, j=j, c0=c0: e.dma_start(out=whd[h % 2][:, :, j * 128:(j + 1) * 128],
                                                                     in_=wv_in[:, :, c0:c0 + 128]),
                      reads=list(extra), writes=[("whd", h % 2, j)])

        P.op("pe", lambda e: e.transpose(out=PS(0, 0, 128), in_=pst, identity=ident_f),
             reads=[("pst", i) for i in range(6)] + [("c", id(ident_f))], writes=[("ps", 0)])
        P.op("dve", lambda e: e.tensor_copy(out=pp, in_=PS(0, 0, 128)), reads=[("ps", 0)], writes=["pp"])
        P.op("pe", lambda e: e.transpose(out=PS(1, 0, 64), in_=pst2[0:64, :], identity=ident_f[0:64, 0:64]),
             reads=["pst2", ("pst2", 1), ("pst2", 2), ("pst2", 3), ("c", id(ident_f))], writes=[("ps", 1)])
        P.op("dve", lambda e: e.tensor_copy(out=pp2, in_=PS(1, 0, 40)), reads=[("ps", 1)], writes=["pp2"])
        if stop_after <= 1:
            P.emit()
            return nc
        P.op("act", lambda e: e.activation(out=s_t, in_=pp2[:, 16:32], func=AF.Exp, scale=-1.0), reads=["pp2"], writes=["s_t"])
        P.op("dve", lambda e: e.tensor_scalar(out=s_t, in0=s_t, scalar1=1.0, scalar2=None, op0=ALU.add),
             reads=["s_t"], writes=["s_t"])
        P.op("dve", lambda e: e.reciprocal(out=s_t, in_=s_t), reads=["s_t"], writes=["s_t"])
        P.op("dve", lambda e: e.tensor_tensor(out=s_f, in0=s_t, in1=pp2[:, 16:32], op=ALU.mult),
             reads=["s_t", "pp2"], writes=["s_f"])
        P.op("dve", lambda e: e.tensor_copy(out=s2, in_=s_f), reads=["s_f"], writes=["s2"])
        for j in range(2):
            for mc in range(8):
                for kc in range(8):
                    m = j * 8 + mc
                    P.op("pe", lambda e, j=j, mc=mc, kc=kc, m=m: e.matmul(
                        PS(2, 2 * m, 2), lhsT=wada[j][:, kc, mc * 128:(mc + 1) * 128],
                        rhs=s2[:, kc::8], start=(kc == 0), stop=(kc == 7)),
                        reads=[("wada", j), "s2"], writes=[("ps", 2)])
        modv = PS(2, 0, 32).rearrange("p (m n) -> p m n", n=2)
        P.op("dve", lambda e: e.tensor_tensor(out=modx, in0=modv[:, :, 0], in1=pp2[:, 0:16], op=ALU.add),
             reads=[("ps", 2), "pp2"], writes=["modx"])
        P.op("dve", lambda e: e.tensor_tensor(out=modc, in0=modv[:, :, 1], in1=pp2[:, 0:16], op=ALU.add),
             reads=[("ps", 2), "pp2"], writes=["modc"])
        for (mod, a0, b0, nm) in ((modx, AX, BX, "modx"), (modc, AC, BC, "modc")):
            P.op("dve", lambda e, mod=mod, a0=a0: e.scalar_tensor_tensor(
                out=dp[:, a0:a0 + 8], in0=mod[:, 8:16], scalar=1.0, in1=pp[:, 0:8], op0=ALU.add, op1=ALU.mult),
                reads=[nm, "pp"], writes=[("dp", a0)])
            P.op("dve", lambda e, mod=mod, b0=b0: e.tensor_copy(out=dp[:, b0:b0 + 8], in_=mod[:, 0:8]),
                 reads=[nm], writes=[("dp", b0)])
        for (dst, src) in ((BOH, B_O), (BGAH, B_GA), (BGBH, B_GB), (GHH, GHD)):
            P.op("dve", lambda e, dst=dst, src=src: e.tensor_scalar(out=dp[:, dst:dst + 8], in0=pp[:, src:src + 8],
                                                                    scalar1=0.5, scalar2=None, op0=ALU.mult),
                 reads=["pp"], writes=[("dp", dst)])
        if stop_after <= 2:
            P.emit()
            return nc
        def xsrc(tt):
            return ctx[tt * 128:(tt + 1) * 128, :] if tt < 2 else x[(tt - 2) * 128:(tt - 1) * 128, :]

        for tt in range(NT):
            xb = xst[tt % 6]
            P.dma("sp", lambda e, tt=tt, xb=xb: e.dma_start(out=xb, in_=xsrc(tt)), writes=[("xst", tt % 6), ("xld", tt)])
            P.op("act", lambda e, tt=tt, xb=xb: e.activation(out=sqj, in_=xb, func=AF.Square, accum_out=ssx[:, tt:tt + 1]),
                 reads=[("xst", tt % 6)], writes=["sqj", ("ssx", tt)])
            P.op("act", lambda e, tt=tt: e.activation(out=lnx[:, tt:tt + 1], in_=ssx[:, tt:tt + 1], func=AF.Ln,
                                                      scale=1.0 / D, bias=EPS),
                 reads=[("ssx", tt)], writes=[("lnx", tt)])
            P.op("act", lambda e, tt=tt: e.activation(out=rsx[:, tt:tt + 1], in_=lnx[:, tt:tt + 1], func=AF.Exp, scale=-0.5),
                 reads=[("lnx", tt)], writes=[("rsx", tt)])
            xsb = xs[tt % 4]
            P.op("dve", lambda e, tt=tt, xb=xb, xsb=xsb: e.tensor_scalar(out=xsb, in0=xb, scalar1=rsx[:, tt:tt + 1],
                                                                          scalar2=None, op0=ALU.mult),
                 reads=[("xst", tt % 6), ("rsx", tt)], writes=[("xs", tt % 4)])
            grp = tt // 2
            bank0 = (grp % 4) * 2
            half = tt % 2
            for kc in range(8):
                b = bank0 + (kc // 4)
                off = (kc % 4) * 256 + half * 128
                P.op("pe", lambda e, xsb=xsb, kc=kc, b=b, off=off: e.transpose(
                    out=PSB(b, off, 128), in_=xsb[:, kc * 128:(kc + 1) * 128], identity=ident_bf),
                    reads=[("xs", tt % 4), "ident_bf"], writes=[("ps", b, kc % 4, half)])
            if half == 1:
                a0, b0 = (AC, BC) if tt < 2 else (AX, BX)
                for kc in range(8):
                    b = bank0 + (kc // 4)
                    src = PSB(b, (kc % 4) * 256, 256)
                    dst = hxT[:, kc, (tt - 1) * 128:(tt + 1) * 128]
                    rd = [("ps", b, kc % 4, 0), ("ps", b, kc % 4, 1), ("dp", a0), ("dp", b0)]
                    wr = [("hxT", kc, tt - 1), ("hxT", kc, tt)]
                    if kc < 2:
                        P.op("act", lambda e, src=src, dst=dst, kc=kc, a0=a0, b0=b0: e.activation(
                            out=dst, in_=src, func=AF.Identity, scale=dp[:, a0 + kc:a0 + kc + 1],
                            bias=dp[:, b0 + kc:b0 + kc + 1]), reads=rd, writes=wr)
                    else:
                        P.op("dve", lambda e, src=src, dst=dst, kc=kc, a0=a0, b0=b0: e.tensor_scalar(
                            out=dst, in0=src, scalar1=dp[:, a0 + kc:a0 + kc + 1], scalar2=dp[:, b0 + kc:b0 + kc + 1],
                            op0=ALU.mult, op1=ALU.add), reads=rd, writes=wr)

        def dump(name, src_ap, reads):
            if name in dbg_out:
                a = src_ap
                if len(a.shape) == 3:
                    a = a.rearrange("p a b -> p (a b)")
                if a.dtype == BF16:
                    a = a.bitcast(F32)
                P.dma("sp", lambda e: e.dma_start(out=dbg_out[name], in_=a), reads=reads)

        load_head_w(0, [("xld", 9)])
        load_head_w(1, [("xld", 17)])
        if stop_after <= 3:
            P.barrier()
            dump("pp", pp, [])
            dump("dp", dp, [])
            dump("hxT", hxT, [])
            P.emit()
            return nc

        allhx = [("hxT", kc, tt) for kc in range(8) for tt in range(NT)]
        for tt in range(NT):
            b, off = (0, tt * 32) if tt < 16 else (1, (tt - 16) * 32)
            for kc in range(8):
                P.op("pe", lambda e, tt=tt, kc=kc, b=b, off=off: e.matmul(
                    PS(b, off, 32), lhsT=hxT[:, kc, tt * 128:(tt + 1) * 128], rhs=wg[:, kc, :],
                    start=(kc == 0), stop=(kc == 7)), reads=["wg", ("hxT", kc, tt)], writes=[("ps", b)])
        P.op("dve", lambda e: e.tensor_tensor(out=gts[:, 0:16, :], in0=PS(0, 0, 512).rearrange("p (a b) -> p a b", b=32),
                                              in1=bg_bc.unsqueeze(1).to_broadcast([128, 16, 32]), op=ALU.add),
             reads=[("ps", 0), "bg"], writes=[("gts", 0)])
        P.op("dve", lambda e: e.tensor_tensor(out=gts[:, 16:18, :], in0=PS(1, 0, 64).rearrange("p (a b) -> p a b", b=32),
                                              in1=bg_bc.unsqueeze(1).to_broadcast([128, 2, 32]), op=ALU.add),
             reads=[("ps", 1), "bg"], writes=[("gts", 1)])
        allg = [("gts", 0), ("gts", 1)]
        P.op("act", lambda e: e.activation(out=lf[:, :, 0:8], in_=gts[:, :, 8:16], func=AF.Exp, scale=-1.0),
             reads=allg, writes=["lf0"])
        P.op("act", lambda e: e.activation(out=lf[:, :, 8:16], in_=gts[:, :, 24:32], func=AF.Exp, scale=-1.0),
             reads=allg, writes=["lf1"])
        P.op("act", lambda e: e.activation(out=lf, in_=lf, func=AF.Ln, bias=1.0), reads=["lf0", "lf1"], writes=["lf"])
        lf2 = lf.rearrange("p a b -> p (a b)")
        for i, Mk in enumerate((M_le, M_gt, M_ge, M_lt, ones_f)):
            P.op("pe", lambda e, i=i, Mk=Mk: e.matmul(PS(2 + i, 0, 288), lhsT=Mk, rhs=lf2, start=True, stop=True),
                 reads=["lf", ("c", id(Mk)), "ones_f"], writes=[("ps", 2 + i)])

        def cs(i):
            return PS(2 + i, 0, 288).rearrange("p (a b) -> p a b", b=16)
        Pf, Sf, Pb, Sb, Tt = cs(0), cs(1), cs(2), cs(3), cs(4)
        for (half, Pm, Sm, pb_, sb_, ic) in ((0, Pf, Sf, 2, 3, 0), (1, Pb, Sb, 4, 5, 16)):
            sl = slice(half * 8, half * 8 + 8)
            P.op("dve", lambda e, sl=sl, Pm=Pm, ic=ic: e.tensor_tensor(out=tg1[:, :, sl], in0=Pm[:, :, sl],
                                                                       in1=gts[:, :, ic:ic + 8], op=ALU.add),
                 reads=[("ps", pb_)] + allg, writes=[("tg1", half)])
            P.op("act", lambda e, sl=sl: e.activation(out=ea[:, :, sl], in_=tg1[:, :, sl], func=AF.Exp),
                 reads=[("tg1", half)], writes=[("ea", half)])
            P.op("dve", lambda e, sl=sl, Sm=Sm, ic=ic: e.scalar_tensor_tensor(
                out=tg2[:, :, sl], in0=Sm[:, :, sl], scalar=-1.0, in1=gts[:, :, ic:ic + 8], op0=ALU.mult, op1=ALU.add),
                reads=[("ps", sb_)] + allg, writes=[("tg2", half)])
            P.op("act", lambda e, sl=sl: e.activation(out=wk[:, :, sl], in_=tg2[:, :, sl], func=AF.Exp),
                 reads=[("tg2", half)], writes=[("wk", half)])
            P.op("act", lambda e, sl=sl, Pm=Pm: e.activation(out=ern[:, :, sl], in_=Pm[:, :, sl], func=AF.Exp),
                 reads=[("ps", pb_)], writes=[("ern", half)])
        P.op("act", lambda e: e.activation(out=eB, in_=Tt, func=AF.Exp, scale=-1.0), reads=[("ps", 6)], writes=["eB"])
        if stop_after <= 4:
            P.barrier()
            dump("gts", gts, [])
            dump("ea", ea, [])
            dump("wk", wk, [])
            dump("ern", ern, [])
            dump("eB", eB, [])
            P.emit()
            return nc
        S4OUT = {"gts", "lf0", "lf1", "lf", "tg1", "tg2", "ea", "wk", "ern", "eB", "ps"}
        scratch = {"pst", "pst2", "wada", "modx", "modc"}
        late = {"xst", "xs", "sqj", "ssx", "lnx", "rsx", "xld"}
        allk = set(P.last_w) | set(P.readers)
        rd = [k for k in allk if (k if isinstance(k, str) else k[0]) not in (S4OUT | scratch | late | {"whd", "wg", "hxT"})]
        P.fence_all_later(P.op("pool", lambda e: e.memset(fsc[:, 0:1], 0.0), reads=rd, writes=P.keys_named(scratch) + ["S5"]))

        cur[0] = PH2
        qT, kT, gob, kBf, kBb, V1, Cstf, Cstb = [], [], [], [], [], [], [], []
        for _hb in range(2):
            qT.append(alloc([SEQ], BF16))
            kT.append(alloc([TOK], BF16))
            gob.append(alloc([SEQ], BF16))
            _kB = alloc([NT, 2, 128], BF16)
            kBf.append(_kB[:, :, 0, :])
            kBb.append(_kB[:, :, 1, :])
            kBB = (kBB if _hb else []) + [_kB]
            V1.append(alloc([NT, 130], BF16))
            if _hb == 0:
                S5_HB0_END = cur[0]
        _cf, _cb = alloc([NLT, 130], BF16), alloc([NLT, 130], BF16)
        Cstf, Cstb = [_cf, _cf], [_cb, _cb]
        Cf = [alloc([130], F32) for _ in range(2)]
        Cb = [alloc([130], F32) for _ in range(2)]
        Hh = alloc([NLT, 128], F32)
        hn = alloc([NLT, 128], BF16)
        ssh = alloc([NLT], F32)
        rsh = alloc([NLT], F32)
        mhalf = alloc([NLT], F32)
        t_o = [alloc([512], BF16) for _ in range(2)]
        t_z = [alloc([512], BF16) for _ in range(2)]
        hf = [alloc([128], F32) for _ in range(2)]
        sqh = alloc([128], BF16)
        Spf = [alloc([128], BF16) for _ in range(2)]
        Spb = [alloc([128], BF16) for _ in range(2)]
        Spr = [alloc([128], BF16) for _ in range(2)]
        dcl = [alloc([2], F32) for _ in range(2)]
        dab = [alloc([2], F32) for _ in range(2)]
        sfb = [alloc([2], F32) for _ in range(2)]
        vtmp = [alloc([512], BF16) for _ in range(2)]
        tmpg = [alloc([512], BF16) for _ in range(2)]
        wcv0 = alloc([8, 512], BF16)
        assert cur[0] <= ARENA, cur[0]
        S5_END = cur[0]

        P.op("pool", lambda e: e.memset(mhalf, -0.5), writes=["mhalf"])
        P.op("pool", lambda e: e.memset(V1[0], 1.0), writes=[("V1ones", 0)])
        fm_rot = [0]
        tm_rot = [0]

        def gen_proj(h):
            hb = h % 2
            W = whd[hb]
            wkey = lambda j: ("whd", hb, j)
            jmap = {"q": 0, "k": 1, "v": 2, "o": 3, "zb": 4}
            blocks = [(CTX + bi * 512, 512, bi) for bi in range(4)] + [(0, 256, 4)]
            for (tok0, ntk, bi) in blocks:
                fams = ("q", "k", "v", "o", "zb") if bi < 4 else ("k", "v")
                lsl = slice(bi * 512, (bi + 1) * 512)
                gsl = slice(tok0, tok0 + ntk)
                vt = vtmp[bi % 2]
                for fam in fams:
                    j = jmap[fam]
                    b = fm_rot[0] % 4
                    fm_rot[0] += 1
                    for kc in range(8):
                        P.op("pe", lambda e, ntk=ntk, gsl=gsl, lsl=lsl, bi=bi, vt=vt, b=b, j=j, kc=kc: e.matmul(
                            PS(b, 0, ntk), lhsT=W[:, kc, j * 128:(j + 1) * 128], rhs=hxT[:, kc, gsl],
                            start=(kc == 0), stop=(kc == 7)),
                            reads=[wkey(j)] + [("hxT", kc, t_) for t_ in range(tok0 // 128, (tok0 + ntk) // 128)], writes=[("ps", b)])
                    if fam == "q":
                        P.op("dve", lambda e, ntk=ntk, gsl=gsl, lsl=lsl, bi=bi, vt=vt, b=b: e.tensor_scalar(
                            out=qT[hb][:, lsl], in0=PS(b, 0, ntk), scalar1=pp[:, BQ + h:BQ + h + 1], scalar2=QS,
                            op0=ALU.add, op1=ALU.mult), reads=[("ps", b)], writes=[("qT", hb, bi)])
                    elif fam == "k":
                        P.op("act", lambda e, ntk=ntk, gsl=gsl, lsl=lsl, bi=bi, vt=vt, b=b: e.activation(
                            out=kT[hb][:, gsl], in_=PS(b, 0, ntk), func=AF.Identity, bias=pp[:, BK + h:BK + h + 1]),
                            reads=[("ps", b)], writes=[("kT", hb, bi)])
                    elif fam == "v":
                        P.op("dve", lambda e, ntk=ntk, gsl=gsl, lsl=lsl, bi=bi, vt=vt, b=b: e.tensor_scalar(
                            out=vt[:, 0:ntk], in0=PS(b, 0, ntk), scalar1=pp2[:, 32 + h:33 + h], scalar2=None, op0=ALU.add),
                            reads=[("ps", b)], writes=[("vtmp", bi % 2)])
                    elif fam == "o":
                        P.op("act", lambda e, ntk=ntk, gsl=gsl, lsl=lsl, bi=bi, vt=vt, b=b: e.activation(
                            out=t_o[bi % 2], in_=PS(b, 0, ntk), func=AF.Tanh, scale=0.5,
                            bias=dp[:, BOH + h:BOH + h + 1]), reads=[("ps", b)], writes=[("t_o", bi % 2)])
                    else:
                        P.op("act", lambda e, ntk=ntk, gsl=gsl, lsl=lsl, bi=bi, vt=vt, b=b: e.activation(
                            out=t_z[bi % 2], in_=PS(b, 0, ntk), func=AF.Silu, bias=pp[:, B_ZB + h:B_ZB + h + 1]),
                            reads=[("ps", b)], writes=[("t_z", bi % 2)])
                        P.op("pool", lambda e, ntk=ntk, gsl=gsl, lsl=lsl, bi=bi, vt=vt: e.tensor_tensor(out=tmpg[bi % 2], in0=t_o[bi % 2], in1=t_z[bi % 2], op=ALU.mult),
                             reads=[("t_o", bi % 2), ("t_z", bi % 2)], writes=[("tmpg", bi % 2)])
                        P.op("pool", lambda e, ntk=ntk, gsl=gsl, lsl=lsl, bi=bi, vt=vt: e.tensor_tensor(out=gob[hb][:, lsl], in0=tmpg[bi % 2], in1=t_z[bi % 2], op=ALU.add),
                             reads=[("tmpg", bi % 2), ("t_z", bi % 2)], writes=[("gob", hb, bi)])
                    yield
                for ti in range(ntk // 128):
                    tt = (tok0 // 128 + ti)
                    b = fm_rot[0] % 4
                    fm_rot[0] += 1
                    P.op("pe", lambda e, ntk=ntk, gsl=gsl, lsl=lsl, bi=bi, vt=vt, b=b, tt=tt: e.transpose(out=PSB(b, 0, 128), in_=kT[hb][:, tt * 128:(tt + 1) * 128],
                                                                 identity=ident_bf),
                         reads=[("kT", hb, bi), "ident_bf"], writes=[("ps", b)])
                    P.op("pe", lambda e, ntk=ntk, gsl=gsl, lsl=lsl, bi=bi, vt=vt, b=b, ti=ti: e.transpose(out=PSB(b, 128, 128), in_=vt[:, ti * 128:(ti + 1) * 128],
                                                                 identity=ident_bf),
                         reads=[("vtmp", bi % 2), "ident_bf"], writes=[("ps", b)])
                    P.op("dve", lambda e, ntk=ntk, gsl=gsl, lsl=lsl, bi=bi, vt=vt, tt=tt, b=b: e.tensor_tensor(
                        out=kBB[hb][:, tt, :, :], in0=PSB(b, 0, 128).unsqueeze(1).to_broadcast([128, 2, 128]),
                        in1=wk[:, tt, h::8].unsqueeze(2).to_broadcast([128, 2, 128]), op=ALU.mult),
                        reads=[("ps", b), ("wk", 0), ("wk", 1)], writes=[("kBf", hb, tt), ("kBb", hb, tt)])
                    P.op("act", lambda e, ntk=ntk, gsl=gsl, lsl=lsl, bi=bi, vt=vt, tt=tt, b=b: e.activation(
                        out=V1[hb][:, tt, 0:128], in_=PSB(b, 128, 128), func=AF.Copy),
                        reads=[("ps", b), ("V1ones", hb)], writes=[("V1", hb, tt)])
                    if ti % 2 == 1:
                        yield

        def gen_scan(h):
            hb = h % 2
            P.op("pool", lambda e: e.memset(Cf[0], 0.0), writes=[("Cf", 0)])
            P.op("pool", lambda e: e.memset(Cb[0], 0.0), writes=[("Cb", 0)])
            f_order = list(range(0, 17))
            b_order = [1, 0] + list(range(17, 2, -1))
            for step in range(17):
                for (dirn, order, kB, Cs, Cst, col0, bank) in (("f", f_order, kBf, Cf, Cstf, 0, 4), ("b", b_order, kBb, Cb, Cstb, 8, 5)):
                    tt = order[step]
                    src, dst = Cs[step % 2], Cs[(step + 1) % 2]
                    ck = "C" + dirn
                    P.op("pe", lambda e, tt=tt, kB=kB, bank=bank: e.matmul(
                        PS(bank, 0, 129), lhsT=kB[hb][:, tt, :], rhs=V1[hb][:, tt, 0:129], start=True, stop=True),
                        reads=[("kB" + dirn, hb, tt), ("V1", hb, tt)], writes=[("ps", bank)])
                    P.op("dve", lambda e, tt=tt, src=src, dst=dst, bank=bank, col0=col0: e.scalar_tensor_tensor(
                        out=dst[:, 0:129], in0=src[:, 0:129], scalar=eB[:, tt, col0 + h:col0 + h + 1],
                        in1=PS(bank, 0, 129), op0=ALU.mult, op1=ALU.add),
                        reads=[(ck, step % 2), ("ps", bank), "eB"], writes=[(ck, (step + 1) % 2)])
                    if dirn == "f":
                        nxt = tt + 1
                    else:
                        nxt = 17 if step == 1 else (tt - 1 if step >= 2 else None)
                    if nxt is not None and nxt >= 2:
                        P.op("pool", lambda e, dst=dst, Cst=Cst, nxt=nxt: e.tensor_copy(out=Cst[hb][:, nxt - 2, 0:129], in_=dst[:, 0:129]),
                             reads=[(ck, (step + 1) % 2)], writes=[("Cst" + dirn, 0, nxt)])
                yield

            def emit_S(lt):
                tt = lt + 2
                s2i = lt % 2
                tsl = slice(lt * 128, (lt + 1) * 128)
                for bank in (4,):
                    P.op("pe", lambda e, tsl=tsl, bank=bank, tt=tt: e.matmul(PS(bank, 0, 128), lhsT=kT[hb][:, tt * 128:(tt + 1) * 128],
                                                                             rhs=qT[hb][:, tsl], start=True, stop=True),
                         reads=[("kT", hb, lt // 4), ("qT", hb, lt // 4)], writes=[("ps", bank)])
                P.op("dve", lambda e, tt=tt, s2i=s2i: e.scalar_tensor_tensor(
                    out=Spf[s2i], in0=PS(4, 0, 128), scalar=ea[:, tt, h:h + 1], in1=M_le, op0=ALU.mult, op1=ALU.mult),
                    reads=[("ps", 4), ("ea", 0)], writes=[("Spf", s2i)])
                P.op("act", lambda e, tt=tt, s2i=s2i: e.activation(out=Spr[s2i], in_=PS(4, 0, 128), func=AF.Copy,
                                                                   scale=ea[:, tt, 8 + h:9 + h]),
                     reads=[("ps", 4), ("ea", 1)], writes=[("Spr", s2i)])
                P.op("pool", lambda e, s2i=s2i: e.tensor_tensor(out=Spb[s2i], in0=Spr[s2i], in1=M_ge_bf, op=ALU.mult),
                     reads=[("Spr", s2i)], writes=[("Spb", s2i)])

            def emit_num(lt):
                tt = lt + 2
                s2i = lt % 2
                tsl = slice(lt * 128, (lt + 1) * 128)
                nb = 6 + s2i
                NUM = PS(nb, 0, 260).rearrange("p (a b) -> p a b", a=2)
                for di, (Sp, Cst, dn) in enumerate(((Spf, Cstf, "f"), (Spb, Cstb, "b"))):
                    P.op("pe", lambda e, Sp=Sp, di=di, NUM=NUM: e.matmul(
                        NUM[:, di, 0:129], lhsT=Sp[s2i], rhs=V1[hb][:, tt, 0:129], start=True, stop=False),
                        reads=[("Sp" + dn, s2i), ("V1", hb, tt)], writes=[("ps", nb)])
                    P.op("pe", lambda e, Cst=Cst, di=di, NUM=NUM: e.matmul(
                        NUM[:, di, 0:129], lhsT=qT[hb][:, tsl], rhs=Cst[hb][:, lt, 0:129], start=False, stop=True),
                        reads=[("qT", hb, lt // 4), ("Cst" + dn, 0, tt)], writes=[("ps", nb)])
                numk = [("ps", nb)]
                P.op("act", lambda e, NUM=NUM: e.activation(out=dab[s2i], in_=NUM[:, :, 128], func=AF.Abs),
                     reads=numk, writes=[("dab", s2i)])
                P.op("dve", lambda e: e.tensor_tensor(out=dcl[s2i], in0=dab[s2i], in1=ern[:, tt, h::8], op=ALU.max),
                     reads=[("dab", s2i), ("ern", 0), ("ern", 1)], writes=[("dcl", s2i)])
                P.op("dve", lambda e: e.reciprocal(out=sfb[s2i], in_=dcl[s2i]), reads=[("dcl", s2i)], writes=[("sfb", s2i)])
                P.op("act", lambda e, NUM=NUM: e.activation(out=hf[s2i], in_=NUM[:, 0, 0:128], func=AF.Copy, scale=sfb[s2i][:, 0:1]),
                     reads=numk + [("sfb", s2i)], writes=[("hf", s2i)])
                P.op("dve", lambda e, NUM=NUM: e.scalar_tensor_tensor(
                    out=Hh[:, lt, :], in0=NUM[:, 1, 0:128], scalar=sfb[s2i][:, 1:2], in1=hf[s2i], op0=ALU.mult, op1=ALU.add),
                    reads=numk + [("sfb", s2i), ("hf", s2i)], writes=[("Hh", lt)])
                P.op("act", lambda e: e.activation(out=sqh, in_=Hh[:, lt, :], func=AF.Square, accum_out=ssh[:, lt:lt + 1]),
                     reads=[("Hh", lt)], writes=["sqh", ("ssh", lt)])

            emit_S(0)
            yield
            for lt in range(NLT):
                if lt + 1 < NLT:
                    emit_S(lt + 1)
                emit_num(lt)
                yield
            allss = [("ssh", lt) for lt in range(NLT)]
            P.op("dve", lambda e: e.tensor_scalar(out=ssh, in0=ssh, scalar1=1.0 / 128, scalar2=EPS, op0=ALU.mult, op1=ALU.add),
                 reads=allss, writes=allss)
            P.op("pool", lambda e: e.tensor_tensor(out=rsh, in0=ssh, in1=mhalf, op=ALU.pow), reads=allss + ["mhalf"], writes=["rsh"])
            for blk in range(4):
                for q4 in range(4):
                    lt = blk * 4 + q4
                    if lt % 2 == 0:
                        P.op("act", lambda e, lt=lt: e.activation(out=hn[:, lt, :], in_=Hh[:, lt, :], func=AF.Copy,
                                                                  scale=rsh[:, lt:lt + 1]),
                             reads=[("Hh", lt), "rsh"], writes=[("hn", lt)])
                    else:
                        P.op("pool", lambda e, lt=lt: e.tensor_scalar(out=hn[:, lt, :], in0=Hh[:, lt, :], scalar1=rsh[:, lt:lt + 1],
                                                                      scalar2=0.0, op0=ALU.mult, op1=ALU.add),
                             reads=[("Hh", lt), "rsh"], writes=[("hn", lt)])
                pslot = 6 + blk % 2
                for q4 in range(4):
                    lt = blk * 4 + q4
                    P.op("pe", lambda e, lt=lt, q4=q4, pslot=pslot: e.transpose(
                        out=PSB(pslot, q4 * 128, 128), in_=hn[:, lt, :], identity=ident_bf),
                        reads=[("hn", lt), "ident_bf"], writes=[("ps", pslot)])
                P.op("dve", lambda e, blk=blk, pslot=pslot: e.scalar_tensor_tensor(
                    out=ybT[:, h, blk * 512:(blk + 1) * 512], in0=PSB(pslot, 0, 512), scalar=dp[:, GHH + h:GHH + h + 1],
                    in1=gob[hb][:, blk * 512:(blk + 1) * 512], op0=ALU.mult, op1=ALU.mult),
                    reads=[("ps", pslot), ("gob", hb, blk)], writes=[("ybT", h, blk)])
                yield
            if h == 0:
                dump("Hh0", Hh, [("Hh", lt) for lt in range(NLT)])

        def drive(gens, weights=None):
            pairs = [(g, (weights[i] if weights else 1)) for i, g in enumerate(gens) if g is not None]
            while pairs:
                for (g, wgt) in list(pairs):
                    for _ in range(wgt):
                        try:
                            next(g)
                        except StopIteration:
                            pairs.remove((g, wgt))
                            break

        OFF_BA, OFF_CA, OFF_XA, OFF_ZA, OFF_GA, OFF_GB = [3104 + 1024 * i for i in range(2, 8)]

        def load_cv_w(c):
            for j, o0 in enumerate((OFF_CA, OFF_XA, OFF_ZA, OFF_BA)):
                P.dma("pool", lambda e, c=c, j=j, o0=o0: e.dma_start(
                    out=wcv[c % 2][:, :, j * 128:(j + 1) * 128], in_=wv_in[:, :, o0 + c * 128:o0 + (c + 1) * 128]),
                    writes=[("wcv", c % 2, j)])

        wcv = [wcv0, None]
        assert PRO_LATE >= S5_HB0_END, (PRO_LATE, S5_HB0_END)
        drive([gen_proj(0)])
        P.fence_all_later(P.op("pool", lambda e: e.memset(fsc[:, 4:5], 0.0), writes=P.keys_named(late) + ["S5b"]))
        P.op("pool", lambda e: e.memset(V1[1], 1.0), writes=[("V1ones", 1)])
        for h in range(NH):
            if h == NH - 2:
                load_cv_w(0)
            nxt = None
            if h + 1 < NH:
                nxt = gen_proj(h + 1)
            if h + 2 < NH:
                load_head_w(h + 2)
            if h == NH - 1 and stop_after > 5:
                break
            drive([nxt, gen_scan(h)])
        if stop_after <= 5:
            P.barrier()
            dump("ybT", ybT, [])
            P.emit()
            return nc

        cur[0] = PH
        yaT = alloc([8, SEQ], BF16)
        assert cur[0] == PH + 32768
        wcv = [wcv0, alloc([8, 512], BF16)]
        tca = [alloc([512], F32) for _ in range(1)]
        tu = [alloc([512], F32) for _ in range(1)]
        assert cur[0] <= S5_HB0_END, (cur[0], S5_HB0_END)
        cur[0] = S5_END
        tcv = [alloc([512], F32) for _ in range(1)]
        tsz = [alloc([512], BF16) for _ in range(1)]
        tba = [alloc([512], BF16) for _ in range(1)]
        assert cur[0] <= ARENA, cur[0]
        old_keys = [("whd", hb_, j) for hb_ in range(2) for j in range(5)]
        old_keys += [(nm, 0, i) for nm in ("qT", "kT", "gob") for i in range(4)]
        old_keys += [(nm, 0, tt) for nm in ("kBf", "kBb", "V1") for tt in range(NT)]
        old_keys += [("V1ones", 0)]
        new_keys = [("yaT", c, blk) for c in range(8) for blk in range(4)] + [("wcv", 1, j) for j in range(4)]
        new_keys += [("tca", 0), ("tu", 0), ("tcv", 0), ("tsz", 0), ("tba", 0)]
        P.op("pool", lambda e: e.memset(fsc[:, 1:2], 0.0), writes=old_keys + new_keys + ["fsc1"])
        rot = [0]

        def gen_conv():
          for c in range(8):
            if c + 1 < 8:
                load_cv_w(c + 1)
            W = wcv[c % 2]
            for blk in range(4):
                tok0 = CTX + blk * 512
                i2 = 0
                banks = []
                for j in range(4):
                    b = rot[0] % 4
                    rot[0] += 1
                    banks.append(b)
                    for kc in range(8):
                        P.op("pe", lambda e, b=b, j=j, kc=kc, tok0=tok0, W=W: e.matmul(
                            PS(b, 0, 512), lhsT=W[:, kc, j * 128:(j + 1) * 128], rhs=hxT[:, kc, tok0:tok0 + 512],
                            start=(kc == 0), stop=(kc == 7)),
                            reads=[("wcv", c % 2, j)] + [("hxT", kc, t_) for t_ in range(tok0 // 128, tok0 // 128 + 4)], writes=[("ps", b)])
                bca, bxa, bza, bba = banks
                P.op("act", lambda e, c=c, i2=i2, bca=bca: e.activation(out=tca[i2], in_=PS(bca, 0, 512), func=AF.Identity,
                                                                        bias=pp[:, B_CA + c:B_CA + c + 1]),
                     reads=[("ps", bca)], writes=[("tca", i2)])
                P.op("dve", lambda e, c=c, i2=i2, bxa=bxa: e.scalar_tensor_tensor(
                    out=tu[i2], in0=PS(bxa, 0, 512), scalar=pp[:, B_XA + c:B_XA + c + 1], in1=tca[i2], op0=ALU.add, op1=ALU.mult),
                    reads=[("ps", bxa), ("tca", i2)], writes=[("tu", i2)])
                P.op("act", lambda e, c=c, i2=i2: e.activation(out=tcv[i2], in_=tu[i2], func=AF.Identity,
                                                               scale=pp[:, WCV + 8 + c:WCV + 9 + c],
                                                               bias=pp[:, BCV + c:BCV + c + 1]),
                     reads=[("tu", i2)], writes=[("tcv", i2)])
                u3 = tu[i2].rearrange("p (r w) -> p r w", w=64)
                c3 = tcv[i2].rearrange("p (r w) -> p r w", w=64)
                P.op("dve", lambda e, c=c, u3=u3, c3=c3: e.scalar_tensor_tensor(
                    out=c3[:, :, 1:64], in0=u3[:, :, 0:63], scalar=pp[:, WCV + c:WCV + c + 1], in1=c3[:, :, 1:64],
                    op0=ALU.mult, op1=ALU.add), reads=[("tu", i2), ("tcv", i2)], writes=[("tcv", i2)])
                P.op("dve", lambda e, c=c, u3=u3, c3=c3: e.scalar_tensor_tensor(
                    out=c3[:, :, 0:63], in0=u3[:, :, 1:64], scalar=pp[:, WCV + 16 + c:WCV + 17 + c], in1=c3[:, :, 0:63],
                    op0=ALU.mult, op1=ALU.add), reads=[("tu", i2), ("tcv", i2)], writes=[("tcv", i2)])
                P.op("act", lambda e, c=c, i2=i2, bza=bza: e.activation(out=tsz[i2], in_=PS(bza, 0, 512), func=AF.Silu,
                                                                        bias=pp[:, B_ZA + c:B_ZA + c + 1]),
                     reads=[("ps", bza)], writes=[("tsz", i2)])
                P.op("dve", lambda e, c=c, i2=i2, bba=bba: e.scalar_tensor_tensor(
                    out=tba[i2], in0=PS(bba, 0, 512), scalar=pp[:, B_BA + c:B_BA + c + 1], in1=tsz[i2], op0=ALU.add, op1=ALU.mult),
                    reads=[("ps", bba), ("tsz", i2)], writes=[("tba", i2)])
                P.op("pool", lambda e, c=c, i2=i2, blk=blk: e.tensor_tensor(
                    out=yaT[:, c, blk * 512:(blk + 1) * 512], in0=tba[i2], in1=tcv[i2], op=ALU.mult),
                    reads=[("tba", i2), ("tcv", i2)], writes=[("yaT", c, blk)])
                yield

        drive([gen_scan(NH - 1), gen_conv()], weights=[3, 1])
        if stop_after <= 6:
            P.barrier()
            dump("yaT", yaT, [])
            P.emit()
            return nc

        cur[0] = PH + 32768
        mgT = alloc([8, SEQ], BF16)
        wmg = [alloc([8, 512], BF16) for _ in range(2)]
        tga = [alloc([512], F32) for _ in range(2)]
        tgb = [alloc([512], F32) for _ in range(2)]
        tmA, tmB = tga, tgb
        assert cur[0] <= PH + 90112, cur[0]
        cur[0] = PH + 90112
        wo = alloc([8, 1024], BF16)
        wada2 = alloc([8, 1024], BF16)
        assert cur[0] <= ARENA, cur[0]
        S5N = {"whd", "qT", "kT", "gob", "kBf", "kBb", "V1", "V1ones", "Cstf", "Cstb", "Cf", "Cb", "Hh", "hn", "ssh", "rsh",
               "mhalf", "t_o", "t_z", "hf", "sqh", "Spf", "Spb", "Spr", "dcl", "dab", "sfb", "vtmp", "tmpg"}
        S6N = {"wcv", "tca", "tu", "tcv", "tsz", "tba"}
        P.op("pool", lambda e: e.memset(fsc[:, 2:3], 0.0),
             writes=P.keys_named(S5N) + [("wmg", i, j) for i in range(2) for j in range(4)] + ["fsc2"])
        P.op("pool", lambda e: e.memset(fsc[:, 3:4], 0.0),
             writes=P.keys_named(S5N | S6N) + [("mgT", m, blk) for m in range(8) for blk in range(4)]
             + [(nm, i) for nm in ("tga", "tgb") for i in range(2)] + ["wo", "wada2", "fsc3"])
        wv_out = w_out.rearrange("(kc p) n -> p kc n", p=128)
        wv_pa = w_pa.rearrange("(kc p) n -> p kc n", p=128)
        wv_pb = w_pb.rearrange("(kc p) n -> p kc n", p=128)

        def load_mg_w(m):
            srcs = (wv_pa[:, :, m * 128:(m + 1) * 128], wv_pb[:, :, m * 128:(m + 1) * 128],
                    wv_in[:, :, OFF_GA + m * 128:OFF_GA + (m + 1) * 128], wv_in[:, :, OFF_GB + m * 128:OFF_GB + (m + 1) * 128])
            for j, s in enumerate(srcs):
                P.dma("pool", lambda e, m=m, j=j, s=s: e.dma_start(out=wmg[m % 2][:, :, j * 128:(j + 1) * 128], in_=s),
                      writes=[("wmg", m % 2, j)])

        load_mg_w(0)
        load_mg_w(1)
        P.dma("pool", lambda e: e.dma_start(out=wo, in_=wv_out), writes=["wo"])
        P.dma("pool", lambda e: e.dma_start(out=wada2, in_=wv_ada[:, :, 2048:3072]), writes=["wada2"])
        rot = [0]
        for m in range(8):
            if 1 <= m and m + 1 < 8:
                load_mg_w(m + 1)
            W = wmg[m % 2]
            for blk in range(4):
                i2 = blk % 2
                tsl = slice(blk * 512, (blk + 1) * 512)
                tok0 = CTX + blk * 512
                banks = []
                for j in range(4):
                    b = rot[0] % 8
                    rot[0] += 1
                    banks.append(b)
                    for kc in range(8):
                        if j == 0:
                            rhs = yaT[:, kc, tsl]
                            rk = [("yaT", kc, blk)]
                        elif j == 1:
                            rhs = ybT[:, kc, tsl]
                            rk = [("ybT", kc, blk)]
                        else:
                            rhs = hxT[:, kc, tok0:tok0 + 512]
                            rk = [("hxT", kc, t_) for t_ in range(tok0 // 128, tok0 // 128 + 4)]
                        P.op("pe", lambda e, b=b, j=j, kc=kc, rhs=rhs, W=W: e.matmul(
                            PS(b, 0, 512), lhsT=W[:, kc, j * 128:(j + 1) * 128], rhs=rhs, start=(kc == 0), stop=(kc == 7)),
                            reads=[("wmg", m % 2, j)] + rk, writes=[("ps", b)])
                bpa, bpb, bga, bgb = banks
                P.op("act", lambda e, m=m, i2=i2, bga=bga: e.activation(out=tga[i2], in_=PS(bga, 0, 512), func=AF.Tanh, scale=0.5,
                                                                        bias=dp[:, BGAH + m:BGAH + m + 1]),
                     reads=[("ps", bga)], writes=[("tga", i2)])
                P.op("act", lambda e, m=m, i2=i2, bgb=bgb: e.activation(out=tgb[i2], in_=PS(bgb, 0, 512), func=AF.Tanh, scale=0.5,
                                                                        bias=dp[:, BGBH + m:BGBH + m + 1]),
                     reads=[("ps", bgb)], writes=[("tgb", i2)])
                P.op("dve", lambda e, i2=i2, bpa=bpa: e.scalar_tensor_tensor(
                    out=tmA[i2], in0=tga[i2], scalar=1.0, in1=PS(bpa, 0, 512), op0=ALU.add, op1=ALU.mult),
                    reads=[("tga", i2), ("ps", bpa)], writes=[("tga", i2)])
                P.op("dve", lambda e, i2=i2, bpb=bpb: e.scalar_tensor_tensor(
                    out=tmB[i2], in0=tgb[i2], scalar=1.0, in1=PS(bpb, 0, 512), op0=ALU.add, op1=ALU.mult),
                    reads=[("tgb", i2), ("ps", bpb)], writes=[("tgb", i2)])
                P.op("pool", lambda e, m=m, i2=i2, tsl=tsl: e.tensor_tensor(out=mgT[:, m, tsl], in0=tmA[i2], in1=tmB[i2], op=ALU.add),
                     reads=[("tga", i2), ("tgb", i2)], writes=[("mgT", m, blk)])
        P.barrier()
        dump("mgT", mgT, [])
        if stop_after <= 7:
            P.emit()
            return nc

        cur[0] = PH
        bada_g = alloc([D], F32)
        gate_bc = alloc([D], F32)
        gfin_bc = alloc([D], F32)
        sbc = alloc([8, 128], BF16)
        ones_bf = alloc([128], BF16)
        bo_row = alloc([D], BF16)
        ssf = alloc([16], F32)
        lsf = alloc([16], F32)
        rsf = alloc([16], F32)
        sqf = alloc([D], BF16)
        NB8 = 4
        xt = [alloc([D], F32) for _ in range(1)]
        xn = [alloc([D], F32) for _ in range(1)]
        assert cur[0] <= PH + 32768, cur[0]
        cur[0] = PH + 65536
        xt += [alloc([D], F32) for _ in range(3)]
        xn += [alloc([D], F32) for _ in range(3)]
        assert cur[0] <= PH + 90112, cur[0]
        P.dma("pool", lambda e: e.dma_start(out=bo_row[0:1, :], in_=b_out.rearrange("(o n) -> o n", o=1)), writes=["bo_row"])
        for (dst, src, nm) in ((bada_g, b_ada[2048:3072], "bada_g"), (gfin_bc, g_final, "gfin")):
            P.dma("sp", lambda e, dst=dst, src=src: e.dma_start(out=dst, in_=src.partition_broadcast(128)), writes=[nm])
        P.op("pool", lambda e: e.memset(ones_bf, 1.0), writes=["ones_bf"])
        P.op("pool", lambda e: e.tensor_scalar(out=bada_g, in0=bada_g, scalar1=0.5, scalar2=0.0, op0=ALU.mult, op1=ALU.add),
             reads=["bada_g"], writes=["bada_g"])
        P.op("pool", lambda e: e.tensor_scalar(out=bo_row[0:1, :], in0=bo_row[0:1, :], scalar1=2.0, scalar2=0.0, op0=ALU.mult, op1=ALU.add),
             reads=["bo_row"], writes=["bo_row"])
        for kc in range(8):
            P.op("dve", lambda e, kc=kc: e.tensor_scalar(out=sbc[:, kc, :], in0=ones_bf, scalar1=s_f[:, kc:kc + 1],
                                                         scalar2=None, op0=ALU.mult),
                 reads=["ones_bf"], writes=[("sbc", kc)])
        for nb in range(2):
            for kc in range(8):
                P.op("pe", lambda e, nb=nb, kc=kc: e.matmul(PS(6 + nb, 0, 512), lhsT=sbc[:, kc, :],
                                                            rhs=wada2[:, kc, nb * 512:(nb + 1) * 512],
                                                            start=(kc == 0), stop=(kc == 7)),
                     reads=[("sbc", kc), "wada2"], writes=[("ps", 6 + nb)])
            P.op("dve", lambda e, nb=nb: e.scalar_tensor_tensor(out=gate_bc[:, nb * 512:(nb + 1) * 512], in0=PS(6 + nb, 0, 512),
                                                                scalar=0.5, in1=bada_g[:, nb * 512:(nb + 1) * 512],
                                                                op0=ALU.mult, op1=ALU.add),
                 reads=[("ps", 6 + nb), "bada_g"], writes=[("gate_bc", nb)])
        gk = [("gate_bc", 0), ("gate_bc", 1)]
        for lt in range(NLT):
            i2 = lt % NB8
            tsl = slice(lt * 128, (lt + 1) * 128)
            P.dma("sp", lambda e, lt=lt, i2=i2: e.dma_start(out=xt[i2], in_=x[lt * 128:(lt + 1) * 128, :]), writes=[("xt", i2)])
            bank0 = (lt % 3) * 2
            for nb in range(2):
                for m in range(8):
                    P.op("pe", lambda e, nb=nb, m=m, tsl=tsl, bank0=bank0: e.matmul(
                        PS(bank0 + nb, 0, 512), lhsT=mgT[:, m, tsl], rhs=wo[:, m, nb * 512:(nb + 1) * 512],
                        start=(m == 0), stop=False), reads=["wo"], writes=[("ps", bank0 + nb)])
                P.op("pe", lambda e, nb=nb, bank0=bank0: e.matmul(
                    PS(bank0 + nb, 0, 512), lhsT=ones_bf[0:1, :], rhs=bo_row[0:1, nb * 512:(nb + 1) * 512],
                    start=False, stop=True), reads=["ones_bf", "bo_row"], writes=[("ps", bank0 + nb)])
                P.op("dve", lambda e, nb=nb, i2=i2, bank0=bank0: e.tensor_tensor(
                    out=xn[i2][:, nb * 512:(nb + 1) * 512], in0=PS(bank0 + nb, 0, 512), in1=gate_bc[:, nb * 512:(nb + 1) * 512],
                    op=ALU.mult), reads=[("ps", bank0 + nb)] + gk, writes=[("xn", i2, nb)])
            xk = [("xn", i2, 0), ("xn", i2, 1)]
            P.op("pool", lambda e, i2=i2: e.tensor_tensor(out=xn[i2], in0=xn[i2], in1=xt[i2], op=ALU.add),
                 reads=xk + [("xt", i2)], writes=xk)
            P.op("act", lambda e, i2=i2, lt=lt: e.activation(out=sqf, in_=xn[i2], func=AF.Square, accum_out=ssf[:, lt:lt + 1]),
                 reads=xk, writes=["sqf", ("ssf", lt)])
            P.op("act", lambda e, lt=lt: e.activation(out=lsf[:, lt:lt + 1], in_=ssf[:, lt:lt + 1], func=AF.Ln, scale=1.0 / D, bias=EPS),
                 reads=[("ssf", lt)], writes=[("lsf", lt)])
            P.op("act", lambda e, lt=lt: e.activation(out=rsf[:, lt:lt + 1], in_=lsf[:, lt:lt + 1], func=AF.Exp, scale=-0.5),
                 reads=[("lsf", lt)], writes=[("rsf", lt)])
            P.op("dve", lambda e, i2=i2, lt=lt: e.scalar_tensor_tensor(
                out=xn[i2], in0=xn[i2], scalar=rsf[:, lt:lt + 1], in1=gfin_bc, op0=ALU.mult, op1=ALU.mult),
                reads=xk + [("rsf", lt), "gfin"], writes=xk)
            P.dma("sp", lambda e, i2=i2, lt=lt: e.dma_start(out=y[lt * 128:(lt + 1) * 128, :], in_=xn[i2]), reads=xk)
        P.emit()
    return nc


_NC_CACHE = {}


def _core_inputs(b, x, c, ctx, c_ctx, w_ada, b_ada, g_norm, w_in, b_in, w_conv, b_conv, g_head, w_pa, w_pb, w_out,
                 b_out, g_final):
    f = lambda a: np.ascontiguousarray(a, dtype=np.float32)
    return {
        "x": f(x[b]), "ctx": f(ctx[b]), "c": f(c[b]), "c_ctx": f(c_ctx),
        "w_ada": f(w_ada[0]), "b_ada": f(b_ada[0]), "g_norm": f(g_norm[0]), "w_in": f(w_in[0]), "b_in": f(b_in[0]),
        "w_conv": f(w_conv[0]), "b_conv": f(b_conv[0]), "g_head": f(g_head[0]), "w_pa": f(w_pa[0]), "w_pb": f(w_pb[0]),
        "w_out": f(w_out[0]), "b_out": f(b_out[0]), "g_final": f(g_final),
    }


def kernel(**inputs):
    if "nc" not in _NC_CACHE:
        _NC_CACHE["nc"] = build_program()
    nc = _NC_CACHE["nc"]
    in_maps = [_core_inputs(b, **inputs) for b in range(8)]
    res = run_bass_kernel_spmd(nc, in_maps, core_ids=list(range(8)))
    return np.stack([np.asarray(r["y"], dtype=np.float32).reshape(SEQ, D) for r in res.results], axis=0)
```

```python
import contextlib
import numpy as np
import concourse.bass as bass
import concourse.mybir as mybir
from concourse.bass_utils import run_bass_kernel_spmd

F32 = mybir.dt.float32
BF16 = mybir.dt.bfloat16
AF = mybir.ActivationFunctionType
ALU = mybir.AluOpType

D = 1024
SEQ = 2048
CTX = 256
NT = 18
NLT = 16
TOK = NT * 128
NH = 8
N_IN = 11296
EPS = 1e-6
QS = 128 ** -0.5


class _Probe:
    def __init__(self):
        self.rec = None

    def __getattr__(self, name):
        def f(*a, **k):
            self.rec = (name, a, k)
            return None
        return f


def _free(ap):
    n = 1
    for v in ap.shape[1:]:
        n *= v
    return n


def _cost(eng, fn, dma):
    pr = _Probe()
    fn(pr)
    name, a, k = pr.rec
    if dma:
        out = k.get("out", a[0] if a else None)
        nbytes = _free(out) * out.shape[0] * (2 if out.dtype == BF16 else 4)
        issue = 1100.0 if eng == "pool" else 100.0
        return issue, issue + 2000.0 + nbytes / 250.0
    if eng == "pe":
        if name == "transpose":
            t = 128 / 2.4 + 3
        else:
            rhs = k.get("rhs", a[2] if len(a) > 2 else None)
            n = _free(rhs)
            t = max(n, 56) / 2.4 + 3
            if rhs.dtype == F32:
                t *= 4
        return t, t + 170.0
    out = k.get("out", a[0] if a else None)
    n = _free(out) if out is not None else 128
    if eng == "act":
        t = (170.0 + n * 1.8) if n <= 128 else (300.0 + n * 0.74)
        if k.get("accum_out") is not None:
            t += 190
    elif eng == "dve":
        t = 200.0 + n * 1.05
    else:
        if name == "tensor_tensor":
            t = 180.0 + n * 2.0
            if k.get("op") == ALU.pow:
                t += 2700
        elif name == "tensor_copy":
            t = 340.0 + n * 2.0
        else:
            t = 250.0 + n * 1.0
    return t, t + 60.0


class _Op:
    __slots__ = ("eng", "fn", "deps", "dma", "idx", "pos", "phase", "busy", "lat", "stt", "vc", "waits")


class Prog:
    ENGS = ("pe", "act", "dve", "pool", "sp")
    KDMA = 16
    SYNC = 120.0

    def __init__(self, nc, reorder=True):
        self.nc = nc
        self.ops = []
        self.last_w = {}
        self.readers = {}
        self.phase = 0
        self.reorder = reorder
        self.after = set()

    def _add(self, eng, fn, reads, writes, dma):
        norm = lambda k: k[:2] if (isinstance(k, tuple) and k[0] == "ps") else k
        reads = tuple(dict.fromkeys(norm(k) for k in reads))
        writes = tuple(dict.fromkeys(norm(k) for k in writes))
        writes = writes + tuple(k for k in reads if isinstance(k, tuple) and k[0] == "ps" and k not in writes)
        op = _Op()
        op.eng, op.fn, op.dma, op.idx, op.phase, op.pos = eng, fn, dma, len(self.ops), self.phase, None
        op.busy, op.lat = _cost(eng, fn, dma)
        deps = set()
        for k in reads:
            if k in self.last_w:
                deps.add(self.last_w[k])
        for k in writes:
            if k in self.last_w:
                deps.add(self.last_w[k])
            for r in self.readers.get(k, ()):
                deps.add(r)
        deps |= self.after
        deps.discard(op)
        op.deps = deps
        for k in reads:
            self.readers.setdefault(k, []).append(op)
        for k in writes:
            self.last_w[k] = op
            self.readers[k] = []
        self.ops.append(op)
        return op

    def op(self, eng, fn, reads=(), writes=()):
        return self._add(eng, fn, tuple(reads), tuple(writes), False)

    def dma(self, eng, fn, reads=(), writes=()):
        return self._add(eng, fn, tuple(reads), tuple(writes), True)

    def keys_named(self, names):
        ks = set(self.last_w) | set(self.readers)
        return [k for k in ks if (k if isinstance(k, str) else k[0]) in names]

    def fence_all_later(self, op):
        self.after.add(op)

    def barrier(self):
        self.after = set()
        self.phase += 1
        self.last_w = {}
        self.readers = {}

    def _schedule(self, ops):
        import heapq
        order = {e: [] for e in self.ENGS}
        if not self.reorder:
            for o in ops:
                o.stt = float(o.idx)
                order[o.eng].append(o)
            return order
        inphase = set(ops)
        succs = {o: [] for o in ops}
        indeg = {}
        for o in ops:
            ds = [d for d in o.deps if d in inphase]
            indeg[o] = len(ds)
            for d in ds:
                succs[d].append(o)
        tail = {}
        for o in reversed(ops):
            t_ = 0.0
            for su in succs[o]:
                t_ = max(t_, tail[su] + (30.0 if su.eng == o.eng else self.SYNC))
            tail[o] = o.lat + t_
        pk = lambda o: (-tail[o], o.idx)
        ready = {o: 0.0 for o in ops}
        fut = {e: [] for e in self.ENGS}
        now = {e: [] for e in self.ENGS}
        for o in ops:
            if indeg[o] == 0:
                heapq.heappush(fut[o.eng], (0.0, o.idx, o))
        free = {e: 0.0 for e in self.ENGS}
        dfin = {e: [] for e in self.ENGS}
        KD = self.KDMA
        left = len(ops)
        while left:
            best = None
            for e in self.ENGS:
                while fut[e] and fut[e][0][0] <= free[e]:
                    rt, idx, o = heapq.heappop(fut[e])
                    heapq.heappush(now[e], (pk(o), o.idx, o))
                if now[e]:
                    _, idx, o = now[e][0]
                    stt = free[e]
                elif fut[e]:
                    rt, idx, o = fut[e][0]
                    stt = max(rt, free[e])
                else:
                    continue
                if o.dma and len(dfin[e]) >= KD:
                    stt = max(stt, dfin[e][-KD])
                if best is None or (stt, pk(o)) < (best[0], pk(best[3])):
                    best = (stt, idx, e, o)
            stt, idx, e, o = best
            if now[e] and now[e][0][2] is o:
                heapq.heappop(now[e])
            else:
                heapq.heappop(fut[e])
            free[e] = stt + o.busy
            o.stt = stt
            fin = stt + o.lat
            if o.dma:
                dfin[e].append(fin)
            order[e].append(o)
            left -= 1
            for su in succs[o]:
                lat = 30.0 if (su.eng == o.eng) else self.SYNC
                ready[su] = max(ready[su], fin + lat)
                indeg[su] -= 1
                if indeg[su] == 0:
                    heapq.heappush(fut[su.eng], (ready[su], su.idx, su))
        return order

    def emit(self, final_wait_eng="sp"):
        nc = self.nc
        nph = self.phase + 1
        phases = [[] for _ in range(nph)]
        for o in self.ops:
            phases[o.phase].append(o)
        streams = {e: [] for e in self.ENGS}
        for ph in range(nph):
            od = self._schedule(phases[ph])
            for e in self.ENGS:
                streams[e].extend(od[e])
        cnt = {e: 0 for e in self.ENGS}
        dcnt = {e: 0 for e in self.ENGS}
        last_c = [dict() for _ in range(nph)]
        last_d = [dict() for _ in range(nph)]
        for e in self.ENGS:
            for o in streams[e]:
                if o.dma:
                    dcnt[e] += 1
                    o.pos = dcnt[e]
                    last_d[o.phase][e] = dcnt[e]
                else:
                    cnt[e] += 1
                    o.pos = cnt[e]
                    last_c[o.phase][e] = cnt[e]
        cum_c = [dict() for _ in range(nph + 1)]
        cum_d = [dict() for _ in range(nph + 1)]
        for ph in range(nph):
            cum_c[ph + 1] = dict(cum_c[ph]); cum_c[ph + 1].update(last_c[ph])
            cum_d[ph + 1] = dict(cum_d[ph]); cum_d[ph + 1].update(last_d[ph])
        self.cnt, self.dcnt = cnt, dcnt
        KD = self.KDMA

        def dkey(p, n):
            return ("d", p, (n - 1) % KD), 16 * ((n - 1) // KD + 1)

        know = {e: {} for e in self.ENGS}
        cur_ph = {e: 0 for e in self.ENGS}

        def need(kn, waits, key, val, vc):
            if kn.get(key, 0) >= val:
                return
            waits.append((key, val))
            kn[key] = val
            if vc:
                for k2, v2 in vc.items():
                    if kn.get(k2, 0) < v2:
                        kn[k2] = v2

        def barrier_waits(E, kn, waits, ph):
            for p, n in cum_c[ph].items():
                if not (p == E == "pe"):
                    need(kn, waits, ("c", p), n, None)
            for p, n in cum_d[ph].items():
                for m in range(max(1, n - KD + 1), n + 1):
                    k_, v_ = dkey(p, m)
                    need(kn, waits, k_, v_, None)

        for o in sorted(self.ops, key=lambda o: (o.phase, o.stt, o.idx)):
            E = o.eng
            kn = know[E]
            waits = []
            if o.phase != cur_ph[E]:
                cur_ph[E] = o.phase
                barrier_waits(E, kn, waits, o.phase)
            deps = [d for d in o.deps if d.phase == o.phase]
            deps.sort(key=lambda d: (-d.stt, d.idx))
            for d in deps:
                if d.dma:
                    k_, v_ = dkey(d.eng, d.pos)
                elif d.eng == E == "pe":
                    continue
                else:
                    k_, v_ = ("c", d.eng), d.pos
                need(kn, waits, k_, v_, d.vc)
            if o.dma and o.pos > KD:
                k_, v_ = dkey(E, o.pos - KD)
                need(kn, waits, k_, v_, None)
            o.waits = waits
            o.vc = dict(kn)
        final_waits = []
        barrier_waits(final_wait_eng, know[final_wait_eng], final_waits, nph)

        with contextlib.ExitStack() as st:
            csem = {e: st.enter_context(nc.semaphore("c_" + e)) for e in self.ENGS if cnt[e]}
            dsem = {e: [st.enter_context(nc.semaphore("d_%s%d" % (e, i))) for i in range(KD)]
                    for e in self.ENGS if dcnt[e]}
            block = st.enter_context(nc.Block())

            def body(ename, eng):
                def do_wait(key, val):
                    if key[0] == "c":
                        eng.wait_ge(csem[key[1]], val)
                    else:
                        eng.wait_ge(dsem[key[1]][key[2]], val)

                for o in streams[ename]:
                    for (key, val) in o.waits:
                        do_wait(key, val)
                    ins = o.fn(eng)
                    if o.dma:
                        ins.then_inc(dsem[ename][(o.pos - 1) % KD], 16)
                    else:
                        ins.then_inc(csem[ename], 1)
                if ename == final_wait_eng:
                    for (key, val) in final_waits:
                        do_wait(key, val)

            @block.tensor
            def _(e):
                body("pe", e)

            @block.scalar
            def _(e):
                body("act", e)

            @block.vector
            def _(e):
                body("dve", e)

            @block.gpsimd
            def _(e):
                body("pool", e)

            @block.sync
            def _(e):
                body("sp", e)


def _prod(s):
    r = 1
    for v in s:
        r *= v
    return r


def build_program(stop_after=99, dbg=None):
    nc = bass.Bass("TRN2", target_bir_lowering=False)

    def din(name, shape):
        return nc.dram_tensor(name, list(shape), F32, kind="ExternalInput").ap()

    x = din("x", [SEQ, D])
    ctx = din("ctx", [CTX, D])
    c_in = din("c", [D])
    cctx_in = din("c_ctx", [D])
    w_ada = din("w_ada", [D, 3 * D])
    b_ada = din("b_ada", [3 * D])
    g_norm = din("g_norm", [D])
    w_in = din("w_in", [D, N_IN])
    b_in = din("b_in", [N_IN])
    w_conv = din("w_conv", [3, D])
    b_conv = din("b_conv", [D])
    g_head = din("g_head", [D])
    w_pa = din("w_pa", [D, D])
    w_pb = din("w_pb", [D, D])
    w_out = din("w_out", [D, D])
    b_out = din("b_out", [D])
    g_final = din("g_final", [D])
    y = nc.dram_tensor("y", [SEQ, D], F32, kind="ExternalOutput").ap()
    dbg_out = {}
    if dbg:
        for name, shape in dbg.items():
            dbg_out[name] = nc.dram_tensor("dbg_" + name, [128, _prod(shape)], F32, kind="ExternalOutput").ap()

    ARENA = 210944
    with contextlib.ExitStack() as st:
        arena = st.enter_context(nc.sbuf_tensor("arena", [128, ARENA // 2], BF16))
        psum = st.enter_context(nc.psum_tensor("psum", [128, 4096], F32))
        P = Prog(nc)

        cur = [0]

        def V(off, shape, dt):
            n = _prod(shape)
            sz = 2 if dt == BF16 else 4
            assert off % 4 == 0 and off + n * sz <= ARENA, (off, shape)
            a = arena[:, off // 2: off // 2 + n * sz // 2]
            if dt != BF16:
                a = a.bitcast(dt)
            if len(shape) == 2:
                a = a.rearrange("p (a b) -> p a b", a=shape[0])
            elif len(shape) == 3:
                a = a.rearrange("p (a b c) -> p a b c", a=shape[0], b=shape[1])
            return a

        def alloc(shape, dt):
            n = _prod(shape) * (2 if dt == BF16 else 4)
            n = (n + 31) // 32 * 32
            off = cur[0]
            cur[0] += n
            return V(off, shape, dt)

        def PS(bank, off_f32, n_f32):
            return psum[:, bank * 512 + off_f32: bank * 512 + off_f32 + n_f32]

        def PSB(bank, off_bf, n_bf):
            return psum[:, bank * 512: (bank + 1) * 512].bitcast(BF16)[:, off_bf: off_bf + n_bf]

        ident_bf = alloc([128], BF16)
        ident_f = alloc([128], F32)
        M_le = alloc([128], F32)
        M_lt = alloc([128], F32)
        M_ge = alloc([128], F32)
        M_gt = alloc([128], F32)
        ones_f = alloc([128], F32)
        M_ge_bf = alloc([128], BF16)
        pp = alloc([128], F32)
        pp2 = alloc([40], F32)
        dp = alloc([64], F32)
        s_f = alloc([16], F32)
        s_t = alloc([16], F32)
        s2 = alloc([16], BF16)
        gts = alloc([NT, 32], F32)
        lf = alloc([NT, 16], F32)
        ea = alloc([NT, 16], F32)
        wk = alloc([NT, 16], F32)
        ern = alloc([NT, 16], F32)
        eB = alloc([NT, 16], F32)
        tg1 = alloc([NT, 16], F32)
        tg2 = alloc([NT, 16], F32)
        hxT = alloc([8, TOK], BF16)
        ybT = alloc([8, SEQ], BF16)
        bg_bc = alloc([32], F32)
        wg = alloc([8, 32], BF16)
        fsc = alloc([16], F32)
        PH = cur[0]
        assert PH % 32 == 0
        whd = [alloc([8, 640], BF16) for _ in range(2)]
        PH2 = cur[0]
        wv_in = w_in.rearrange("(kc p) n -> p kc n", p=128)

        BQ, BK, BREST, WCV, BCV, GHD = 8, 16, 24, 88, 112, 120
        B_O, B_ZB, B_BA, B_CA, B_XA, B_ZA, B_GA, B_GB = [BREST + 8 * i for i in range(8)]
        AX, BX, AC, BC, BOH, BGAH, BGBH, GHH = [8 * i for i in range(8)]

        cur[0] = PH2
        pst = alloc([128], F32)
        pst2 = alloc([128], F32)
        wada = [alloc([8, 1024], BF16) for _ in range(2)]
        modx = alloc([16], F32)
        modc = alloc([16], F32)
        PRO_LATE = cur[0]
        xst = [alloc([D], F32) for _ in range(6)]
        xs = [alloc([D], BF16) for _ in range(4)]
        sqj = alloc([D], BF16)
        ssx = alloc([NT], F32)
        rsx = alloc([NT], F32)
        lnx = alloc([NT], F32)
        assert cur[0] <= ARENA, cur[0]

        def mk_mask(t, pattern_step, cm, op, fill_in, fill_out):
            P.op("pool", lambda e: e.memset(t, fill_in), writes=[("c", id(t))])
            P.op("pool", lambda e: e.affine_select(out=t, in_=t, pattern=[[pattern_step, 128]], compare_op=op,
                                                   fill=fill_out, base=0, channel_multiplier=cm),
                 reads=[("c", id(t))], writes=[("c", id(t))])

        mk_mask(ident_f, -1, 1, ALU.not_equal, 0.0, 1.0)
        mk_mask(M_le, 1, -1, ALU.is_ge, 1.0, 0.0)
        mk_mask(M_lt, 1, -1, ALU.is_gt, 1.0, 0.0)
        mk_mask(M_ge, -1, 1, ALU.is_ge, 1.0, 0.0)
        mk_mask(M_gt, -1, 1, ALU.is_gt, 1.0, 0.0)
        P.op("pool", lambda e: e.memset(ones_f, 1.0), writes=["ones_f"])
        P.op("pool", lambda e: e.tensor_copy(out=ident_bf, in_=ident_f), reads=[("c", id(ident_f))], writes=["ident_bf"])
        P.op("pool", lambda e: e.tensor_copy(out=M_ge_bf, in_=M_ge), reads=[("c", id(M_ge))], writes=["M_ge_bf"])
        P.op("pool", lambda e: e.memset(pst2, 0.0), writes=["pst2", ("pst2", 1), ("pst2", 2), ("pst2", 3)])

        def row(ap1d, n):
            return ap1d.rearrange("(c p) -> c p", p=128)

        P.dma("sp", lambda e: e.dma_start(out=pst[0:8, :], in_=row(g_norm, 8)), writes=[("pst", 0)])
        P.dma("sp", lambda e: e.dma_start(out=pst[8:24, :], in_=row(b_in[0:2048], 16)), writes=[("pst", 1)])
        P.dma("sp", lambda e: e.dma_start(out=pst[24:88, :], in_=row(b_in[3104:3104 + 8192], 64)), writes=[("pst", 2)])
        P.dma("sp", lambda e: e.dma_start(out=pst[88:112, :], in_=w_conv.rearrange("t (c p) -> (t c) p", p=128)),
              writes=[("pst", 3)])
        P.dma("sp", lambda e: e.dma_start(out=pst[112:120, :], in_=row(b_conv, 8)), writes=[("pst", 4)])
        P.dma("sp", lambda e: e.dma_start(out=pst[120:128, :], in_=row(g_head, 8)), writes=[("pst", 5)])
        P.dma("sp", lambda e: e.dma_start(out=pst2[0:16, :], in_=row(b_ada[0:2048], 16)), reads=[], writes=["pst2"])
        P.dma("sp", lambda e: e.dma_start(out=pst2[16:24, :], in_=row(c_in, 8)), writes=[("pst2", 1)])
        P.dma("sp", lambda e: e.dma_start(out=pst2[24:32, :], in_=row(cctx_in, 8)), writes=[("pst2", 2)])
        P.dma("sp", lambda e: e.dma_start(out=pst2[32:40, :], in_=row(b_in[2048:3072], 8)), writes=[("pst2", 3)])
        for (dst, src, nm) in ((bg_bc, b_in[3072:3104], "bg"),):
            P.dma("sp", lambda e, dst=dst, src=src: e.dma_start(out=dst, in_=src.partition_broadcast(128)), writes=[nm])
        wv_ada = w_ada.rearrange("(kc p) n -> p kc n", p=128)
        for j in range(2):
            P.dma("pool", lambda e, j=j: e.dma_start(out=wada[j], in_=wv_ada[:, :, j * 1024:(j + 1) * 1024]),
                  writes=[("wada", j)])

        P.dma("pool", lambda e: e.dma_start(out=wg, in_=wv_in[:, :, 3072:3104]), writes=["wg"])

        def head_cols(h):
            return [h * 128, 1024 + h * 128, 2048 + h * 128, 3104 + h * 128, 3104 + 1024 + h * 128]

        def load_head_w(h, extra=()):
            for j, c0 in enumerate(head_cols(h)):
                P.dma("pool", lambda e, h=h, j=j, c0=c0: e.dma_start(out=whd[h % 2][:, :, j * 128:(j + 1) * 128],
                                                                     in_=wv_in[:, :, c0:c0 + 128]),
                      reads=list(extra), writes=[("whd", h % 2, j)])

        P.op("pe", lambda e: e.transpose(out=PS(0, 0, 128), in_=pst, identity=ident_f),
             reads=[("pst", i) for i in range(6)] + [("c", id(ident_f))], writes=[("ps", 0)])
        P.op("dve", lambda e: e.tensor_copy(out=pp, in_=PS(0, 0, 128)), reads=[("ps", 0)], writes=["pp"])
        P.op("pe", lambda e: e.transpose(out=PS(1, 0, 64), in_=pst2[0:64, :], identity=ident_f[0:64, 0:64]),
             reads=["pst2", ("pst2", 1), ("pst2", 2), ("pst2", 3), ("c", id(ident_f))], writes=[("ps", 1)])
        P.op("dve", lambda e: e.tensor_copy(out=pp2, in_=PS(1, 0, 40)), reads=[("ps", 1)], writes=["pp2"])
        if stop_after <= 1:
            P.emit()
            return nc
        P.op("act", lambda e: e.activation(out=s_t, in_=pp2[:, 16:32], func=AF.Exp, scale=-1.0), reads=["pp2"], writes=["s_t"])
        P.op("dve", lambda e: e.tensor_scalar(out=s_t, in0=s_t, scalar1=1.0, scalar2=None, op0=ALU.add),
             reads=["s_t"], writes=["s_t"])
        P.op("dve", lambda e: e.reciprocal(out=s_t, in_=s_t), reads=["s_t"], writes=["s_t"])
        P.op("dve", lambda e: e.tensor_tensor(out=s_f, in0=s_t, in1=pp2[:, 16:32], op=ALU.mult),
             reads=["s_t", "pp2"], writes=["s_f"])
        P.op("dve", lambda e: e.tensor_copy(out=s2, in_=s_f), reads=["s_f"], writes=["s2"])
        for j in range(2):
            for mc in range(8):
                for kc in range(8):
                    m = j * 8 + mc
                    P.op("pe", lambda e, j=j, mc=mc, kc=kc, m=m: e.matmul(
                        PS(2, 2 * m, 2), lhsT=wada[j][:, kc, mc * 128:(mc + 1) * 128],
                        rhs=s2[:, kc::8], start=(kc == 0), stop=(kc == 7)),
                        reads=[("wada", j), "s2"], writes=[("ps", 2)])
        modv = PS(2, 0, 32).rearrange("p (m n) -> p m n", n=2)
        P.op("dve", lambda e: e.tensor_tensor(out=modx, in0=modv[:, :, 0], in1=pp2[:, 0:16], op=ALU.add),
             reads=[("ps", 2), "pp2"], writes=["modx"])
        P.op("dve", lambda e: e.tensor_tensor(out=modc, in0=modv[:, :, 1], in1=pp2[:, 0:16], op=ALU.add),
             reads=[("ps", 2), "pp2"], writes=["modc"])
        for (mod, a0, b0, nm) in ((modx, AX, BX, "modx"), (modc, AC, BC, "modc")):
            P.op("dve", lambda e, mod=mod, a0=a0: e.scalar_tensor_tensor(
                out=dp[:, a0:a0 + 8], in0=mod[:, 8:16], scalar=1.0, in1=pp[:, 0:8], op0=ALU.add, op1=ALU.mult),
                reads=[nm, "pp"], writes=[("dp", a0)])
            P.op("dve", lambda e, mod=mod, b0=b0: e.tensor_copy(out=dp[:, b0:b0 + 8], in_=mod[:, 0:8]),
                 reads=[nm], writes=[("dp", b0)])
        for (dst, src) in ((BOH, B_O), (BGAH, B_GA), (BGBH, B_GB), (GHH, GHD)):
            P.op("dve", lambda e, dst=dst, src=src: e.tensor_scalar(out=dp[:, dst:dst + 8], in0=pp[:, src:src + 8],
                                                                    scalar1=0.5, scalar2=None, op0=ALU.mult),
                 reads=["pp"], writes=[("dp", dst)])
        if stop_after <= 2:
            P.emit()
            return nc
        def xsrc(tt):
            return ctx[tt * 128:(tt + 1) * 128, :] if tt < 2 else x[(tt - 2) * 128:(tt - 1) * 128, :]

        for tt in range(NT):
            xb = xst[tt % 6]
            P.dma("sp", lambda e, tt=tt, xb=xb: e.dma_start(out=xb, in_=xsrc(tt)), writes=[("xst", tt % 6), ("xld", tt)])
            P.op("act", lambda e, tt=tt, xb=xb: e.activation(out=sqj, in_=xb, func=AF.Square, accum_out=ssx[:, tt:tt + 1]),
                 reads=[("xst", tt % 6)], writes=["sqj", ("ssx", tt)])
            P.op("act", lambda e, tt=tt: e.activation(out=lnx[:, tt:tt + 1], in_=ssx[:, tt:tt + 1], func=AF.Ln,
                                                      scale=1.0 / D, bias=EPS),
                 reads=[("ssx", tt)], writes=[("lnx", tt)])
            P.op("act", lambda e, tt=tt: e.activation(out=rsx[:, tt:tt + 1], in_=lnx[:, tt:tt + 1], func=AF.Exp, scale=-0.5),
                 reads=[("lnx", tt)], writes=[("rsx", tt)])
            xsb = xs[tt % 4]
            P.op("dve", lambda e, tt=tt, xb=xb, xsb=xsb: e.tensor_scalar(out=xsb, in0=xb, scalar1=rsx[:, tt:tt + 1],
                                                                          scalar2=None, op0=ALU.mult),
                 reads=[("xst", tt % 6), ("rsx", tt)], writes=[("xs", tt % 4)])
            grp = tt // 2
            bank0 = (grp % 4) * 2
            half = tt % 2
            for kc in range(8):
                b = bank0 + (kc // 4)
                off = (kc % 4) * 256 + half * 128
                P.op("pe", lambda e, xsb=xsb, kc=kc, b=b, off=off: e.transpose(
                    out=PSB(b, off, 128), in_=xsb[:, kc * 128:(kc + 1) * 128], identity=ident_bf),
                    reads=[("xs", tt % 4), "ident_bf"], writes=[("ps", b, kc % 4, half)])
            if half == 1:
                a0, b0 = (AC, BC) if tt < 2 else (AX, BX)
                for kc in range(8):
                    b = bank0 + (kc // 4)
                    src = PSB(b, (kc % 4) * 256, 256)
                    dst = hxT[:, kc, (tt - 1) * 128:(tt + 1) * 128]
                    rd = [("ps", b, kc % 4, 0), ("ps", b, kc % 4, 1), ("dp", a0), ("dp", b0)]
                    wr = [("hxT", kc, tt - 1), ("hxT", kc, tt)]
                    if kc < 2:
                        P.op("act", lambda e, src=src, dst=dst, kc=kc, a0=a0, b0=b0: e.activation(
                            out=dst, in_=src, func=AF.Identity, scale=dp[:, a0 + kc:a0 + kc + 1],
                            bias=dp[:, b0 + kc:b0 + kc + 1]), reads=rd, writes=wr)
                    else:
                        P.op("dve", lambda e, src=src, dst=dst, kc=kc, a0=a0, b0=b0: e.tensor_scalar(
                            out=dst, in0=src, scalar1=dp[:, a0 + kc:a0 + kc + 1], scalar2=dp[:, b0 + kc:b0 + kc + 1],
                            op0=ALU.mult, op1=ALU.add), reads=rd, writes=wr)

        def dump(name, src_ap, reads):
            if name in dbg_out:
                a = src_ap
                if len(a.shape) == 3:
                    a = a.rearrange("p a b -> p (a b)")
                if a.dtype == BF16:
                    a = a.bitcast(F32)
                P.dma("sp", lambda e: e.dma_start(out=dbg_out[name], in_=a), reads=reads)

        load_head_w(0, [("xld", 9)])
        load_head_w(1, [("xld", 17)])
        if stop_after <= 3:
            P.barrier()
            dump("pp", pp, [])
            dump("dp", dp, [])
            dump("hxT", hxT, [])
            P.emit()
            return nc

        allhx = [("hxT", kc, tt) for kc in range(8) for tt in range(NT)]
        for tt in range(NT):
            b, off = (0, tt * 32) if tt < 16 else (1, (tt - 16) * 32)
            for kc in range(8):
                P.op("pe", lambda e, tt=tt, kc=kc, b=b, off=off: e.matmul(
                    PS(b, off, 32), lhsT=hxT[:, kc, tt * 128:(tt + 1) * 128], rhs=wg[:, kc, :],
                    start=(kc == 0), stop=(kc == 7)), reads=["wg", ("hxT", kc, tt)], writes=[("ps", b)])
        P.op("dve", lambda e: e.tensor_tensor(out=gts[:, 0:16, :], in0=PS(0, 0, 512).rearrange("p (a b) -> p a b", b=32),
                                              in1=bg_bc.unsqueeze(1).to_broadcast([128, 16, 32]), op=ALU.add),
             reads=[("ps", 0), "bg"], writes=[("gts", 0)])
        P.op("dve", lambda e: e.tensor_tensor(out=gts[:, 16:18, :], in0=PS(1, 0, 64).rearrange("p (a b) -> p a b", b=32),
                                              in1=bg_bc.unsqueeze(1).to_broadcast([128, 2, 32]), op=ALU.add),
             reads=[("ps", 1), "bg"], writes=[("gts", 1)])
        allg = [("gts", 0), ("gts", 1)]
        P.op("act", lambda e: e.activation(out=lf[:, :, 0:8], in_=gts[:, :, 8:16], func=AF.Exp, scale=-1.0),
             reads=allg, writes=["lf0"])
        P.op("act", lambda e: e.activation(out=lf[:, :, 8:16], in_=gts[:, :, 24:32], func=AF.Exp, scale=-1.0),
             reads=allg, writes=["lf1"])
        P.op("act", lambda e: e.activation(out=lf, in_=lf, func=AF.Ln, bias=1.0), reads=["lf0", "lf1"], writes=["lf"])
        lf2 = lf.rearrange("p a b -> p (a b)")
        for i, Mk in enumerate((M_le, M_gt, M_ge, M_lt, ones_f)):
            P.op("pe", lambda e, i=i, Mk=Mk: e.matmul(PS(2 + i, 0, 288), lhsT=Mk, rhs=lf2, start=True, stop=True),
                 reads=["lf", ("c", id(Mk)), "ones_f"], writes=[("ps", 2 + i)])

        def cs(i):
            return PS(2 + i, 0, 288).rearrange("p (a b) -> p a b", b=16)
        Pf, Sf, Pb, Sb, Tt = cs(0), cs(1), cs(2), cs(3), cs(4)
        for (half, Pm, Sm, pb_, sb_, ic) in ((0, Pf, Sf, 2, 3, 0), (1, Pb, Sb, 4, 5, 16)):
            sl = slice(half * 8, half * 8 + 8)
            P.op("dve", lambda e, sl=sl, Pm=Pm, ic=ic: e.tensor_tensor(out=tg1[:, :, sl], in0=Pm[:, :, sl],
                                                                       in1=gts[:, :, ic:ic + 8], op=ALU.add),
                 reads=[("ps", pb_)] + allg, writes=[("tg1", half)])
            P.op("act", lambda e, sl=sl: e.activation(out=ea[:, :, sl], in_=tg1[:, :, sl], func=AF.Exp),
                 reads=[("tg1", half)], writes=[("ea", half)])
            P.op("dve", lambda e, sl=sl, Sm=Sm, ic=ic: e.scalar_tensor_tensor(
                out=tg2[:, :, sl], in0=Sm[:, :, sl], scalar=-1.0, in1=gts[:, :, ic:ic + 8], op0=ALU.mult, op1=ALU.add),
                reads=[("ps", sb_)] + allg, writes=[("tg2", half)])
            P.op("act", lambda e, sl=sl: e.activation(out=wk[:, :, sl], in_=tg2[:, :, sl], func=AF.Exp),
                 reads=[("tg2", half)], writes=[("wk", half)])
            P.op("act", lambda e, sl=sl, Pm=Pm: e.activation(out=ern[:, :, sl], in_=Pm[:, :, sl], func=AF.Exp),
                 reads=[("ps", pb_)], writes=[("ern", half)])
        P.op("act", lambda e: e.activation(out=eB, in_=Tt, func=AF.Exp, scale=-1.0), reads=[("ps", 6)], writes=["eB"])
        if stop_after <= 4:
            P.barrier()
            dump("gts", gts, [])
            dump("ea", ea, [])
            dump("wk", wk, [])
            dump("ern", ern, [])
            dump("eB", eB, [])
            P.emit()
            return nc
        S4OUT = {"gts", "lf0", "lf1", "lf", "tg1", "tg2", "ea", "wk", "ern", "eB", "ps"}
        scratch = {"pst", "pst2", "wada", "modx", "modc"}
        late = {"xst", "xs", "sqj", "ssx", "lnx", "rsx", "xld"}
        allk = set(P.last_w) | set(P.readers)
        rd = [k for k in allk if (k if isinstance(k, str) else k[0]) not in (S4OUT | scratch | late | {"whd", "wg", "hxT"})]
        P.fence_all_later(P.op("pool", lambda e: e.memset(fsc[:, 0:1], 0.0), reads=rd, writes=P.keys_named(scratch) + ["S5"]))

        cur[0] = PH2
        qT, kT, gob, kBf, kBb, V1, Cstf, Cstb = [], [], [], [], [], [], [], []
        for _hb in range(2):
            qT.append(alloc([SEQ], BF16))
            kT.append(alloc([TOK], BF16))
            gob.append(alloc([SEQ], BF16))
            _kB = alloc([NT, 2, 128], BF16)
            kBf.append(_kB[:, :, 0, :])
            kBb.append(_kB[:, :, 1, :])
            kBB = (kBB if _hb else []) + [_kB]
            V1.append(alloc([NT, 130], BF16))
            if _hb == 0:
                S5_HB0_END = cur[0]
        _cf, _cb = alloc([NLT, 130], BF16), alloc([NLT, 130], BF16)
        Cstf, Cstb = [_cf, _cf], [_cb, _cb]
        Cf = [alloc([130], F32) for _ in range(2)]
        Cb = [alloc([130], F32) for _ in range(2)]
        Hh = alloc([NLT, 128], F32)
        hn = alloc([NLT, 128], BF16)
        ssh = alloc([NLT], F32)
        rsh = alloc([NLT], F32)
        mhalf = alloc([NLT], F32)
        t_o = [alloc([512], BF16) for _ in range(2)]
        t_z = [alloc([512], BF16) for _ in range(2)]
        hf = [alloc([128], F32) for _ in range(2)]
        sqh = alloc([128], BF16)
        Spf = [alloc([128], BF16) for _ in range(2)]
        Spb = [alloc([128], BF16) for _ in range(2)]
        Spr = [alloc([128], BF16) for _ in range(2)]
        dcl = [alloc([2], F32) for _ in range(2)]
        dab = [alloc([2], F32) for _ in range(2)]
        sfb = [alloc([2], F32) for _ in range(2)]
        vtmp = [alloc([512], BF16) for _ in range(2)]
        tmpg = [alloc([512], BF16) for _ in range(2)]
        wcv0 = alloc([8, 512], BF16)
        assert cur[0] <= ARENA, cur[0]
        S5_END = cur[0]

        P.op("pool", lambda e: e.memset(mhalf, -0.5), writes=["mhalf"])
        P.op("pool", lambda e: e.memset(V1[0], 1.0), writes=[("V1ones", 0)])
        fm_rot = [0]
        tm_rot = [0]

        def gen_proj(h):
            hb = h % 2
            W = whd[hb]
            wkey = lambda j: ("whd", hb, j)
            jmap = {"q": 0, "k": 1, "v": 2, "o": 3, "zb": 4}
            blocks = [(CTX + bi * 512, 512, bi) for bi in range(4)] + [(0, 256, 4)]
            for (tok0, ntk, bi) in blocks:
                fams = ("q", "k", "v", "o", "zb") if bi < 4 else ("k", "v")
                lsl = slice(bi * 512, (bi + 1) * 512)
                gsl = slice(tok0, tok0 + ntk)
                vt = vtmp[bi % 2]
                for fam in fams:
                    j = jmap[fam]
                    b = fm_rot[0] % 4
                    fm_rot[0] += 1
                    for kc in range(8):
                        P.op("pe", lambda e, ntk=ntk, gsl=gsl, lsl=lsl, bi=bi, vt=vt, b=b, j=j, kc=kc: e.matmul(
                            PS(b, 0, ntk), lhsT=W[:, kc, j * 128:(j + 1) * 128], rhs=hxT[:, kc, gsl],
                            start=(kc == 0), stop=(kc == 7)),
                            reads=[wkey(j)] + [("hxT", kc, t_) for t_ in range(tok0 // 128, (tok0 + ntk) // 128)], writes=[("ps", b)])
                    if fam == "q":
                        P.op("dve", lambda e, ntk=ntk, gsl=gsl, lsl=lsl, bi=bi, vt=vt, b=b: e.tensor_scalar(
                            out=qT[hb][:, lsl], in0=PS(b, 0, ntk), scalar1=pp[:, BQ + h:BQ + h + 1], scalar2=QS,
                            op0=ALU.add, op1=ALU.mult), reads=[("ps", b)], writes=[("qT", hb, bi)])
                    elif fam == "k":
                        P.op("act", lambda e, ntk=ntk, gsl=gsl, lsl=lsl, bi=bi, vt=vt, b=b: e.activation(
                            out=kT[hb][:, gsl], in_=PS(b, 0, ntk), func=AF.Identity, bias=pp[:, BK + h:BK + h + 1]),
                            reads=[("ps", b)], writes=[("kT", hb, bi)])
                    elif fam == "v":
                        P.op("dve", lambda e, ntk=ntk, gsl=gsl, lsl=lsl, bi=bi, vt=vt, b=b: e.tensor_scalar(
                            out=vt[:, 0:ntk], in0=PS(b, 0, ntk), scalar1=pp2[:, 32 + h:33 + h], scalar2=None, op0=ALU.add),
                            reads=[("ps", b)], writes=[("vtmp", bi % 2)])
                    elif fam == "o":
                        P.op("act", lambda e, ntk=ntk, gsl=gsl, lsl=lsl, bi=bi, vt=vt, b=b: e.activation(
                            out=t_o[bi % 2], in_=PS(b, 0, ntk), func=AF.Tanh, scale=0.5,
                            bias=dp[:, BOH + h:BOH + h + 1]), reads=[("ps", b)], writes=[("t_o", bi % 2)])
                    else:
                        P.op("act", lambda e, ntk=ntk, gsl=gsl, lsl=lsl, bi=bi, vt=vt, b=b: e.activation(
                            out=t_z[bi % 2], in_=PS(b, 0, ntk), func=AF.Silu, bias=pp[:, B_ZB + h:B_ZB + h + 1]),
                            reads=[("ps", b)], writes=[("t_z", bi % 2)])
                        P.op("pool", lambda e, ntk=ntk, gsl=gsl, lsl=lsl, bi=bi, vt=vt: e.tensor_tensor(out=tmpg[bi % 2], in0=t_o[bi % 2], in1=t_z[bi % 2], op=ALU.mult),
                             reads=[("t_o", bi % 2), ("t_z", bi % 2)], writes=[("tmpg", bi % 2)])
                        P.op("pool", lambda e, ntk=ntk, gsl=gsl, lsl=lsl, bi=bi, vt=vt: e.tensor_tensor(out=gob[hb][:, lsl], in0=tmpg[bi % 2], in1=t_z[bi % 2], op=ALU.add),
                             reads=[("tmpg", bi % 2), ("t_z", bi % 2)], writes=[("gob", hb, bi)])
                    yield
                for ti in range(ntk // 128):
                    tt = (tok0 // 128 + ti)
                    b = fm_rot[0] % 4
                    fm_rot[0] += 1
                    P.op("pe", lambda e, ntk=ntk, gsl=gsl, lsl=lsl, bi=bi, vt=vt, b=b, tt=tt: e.transpose(out=PSB(b, 0, 128), in_=kT[hb][:, tt * 128:(tt + 1) * 128],
                                                                 identity=ident_bf),
                         reads=[("kT", hb, bi), "ident_bf"], writes=[("ps", b)])
                    P.op("pe", lambda e, ntk=ntk, gsl=gsl, lsl=lsl, bi=bi, vt=vt, b=b, ti=ti: e.transpose(out=PSB(b, 128, 128), in_=vt[:, ti * 128:(ti + 1) * 128],
                                                                 identity=ident_bf),
                         reads=[("vtmp", bi % 2), "ident_bf"], writes=[("ps", b)])
                    P.op("dve", lambda e, ntk=ntk, gsl=gsl, lsl=lsl, bi=bi, vt=vt, tt=tt, b=b: e.tensor_tensor(
                        out=kBB[hb][:, tt, :, :], in0=PSB(b, 0, 128).unsqueeze(1).to_broadcast([128, 2, 128]),
                        in1=wk[:, tt, h::8].unsqueeze(2).to_broadcast([128, 2, 128]), op=ALU.mult),
                        reads=[("ps", b), ("wk", 0), ("wk", 1)], writes=[("kBf", hb, tt), ("kBb", hb, tt)])
                    P.op("act", lambda e, ntk=ntk, gsl=gsl, lsl=lsl, bi=bi, vt=vt, tt=tt, b=b: e.activation(
                        out=V1[hb][:, tt, 0:128], in_=PSB(b, 128, 128), func=AF.Copy),
                        reads=[("ps", b), ("V1ones", hb)], writes=[("V1", hb, tt)])
                    if ti % 2 == 1:
                        yield

        def gen_scan(h):
            hb = h % 2
            P.op("pool", lambda e: e.memset(Cf[0], 0.0), writes=[("Cf", 0)])
            P.op("pool", lambda e: e.memset(Cb[0], 0.0), writes=[("Cb", 0)])
            f_order = list(range(0, 17))
            b_order = [1, 0] + list(range(17, 2, -1))
            for step in range(17):
                for (dirn, order, kB, Cs, Cst, col0, bank) in (("f", f_order, kBf, Cf, Cstf, 0, 4), ("b", b_order, kBb, Cb, Cstb, 8, 5)):
                    tt = order[step]
                    src, dst = Cs[step % 2], Cs[(step + 1) % 2]
                    ck = "C" + dirn
                    P.op("pe", lambda e, tt=tt, kB=kB, bank=bank: e.matmul(
                        PS(bank, 0, 129), lhsT=kB[hb][:, tt, :], rhs=V1[hb][:, tt, 0:129], start=True, stop=True),
                        reads=[("kB" + dirn, hb, tt), ("V1", hb, tt)], writes=[("ps", bank)])
                    P.op("dve", lambda e, tt=tt, src=src, dst=dst, bank=bank, col0=col0: e.scalar_tensor_tensor(
                        out=dst[:, 0:129], in0=src[:, 0:129], scalar=eB[:, tt, col0 + h:col0 + h + 1],
                        in1=PS(bank, 0, 129), op0=ALU.mult, op1=ALU.add),
                        reads=[(ck, step % 2), ("ps", bank), "eB"], writes=[(ck, (step + 1) % 2)])
                    if dirn == "f":
                        nxt = tt + 1
                    else:
                        nxt = 17 if step == 1 else (tt - 1 if step >= 2 else None)
                    if nxt is not None and nxt >= 2:
                        if h == NH - 1:
                            P.op("act", lambda e, dst=dst, Cst=Cst, nxt=nxt: e.activation(out=Cst[hb][:, nxt - 2, 0:129], in_=dst[:, 0:129], func=AF.Copy),
                                 reads=[(ck, (step + 1) % 2)], writes=[("Cst" + dirn, 0, nxt)])
                        else:
                            P.op("pool", lambda e, dst=dst, Cst=Cst, nxt=nxt: e.tensor_copy(out=Cst[hb][:, nxt - 2, 0:129], in_=dst[:, 0:129]),
                                 reads=[(ck, (step + 1) % 2)], writes=[("Cst" + dirn, 0, nxt)])
                yield

            def emit_S(lt):
                tt = lt + 2
                s2i = lt % 2
                tsl = slice(lt * 128, (lt + 1) * 128)
                for bank in (4,):
                    P.op("pe", lambda e, tsl=tsl, bank=bank, tt=tt: e.matmul(PS(bank, 0, 128), lhsT=kT[hb][:, tt * 128:(tt + 1) * 128],
                                                                             rhs=qT[hb][:, tsl], start=True, stop=True),
                         reads=[("kT", hb, lt // 4), ("qT", hb, lt // 4)], writes=[("ps", bank)])
                P.op("dve", lambda e, tt=tt, s2i=s2i: e.scalar_tensor_tensor(
                    out=Spf[s2i], in0=PS(4, 0, 128), scalar=ea[:, tt, h:h + 1], in1=M_le, op0=ALU.mult, op1=ALU.mult),
                    reads=[("ps", 4), ("ea", 0)], writes=[("Spf", s2i)])
                P.op("act", lambda e, tt=tt, s2i=s2i: e.activation(out=Spr[s2i], in_=PS(4, 0, 128), func=AF.Copy,
                                                                   scale=ea[:, tt, 8 + h:9 + h]),
                     reads=[("ps", 4), ("ea", 1)], writes=[("Spr", s2i)])
                P.op("pool", lambda e, s2i=s2i: e.tensor_tensor(out=Spb[s2i], in0=Spr[s2i], in1=M_ge_bf, op=ALU.mult),
                     reads=[("Spr", s2i)], writes=[("Spb", s2i)])

            def emit_num(lt):
                tt = lt + 2
                s2i = lt % 2
                tsl = slice(lt * 128, (lt + 1) * 128)
                nb = 6 + s2i
                NUM = PS(nb, 0, 260).rearrange("p (a b) -> p a b", a=2)
                for di, (Sp, Cst, dn) in enumerate(((Spf, Cstf, "f"), (Spb, Cstb, "b"))):
                    P.op("pe", lambda e, Sp=Sp, di=di, NUM=NUM: e.matmul(
                        NUM[:, di, 0:129], lhsT=Sp[s2i], rhs=V1[hb][:, tt, 0:129], start=True, stop=False),
                        reads=[("Sp" + dn, s2i), ("V1", hb, tt)], writes=[("ps", nb)])
                    P.op("pe", lambda e, Cst=Cst, di=di, NUM=NUM: e.matmul(
                        NUM[:, di, 0:129], lhsT=qT[hb][:, tsl], rhs=Cst[hb][:, lt, 0:129], start=False, stop=True),
                        reads=[("qT", hb, lt // 4), ("Cst" + dn, 0, tt)], writes=[("ps", nb)])
                numk = [("ps", nb)]
                P.op("act", lambda e, NUM=NUM: e.activation(out=dab[s2i], in_=NUM[:, :, 128], func=AF.Abs),
                     reads=numk, writes=[("dab", s2i)])
                P.op("dve", lambda e: e.tensor_tensor(out=dcl[s2i], in0=dab[s2i], in1=ern[:, tt, h::8], op=ALU.max),
                     reads=[("dab", s2i), ("ern", 0), ("ern", 1)], writes=[("dcl", s2i)])
                P.op("dve", lambda e: e.reciprocal(out=sfb[s2i], in_=dcl[s2i]), reads=[("dcl", s2i)], writes=[("sfb", s2i)])
                P.op("act", lambda e, NUM=NUM: e.activation(out=hf[s2i], in_=NUM[:, 0, 0:128], func=AF.Copy, scale=sfb[s2i][:, 0:1]),
                     reads=numk + [("sfb", s2i)], writes=[("hf", s2i)])
                P.op("dve", lambda e, NUM=NUM: e.scalar_tensor_tensor(
                    out=Hh[:, lt, :], in0=NUM[:, 1, 0:128], scalar=sfb[s2i][:, 1:2], in1=hf[s2i], op0=ALU.mult, op1=ALU.add),
                    reads=numk + [("sfb", s2i), ("hf", s2i)], writes=[("Hh", lt)])
                P.op("act", lambda e: e.activation(out=sqh, in_=Hh[:, lt, :], func=AF.Square, accum_out=ssh[:, lt:lt + 1]),
                     reads=[("Hh", lt)], writes=["sqh", ("ssh", lt)])

            emit_S(0)
            yield
            for lt in range(NLT):
                if lt + 1 < NLT:
                    emit_S(lt + 1)
                emit_num(lt)
                yield
            allss = [("ssh", lt) for lt in range(NLT)]
            P.op("dve", lambda e: e.tensor_scalar(out=ssh, in0=ssh, scalar1=1.0 / 128, scalar2=EPS, op0=ALU.mult, op1=ALU.add),
                 reads=allss, writes=allss)
            P.op("pool", lambda e: e.tensor_tensor(out=rsh, in0=ssh, in1=mhalf, op=ALU.pow), reads=allss + ["mhalf"], writes=["rsh"])
            for blk in range(4):
                for q4 in range(4):
                    lt = blk * 4 + q4
                    if lt % 2 == 0:
                        P.op("act", lambda e, lt=lt: e.activation(out=hn[:, lt, :], in_=Hh[:, lt, :], func=AF.Copy,
                                                                  scale=rsh[:, lt:lt + 1]),
                             reads=[("Hh", lt), "rsh"], writes=[("hn", lt)])
                    else:
                        P.op("pool", lambda e, lt=lt: e.tensor_scalar(out=hn[:, lt, :], in0=Hh[:, lt, :], scalar1=rsh[:, lt:lt + 1],
                                                                      scalar2=0.0, op0=ALU.mult, op1=ALU.add),
                             reads=[("Hh", lt), "rsh"], writes=[("hn", lt)])
                pslot = 6 + blk % 2
                for q4 in range(4):
                    lt = blk * 4 + q4
                    P.op("pe", lambda e, lt=lt, q4=q4, pslot=pslot: e.transpose(
                        out=PSB(pslot, q4 * 128, 128), in_=hn[:, lt, :], identity=ident_bf),
                        reads=[("hn", lt), "ident_bf"], writes=[("ps", pslot)])
                P.op("dve", lambda e, blk=blk, pslot=pslot: e.scalar_tensor_tensor(
                    out=ybT[:, h, blk * 512:(blk + 1) * 512], in0=PSB(pslot, 0, 512), scalar=dp[:, GHH + h:GHH + h + 1],
                    in1=gob[hb][:, blk * 512:(blk + 1) * 512], op0=ALU.mult, op1=ALU.mult),
                    reads=[("ps", pslot), ("gob", hb, blk)], writes=[("ybT", h, blk)])
                yield
            if h == 0:
                dump("Hh0", Hh, [("Hh", lt) for lt in range(NLT)])

        def drive(gens, weights=None):
            pairs = [(g, (weights[i] if weights else 1)) for i, g in enumerate(gens) if g is not None]
            while pairs:
                for (g, wgt) in list(pairs):
                    for _ in range(wgt):
                        try:
                            next(g)
                        except StopIteration:
                            pairs.remove((g, wgt))
                            break

        OFF_BA, OFF_CA, OFF_XA, OFF_ZA, OFF_GA, OFF_GB = [3104 + 1024 * i for i in range(2, 8)]

        def load_cv_w(c):
            for j, o0 in enumerate((OFF_CA, OFF_XA, OFF_ZA, OFF_BA)):
                P.dma("pool", lambda e, c=c, j=j, o0=o0: e.dma_start(
                    out=wcv[c % 2][:, :, j * 128:(j + 1) * 128], in_=wv_in[:, :, o0 + c * 128:o0 + (c + 1) * 128]),
                    writes=[("wcv", c % 2, j)])

        wcv = [wcv0, None]
        assert PRO_LATE >= S5_HB0_END, (PRO_LATE, S5_HB0_END)
        drive([gen_proj(0)])
        P.fence_all_later(P.op("pool", lambda e: e.memset(fsc[:, 4:5], 0.0), writes=P.keys_named(late) + ["S5b"]))
        P.op("pool", lambda e: e.memset(V1[1], 1.0), writes=[("V1ones", 1)])
        for h in range(NH):
            if h == NH - 2:
                load_cv_w(0)
            nxt = None
            if h + 1 < NH:
                nxt = gen_proj(h + 1)
            if h + 2 < NH:
                load_head_w(h + 2)
            if h == NH - 1 and stop_after > 5:
                break
            drive([nxt, gen_scan(h)])
        if stop_after <= 5:
            P.barrier()
            dump("ybT", ybT, [])
            P.emit()
            return nc

        cur[0] = PH
        yaT = alloc([8, SEQ], BF16)
        assert cur[0] == PH + 32768
        wcv = [wcv0, alloc([8, 512], BF16)]
        tca = [alloc([512], F32) for _ in range(1)]
        tu = [alloc([512], F32) for _ in range(1)]
        assert cur[0] <= S5_HB0_END, (cur[0], S5_HB0_END)
        cur[0] = S5_END
        tcv = [alloc([512], F32) for _ in range(1)]
        tsz = [alloc([512], BF16) for _ in range(1)]
        tba = [alloc([512], BF16) for _ in range(1)]
        assert cur[0] <= ARENA, cur[0]
        old_keys = [("whd", hb_, j) for hb_ in range(2) for j in range(5)]
        old_keys += [(nm, 0, i) for nm in ("qT", "kT", "gob") for i in range(4)]
        old_keys += [(nm, 0, tt) for nm in ("kBf", "kBb", "V1") for tt in range(NT)]
        old_keys += [("V1ones", 0)]
        new_keys = [("yaT", c, blk) for c in range(8) for blk in range(4)] + [("wcv", 1, j) for j in range(4)]
        new_keys += [("tca", 0), ("tu", 0), ("tcv", 0), ("tsz", 0), ("tba", 0)]
        P.op("pool", lambda e: e.memset(fsc[:, 1:2], 0.0), writes=old_keys + new_keys + ["fsc1"])
        rot = [0]

        def gen_conv():
          for c in range(8):
            if c + 1 < 8:
                load_cv_w(c + 1)
            W = wcv[c % 2]
            for blk in range(4):
                tok0 = CTX + blk * 512
                i2 = 0
                banks = []
                for j in range(4):
                    b = rot[0] % 4
                    rot[0] += 1
                    banks.append(b)
                    for kc in range(8):
                        P.op("pe", lambda e, b=b, j=j, kc=kc, tok0=tok0, W=W: e.matmul(
                            PS(b, 0, 512), lhsT=W[:, kc, j * 128:(j + 1) * 128], rhs=hxT[:, kc, tok0:tok0 + 512],
                            start=(kc == 0), stop=(kc == 7)),
                            reads=[("wcv", c % 2, j)] + [("hxT", kc, t_) for t_ in range(tok0 // 128, tok0 // 128 + 4)], writes=[("ps", b)])
                bca, bxa, bza, bba = banks
                P.op("act", lambda e, c=c, i2=i2, bca=bca: e.activation(out=tca[i2], in_=PS(bca, 0, 512), func=AF.Identity,
                                                                        bias=pp[:, B_CA + c:B_CA + c + 1]),
                     reads=[("ps", bca)], writes=[("tca", i2)])
                P.op("dve", lambda e, c=c, i2=i2, bxa=bxa: e.scalar_tensor_tensor(
                    out=tu[i2], in0=PS(bxa, 0, 512), scalar=pp[:, B_XA + c:B_XA + c + 1], in1=tca[i2], op0=ALU.add, op1=ALU.mult),
                    reads=[("ps", bxa), ("tca", i2)], writes=[("tu", i2)])
                P.op("act", lambda e, c=c, i2=i2: e.activation(out=tcv[i2], in_=tu[i2], func=AF.Identity,
                                                               scale=pp[:, WCV + 8 + c:WCV + 9 + c],
                                                               bias=pp[:, BCV + c:BCV + c + 1]),
                     reads=[("tu", i2)], writes=[("tcv", i2)])
                u3 = tu[i2].rearrange("p (r w) -> p r w", w=64)
                c3 = tcv[i2].rearrange("p (r w) -> p r w", w=64)
                P.op("dve", lambda e, c=c, u3=u3, c3=c3: e.scalar_tensor_tensor(
                    out=c3[:, :, 1:64], in0=u3[:, :, 0:63], scalar=pp[:, WCV + c:WCV + c + 1], in1=c3[:, :, 1:64],
                    op0=ALU.mult, op1=ALU.add), reads=[("tu", i2), ("tcv", i2)], writes=[("tcv", i2)])
                P.op("dve", lambda e, c=c, u3=u3, c3=c3: e.scalar_tensor_tensor(
                    out=c3[:, :, 0:63], in0=u3[:, :, 1:64], scalar=pp[:, WCV + 16 + c:WCV + 17 + c], in1=c3[:, :, 0:63],
                    op0=ALU.mult, op1=ALU.add), reads=[("tu", i2), ("tcv", i2)], writes=[("tcv", i2)])
                P.op("act", lambda e, c=c, i2=i2, bza=bza: e.activation(out=tsz[i2], in_=PS(bza, 0, 512), func=AF.Silu,
                                                                        bias=pp[:, B_ZA + c:B_ZA + c + 1]),
                     reads=[("ps", bza)], writes=[("tsz", i2)])
                P.op("dve", lambda e, c=c, i2=i2, bba=bba: e.scalar_tensor_tensor(
                    out=tba[i2], in0=PS(bba, 0, 512), scalar=pp[:, B_BA + c:B_BA + c + 1], in1=tsz[i2], op0=ALU.add, op1=ALU.mult),
                    reads=[("ps", bba), ("tsz", i2)], writes=[("tba", i2)])
                P.op("pool", lambda e, c=c, i2=i2, blk=blk: e.tensor_tensor(
                    out=yaT[:, c, blk * 512:(blk + 1) * 512], in0=tba[i2], in1=tcv[i2], op=ALU.mult),
                    reads=[("tba", i2), ("tcv", i2)], writes=[("yaT", c, blk)])
                yield

        drive([gen_scan(NH - 1), gen_conv()], weights=[3, 1])
        if stop_after <= 6:
            P.barrier()
            dump("yaT", yaT, [])
            P.emit()
            return nc

        cur[0] = PH + 32768
        mgT = alloc([8, SEQ], BF16)
        wmg = [alloc([8, 512], BF16) for _ in range(2)]
        tga = [alloc([512], F32) for _ in range(2)]
        tgb = [alloc([512], F32) for _ in range(2)]
        tmA, tmB = tga, tgb
        assert cur[0] <= PH + 90112, cur[0]
        cur[0] = PH + 90112
        wo = alloc([8, 1024], BF16)
        wada2 = alloc([8, 1024], BF16)
        assert cur[0] <= ARENA, cur[0]
        S5N = {"whd", "qT", "kT", "gob", "kBf", "kBb", "V1", "V1ones", "Cstf", "Cstb", "Cf", "Cb", "Hh", "hn", "ssh", "rsh",
               "mhalf", "t_o", "t_z", "hf", "sqh", "Spf", "Spb", "Spr", "dcl", "dab", "sfb", "vtmp", "tmpg"}
        S6N = {"wcv", "tca", "tu", "tcv", "tsz", "tba"}
        P.op("pool", lambda e: e.memset(fsc[:, 2:3], 0.0),
             writes=P.keys_named(S5N) + [("wmg", i, j) for i in range(2) for j in range(4)] + ["fsc2"])
        P.op("pool", lambda e: e.memset(fsc[:, 3:4], 0.0),
             writes=P.keys_named(S5N | S6N) + [("mgT", m, blk) for m in range(8) for blk in range(4)]
             + [(nm, i) for nm in ("tga", "tgb") for i in range(2)] + ["wo", "wada2", "fsc3"])
        wv_out = w_out.rearrange("(kc p) n -> p kc n", p=128)
        wv_pa = w_pa.rearrange("(kc p) n -> p kc n", p=128)
        wv_pb = w_pb.rearrange("(kc p) n -> p kc n", p=128)

        def load_mg_w(m):
            srcs = (wv_pa[:, :, m * 128:(m + 1) * 128], wv_pb[:, :, m * 128:(m + 1) * 128],
                    wv_in[:, :, OFF_GA + m * 128:OFF_GA + (m + 1) * 128], wv_in[:, :, OFF_GB + m * 128:OFF_GB + (m + 1) * 128])
            for j, s in enumerate(srcs):
                P.dma("pool", lambda e, m=m, j=j, s=s: e.dma_start(out=wmg[m % 2][:, :, j * 128:(j + 1) * 128], in_=s),
                      writes=[("wmg", m % 2, j)])

        load_mg_w(0)
        load_mg_w(1)
        P.dma("pool", lambda e: e.dma_start(out=wo, in_=wv_out), writes=["wo"])
        P.dma("pool", lambda e: e.dma_start(out=wada2, in_=wv_ada[:, :, 2048:3072]), writes=["wada2"])
        rot = [0]
        for m in range(8):
            if 1 <= m and m + 1 < 8:
                load_mg_w(m + 1)
            W = wmg[m % 2]
            for blk in range(4):
                i2 = blk % 2
                tsl = slice(blk * 512, (blk + 1) * 512)
                tok0 = CTX + blk * 512
                banks = []
                for j in range(4):
                    b = rot[0] % 8
                    rot[0] += 1
                    banks.append(b)
                    for kc in range(8):
                        if j == 0:
                            rhs = yaT[:, kc, tsl]
                            rk = [("yaT", kc, blk)]
                        elif j == 1:
                            rhs = ybT[:, kc, tsl]
                            rk = [("ybT", kc, blk)]
                        else:
                            rhs = hxT[:, kc, tok0:tok0 + 512]
                            rk = [("hxT", kc, t_) for t_ in range(tok0 // 128, tok0 // 128 + 4)]
                        P.op("pe", lambda e, b=b, j=j, kc=kc, rhs=rhs, W=W: e.matmul(
                            PS(b, 0, 512), lhsT=W[:, kc, j * 128:(j + 1) * 128], rhs=rhs, start=(kc == 0), stop=(kc == 7)),
                            reads=[("wmg", m % 2, j)] + rk, writes=[("ps", b)])
                bpa, bpb, bga, bgb = banks
                P.op("act", lambda e, m=m, i2=i2, bga=bga: e.activation(out=tga[i2], in_=PS(bga, 0, 512), func=AF.Tanh, scale=0.5,
                                                                        bias=dp[:, BGAH + m:BGAH + m + 1]),
                     reads=[("ps", bga)], writes=[("tga", i2)])
                P.op("act", lambda e, m=m, i2=i2, bgb=bgb: e.activation(out=tgb[i2], in_=PS(bgb, 0, 512), func=AF.Tanh, scale=0.5,
                                                                        bias=dp[:, BGBH + m:BGBH + m + 1]),
                     reads=[("ps", bgb)], writes=[("tgb", i2)])
                P.op("dve", lambda e, i2=i2, bpa=bpa: e.scalar_tensor_tensor(
                    out=tmA[i2], in0=tga[i2], scalar=1.0, in1=PS(bpa, 0, 512), op0=ALU.add, op1=ALU.mult),
                    reads=[("tga", i2), ("ps", bpa)], writes=[("tga", i2)])
                P.op("dve", lambda e, i2=i2, bpb=bpb: e.scalar_tensor_tensor(
                    out=tmB[i2], in0=tgb[i2], scalar=1.0, in1=PS(bpb, 0, 512), op0=ALU.add, op1=ALU.mult),
                    reads=[("tgb", i2), ("ps", bpb)], writes=[("tgb", i2)])
                P.op("pool", lambda e, m=m, i2=i2, tsl=tsl: e.tensor_tensor(out=mgT[:, m, tsl], in0=tmA[i2], in1=tmB[i2], op=ALU.add),
                     reads=[("tga", i2), ("tgb", i2)], writes=[("mgT", m, blk)])
        P.barrier()
        dump("mgT", mgT, [])
        if stop_after <= 7:
            P.emit()
            return nc

        cur[0] = PH
        bada_g = alloc([D], F32)
        gate_bc = alloc([D], F32)
        gfin_bc = alloc([D], F32)
        sbc = alloc([8, 128], BF16)
        ones_bf = alloc([128], BF16)
        bo_row = alloc([D], BF16)
        ssf = alloc([16], F32)
        lsf = alloc([16], F32)
        rsf = alloc([16], F32)
        sqf = alloc([D], BF16)
        NB8 = 4
        xt = [alloc([D], F32) for _ in range(1)]
        xn = [alloc([D], F32) for _ in range(1)]
        assert cur[0] <= PH + 32768, cur[0]
        cur[0] = PH + 65536
        xt += [alloc([D], F32) for _ in range(3)]
        xn += [alloc([D], F32) for _ in range(3)]
        assert cur[0] <= PH + 90112, cur[0]
        P.dma("pool", lambda e: e.dma_start(out=bo_row[0:1, :], in_=b_out.rearrange("(o n) -> o n", o=1)), writes=["bo_row"])
        for (dst, src, nm) in ((bada_g, b_ada[2048:3072], "bada_g"), (gfin_bc, g_final, "gfin")):
            P.dma("sp", lambda e, dst=dst, src=src: e.dma_start(out=dst, in_=src.partition_broadcast(128)), writes=[nm])
        P.op("pool", lambda e: e.memset(ones_bf, 1.0), writes=["ones_bf"])
        P.op("pool", lambda e: e.tensor_scalar(out=bada_g, in0=bada_g, scalar1=0.5, scalar2=0.0, op0=ALU.mult, op1=ALU.add),
             reads=["bada_g"], writes=["bada_g"])
        P.op("pool", lambda e: e.tensor_scalar(out=bo_row[0:1, :], in0=bo_row[0:1, :], scalar1=2.0, scalar2=0.0, op0=ALU.mult, op1=ALU.add),
             reads=["bo_row"], writes=["bo_row"])
        for kc in range(8):
            P.op("dve", lambda e, kc=kc: e.tensor_scalar(out=sbc[:, kc, :], in0=ones_bf, scalar1=s_f[:, kc:kc + 1],
                                                         scalar2=None, op0=ALU.mult),
                 reads=["ones_bf"], writes=[("sbc", kc)])
        for nb in range(2):
            for kc in range(8):
                P.op("pe", lambda e, nb=nb, kc=kc: e.matmul(PS(6 + nb, 0, 512), lhsT=sbc[:, kc, :],
                                                            rhs=wada2[:, kc, nb * 512:(nb + 1) * 512],
                                                            start=(kc == 0), stop=(kc == 7)),
                     reads=[("sbc", kc), "wada2"], writes=[("ps", 6 + nb)])
            P.op("dve", lambda e, nb=nb: e.scalar_tensor_tensor(out=gate_bc[:, nb * 512:(nb + 1) * 512], in0=PS(6 + nb, 0, 512),
                                                                scalar=0.5, in1=bada_g[:, nb * 512:(nb + 1) * 512],
                                                                op0=ALU.mult, op1=ALU.add),
                 reads=[("ps", 6 + nb), "bada_g"], writes=[("gate_bc", nb)])
        gk = [("gate_bc", 0), ("gate_bc", 1)]
        for lt in range(NLT):
            i2 = lt % NB8
            tsl = slice(lt * 128, (lt + 1) * 128)
            P.dma("sp", lambda e, lt=lt, i2=i2: e.dma_start(out=xt[i2], in_=x[lt * 128:(lt + 1) * 128, :]), writes=[("xt", i2)])
            bank0 = (lt % 3) * 2
            for nb in range(2):
                for m in range(8):
                    P.op("pe", lambda e, nb=nb, m=m, tsl=tsl, bank0=bank0: e.matmul(
                        PS(bank0 + nb, 0, 512), lhsT=mgT[:, m, tsl], rhs=wo[:, m, nb * 512:(nb + 1) * 512],
                        start=(m == 0), stop=False), reads=["wo"], writes=[("ps", bank0 + nb)])
                P.op("pe", lambda e, nb=nb, bank0=bank0: e.matmul(
                    PS(bank0 + nb, 0, 512), lhsT=ones_bf[0:1, :], rhs=bo_row[0:1, nb * 512:(nb + 1) * 512],
                    start=False, stop=True), reads=["ones_bf", "bo_row"], writes=[("ps", bank0 + nb)])
                P.op("dve", lambda e, nb=nb, i2=i2, bank0=bank0: e.tensor_tensor(
                    out=xn[i2][:, nb * 512:(nb + 1) * 512], in0=PS(bank0 + nb, 0, 512), in1=gate_bc[:, nb * 512:(nb + 1) * 512],
                    op=ALU.mult), reads=[("ps", bank0 + nb)] + gk, writes=[("xn", i2, nb)])
            xk = [("xn", i2, 0), ("xn", i2, 1)]
            P.op("pool", lambda e, i2=i2: e.tensor_tensor(out=xn[i2], in0=xn[i2], in1=xt[i2], op=ALU.add),
                 reads=xk + [("xt", i2)], writes=xk)
            P.op("act", lambda e, i2=i2, lt=lt: e.activation(out=sqf, in_=xn[i2], func=AF.Square, accum_out=ssf[:, lt:lt + 1]),
                 reads=xk, writes=["sqf", ("ssf", lt)])
            P.op("act", lambda e, lt=lt: e.activation(out=lsf[:, lt:lt + 1], in_=ssf[:, lt:lt + 1], func=AF.Ln, scale=1.0 / D, bias=EPS),
                 reads=[("ssf", lt)], writes=[("lsf", lt)])
            P.op("act", lambda e, lt=lt: e.activation(out=rsf[:, lt:lt + 1], in_=lsf[:, lt:lt + 1], func=AF.Exp, scale=-0.5),
                 reads=[("lsf", lt)], writes=[("rsf", lt)])
            P.op("dve", lambda e, i2=i2, lt=lt: e.scalar_tensor_tensor(
                out=xn[i2], in0=xn[i2], scalar=rsf[:, lt:lt + 1], in1=gfin_bc, op0=ALU.mult, op1=ALU.mult),
                reads=xk + [("rsf", lt), "gfin"], writes=xk)
            P.dma("sp", lambda e, i2=i2, lt=lt: e.dma_start(out=y[lt * 128:(lt + 1) * 128, :], in_=xn[i2]), reads=xk)
        P.emit()
    return nc


_NC_CACHE = {}


def _core_inputs(b, x, c, ctx, c_ctx, w_ada, b_ada, g_norm, w_in, b_in, w_conv, b_conv, g_head, w_pa, w_pb, w_out,
                 b_out, g_final):
    f = lambda a: np.ascontiguousarray(a, dtype=np.float32)
    return {
        "x": f(x[b]), "ctx": f(ctx[b]), "c": f(c[b]), "c_ctx": f(c_ctx),
        "w_ada": f(w_ada[0]), "b_ada": f(b_ada[0]), "g_norm": f(g_norm[0]), "w_in": f(w_in[0]), "b_in": f(b_in[0]),
        "w_conv": f(w_conv[0]), "b_conv": f(b_conv[0]), "g_head": f(g_head[0]), "w_pa": f(w_pa[0]), "w_pb": f(w_pb[0]),
        "w_out": f(w_out[0]), "b_out": f(b_out[0]), "g_final": f(g_final),
    }


def kernel(**inputs):
    if "nc" not in _NC_CACHE:
        _NC_CACHE["nc"] = build_program()
    nc = _NC_CACHE["nc"]
    in_maps = [_core_inputs(b, **inputs) for b in range(8)]
    res = run_bass_kernel_spmd(nc, in_maps, core_ids=list(range(8)))
    return np.stack([np.asarray(r["y"], dtype=np.float32).reshape(SEQ, D) for r in res.results], axis=0)
```

```python
import contextlib
import numpy as np
import concourse.bass as bass
import concourse.mybir as mybir
from concourse.bass_utils import run_bass_kernel_spmd

F32 = mybir.dt.float32
BF16 = mybir.dt.bfloat16
AF = mybir.ActivationFunctionType
ALU = mybir.AluOpType

D = 1024
SEQ = 2048
CTX = 256
NT = 18
NLT = 16
TOK = NT * 128
NH = 8
N_IN = 11296
EPS = 1e-6
QS = 128 ** -0.5


class _Probe:
    def __init__(self):
        self.rec = None

    def __getattr__(self, name):
        def f(*a, **k):
            self.rec = (name, a, k)
            return None
        return f


def _free(ap):
    n = 1
    for v in ap.shape[1:]:
        n *= v
    return n


def _cost(eng, fn, dma):
    pr = _Probe()
    fn(pr)
    name, a, k = pr.rec
    if dma:
        out = k.get("out", a[0] if a else None)
        nbytes = _free(out) * out.shape[0] * (2 if out.dtype == BF16 else 4)
        issue = 1100.0 if eng == "pool" else 100.0
        return issue, issue + 2000.0 + nbytes / 250.0
    if eng == "pe":
        if name == "transpose":
            t = 128 / 2.4 + 3
        else:
            rhs = k.get("rhs", a[2] if len(a) > 2 else None)
            n = _free(rhs)
            t = max(n, 56) / 2.4 + (45 if n <= 130 else 5)
            if rhs.dtype == F32:
                t *= 4
        return t, t + 170.0
    out = k.get("out", a[0] if a else None)
    n = _free(out) if out is not None else 128
    if eng == "act":
        t = (170.0 + n * 1.8) if n <= 128 else (300.0 + n * 0.74)
        if k.get("accum_out") is not None:
            t += 190
    elif eng == "dve":
        t = 200.0 + n * 1.05
    else:
        if name == "tensor_tensor":
            t = 180.0 + n * 2.0
            if k.get("op") == ALU.pow:
                t += 2700
        elif name == "tensor_copy":
            t = 340.0 + n * 2.0
        else:
            t = 250.0 + n * 1.0
    return t, t + 60.0


class _Op:
    __slots__ = ("eng", "fn", "deps", "dma", "idx", "pos", "phase", "busy", "lat", "stt", "vc", "waits")


class Prog:
    ENGS = ("pe", "act", "dve", "pool", "sp")
    KDMA = 16
    SYNC = 120.0

    def __init__(self, nc, reorder=True):
        self.nc = nc
        self.ops = []
        self.last_w = {}
        self.readers = {}
        self.phase = 0
        self.reorder = reorder
        self.after = set()

    def _add(self, eng, fn, reads, writes, dma):
        norm = lambda k: k[:2] if (isinstance(k, tuple) and k[0] == "ps") else k
        reads = tuple(dict.fromkeys(norm(k) for k in reads))
        writes = tuple(dict.fromkeys(norm(k) for k in writes))
        writes = writes + tuple(k for k in reads if isinstance(k, tuple) and k[0] == "ps" and k not in writes)
        op = _Op()
        op.eng, op.fn, op.dma, op.idx, op.phase, op.pos = eng, fn, dma, len(self.ops), self.phase, None
        op.busy, op.lat = _cost(eng, fn, dma)
        deps = set()
        for k in reads:
            if k in self.last_w:
                deps.add(self.last_w[k])
        for k in writes:
            if k in self.last_w:
                deps.add(self.last_w[k])
            for r in self.readers.get(k, ()):
                deps.add(r)
        deps |= self.after
        deps.discard(op)
        op.deps = deps
        for k in reads:
            self.readers.setdefault(k, []).append(op)
        for k in writes:
            self.last_w[k] = op
            self.readers[k] = []
        self.ops.append(op)
        return op

    def op(self, eng, fn, reads=(), writes=()):
        return self._add(eng, fn, tuple(reads), tuple(writes), False)

    def dma(self, eng, fn, reads=(), writes=()):
        return self._add(eng, fn, tuple(reads), tuple(writes), True)

    def keys_named(self, names):
        ks = set(self.last_w) | set(self.readers)
        return [k for k in ks if (k if isinstance(k, str) else k[0]) in names]

    def fence_all_later(self, op):
        self.after.add(op)

    def barrier(self):
        self.after = set()
        self.phase += 1
        self.last_w = {}
        self.readers = {}

    def _schedule(self, ops):
        import heapq
        order = {e: [] for e in self.ENGS}
        if not self.reorder:
            for o in ops:
                o.stt = float(o.idx)
                order[o.eng].append(o)
            return order
        inphase = set(ops)
        succs = {o: [] for o in ops}
        indeg = {}
        for o in ops:
            ds = [d for d in o.deps if d in inphase]
            indeg[o] = len(ds)
            for d in ds:
                succs[d].append(o)
        tail = {}
        for o in reversed(ops):
            t_ = 0.0
            for su in succs[o]:
                t_ = max(t_, tail[su] + (30.0 if su.eng == o.eng else self.SYNC))
            tail[o] = o.lat + t_
        pk = lambda o: (-tail[o], o.idx)
        ready = {o: 0.0 for o in ops}
        fut = {e: [] for e in self.ENGS}
        now = {e: [] for e in self.ENGS}
        for o in ops:
            if indeg[o] == 0:
                heapq.heappush(fut[o.eng], (0.0, o.idx, o))
        free = {e: 0.0 for e in self.ENGS}
        dfin = {e: [] for e in self.ENGS}
        KD = self.KDMA
        left = len(ops)
        while left:
            best = None
            for e in self.ENGS:
                while fut[e] and fut[e][0][0] <= free[e]:
                    rt, idx, o = heapq.heappop(fut[e])
                    heapq.heappush(now[e], (pk(o), o.idx, o))
                if now[e]:
                    _, idx, o = now[e][0]
                    stt = free[e]
                elif fut[e]:
                    rt, idx, o = fut[e][0]
                    stt = max(rt, free[e])
                else:
                    continue
                if o.dma and len(dfin[e]) >= KD:
                    stt = max(stt, dfin[e][-KD])
                if best is None or (stt, pk(o)) < (best[0], pk(best[3])):
                    best = (stt, idx, e, o)
            stt, idx, e, o = best
            if now[e] and now[e][0][2] is o:
                heapq.heappop(now[e])
            else:
                heapq.heappop(fut[e])
            free[e] = stt + o.busy
            o.stt = stt
            fin = stt + o.lat
            if o.dma:
                dfin[e].append(fin)
            order[e].append(o)
            left -= 1
            for su in succs[o]:
                lat = 30.0 if (su.eng == o.eng) else self.SYNC
                ready[su] = max(ready[su], fin + lat)
                indeg[su] -= 1
                if indeg[su] == 0:
                    heapq.heappush(fut[su.eng], (ready[su], su.idx, su))
        return order

    def emit(self, final_wait_eng="sp"):
        nc = self.nc
        nph = self.phase + 1
        phases = [[] for _ in range(nph)]
        for o in self.ops:
            phases[o.phase].append(o)
        streams = {e: [] for e in self.ENGS}
        for ph in range(nph):
            od = self._schedule(phases[ph])
            for e in self.ENGS:
                streams[e].extend(od[e])
        cnt = {e: 0 for e in self.ENGS}
        dcnt = {e: 0 for e in self.ENGS}
        last_c = [dict() for _ in range(nph)]
        last_d = [dict() for _ in range(nph)]
        for e in self.ENGS:
            for o in streams[e]:
                if o.dma:
                    dcnt[e] += 1
                    o.pos = dcnt[e]
                    last_d[o.phase][e] = dcnt[e]
                else:
                    cnt[e] += 1
                    o.pos = cnt[e]
                    last_c[o.phase][e] = cnt[e]
        cum_c = [dict() for _ in range(nph + 1)]
        cum_d = [dict() for _ in range(nph + 1)]
        for ph in range(nph):
            cum_c[ph + 1] = dict(cum_c[ph]); cum_c[ph + 1].update(last_c[ph])
            cum_d[ph + 1] = dict(cum_d[ph]); cum_d[ph + 1].update(last_d[ph])
        self.cnt, self.dcnt = cnt, dcnt
        KD = self.KDMA

        def dkey(p, n):
            return ("d", p, (n - 1) % KD), 16 * ((n - 1) // KD + 1)

        know = {e: {} for e in self.ENGS}
        cur_ph = {e: 0 for e in self.ENGS}

        def need(kn, waits, key, val, vc):
            if kn.get(key, 0) >= val:
                return
            waits.append((key, val))
            kn[key] = val
            if vc:
                for k2, v2 in vc.items():
                    if kn.get(k2, 0) < v2:
                        kn[k2] = v2

        def barrier_waits(E, kn, waits, ph):
            for p, n in cum_c[ph].items():
                if not (p == E == "pe"):
                    need(kn, waits, ("c", p), n, None)
            for p, n in cum_d[ph].items():
                for m in range(max(1, n - KD + 1), n + 1):
                    k_, v_ = dkey(p, m)
                    need(kn, waits, k_, v_, None)

        for o in sorted(self.ops, key=lambda o: (o.phase, o.stt, o.idx)):
            E = o.eng
            kn = know[E]
            waits = []
            if o.phase != cur_ph[E]:
                cur_ph[E] = o.phase
                barrier_waits(E, kn, waits, o.phase)
            deps = [d for d in o.deps if d.phase == o.phase]
            deps.sort(key=lambda d: (-d.stt, d.idx))
            for d in deps:
                if d.dma:
                    k_, v_ = dkey(d.eng, d.pos)
                elif d.eng == E == "pe":
                    continue
                else:
                    k_, v_ = ("c", d.eng), d.pos
                need(kn, waits, k_, v_, d.vc)
            if o.dma and o.pos > KD:
                k_, v_ = dkey(E, o.pos - KD)
                need(kn, waits, k_, v_, None)
            o.waits = waits
            o.vc = dict(kn)
        final_waits = []
        barrier_waits(final_wait_eng, know[final_wait_eng], final_waits, nph)

        with contextlib.ExitStack() as st:
            csem = {e: st.enter_context(nc.semaphore("c_" + e)) for e in self.ENGS if cnt[e]}
            dsem = {e: [st.enter_context(nc.semaphore("d_%s%d" % (e, i))) for i in range(KD)]
                    for e in self.ENGS if dcnt[e]}
            block = st.enter_context(nc.Block())

            def body(ename, eng):
                def do_wait(key, val):
                    if key[0] == "c":
                        eng.wait_ge(csem[key[1]], val)
                    else:
                        eng.wait_ge(dsem[key[1]][key[2]], val)

                for o in streams[ename]:
                    for (key, val) in o.waits:
                        do_wait(key, val)
                    ins = o.fn(eng)
                    if o.dma:
                        ins.then_inc(dsem[ename][(o.pos - 1) % KD], 16)
                    else:
                        ins.then_inc(csem[ename], 1)
                if ename == final_wait_eng:
                    for (key, val) in final_waits:
                        do_wait(key, val)

            @block.tensor
            def _(e):
                body("pe", e)

            @block.scalar
            def _(e):
                body("act", e)

            @block.vector
            def _(e):
                body("dve", e)

            @block.gpsimd
            def _(e):
                body("pool", e)

            @block.sync
            def _(e):
                body("sp", e)


def _prod(s):
    r = 1
    for v in s:
        r *= v
    return r


def build_program(stop_after=99, dbg=None):
    nc = bass.Bass("TRN2", target_bir_lowering=False)

    def din(name, shape):
        return nc.dram_tensor(name, list(shape), F32, kind="ExternalInput").ap()

    x = din("x", [SEQ, D])
    ctx = din("ctx", [CTX, D])
    c_in = din("c", [D])
    cctx_in = din("c_ctx", [D])
    w_ada = din("w_ada", [D, 3 * D])
    b_ada = din("b_ada", [3 * D])
    g_norm = din("g_norm", [D])
    w_in = din("w_in", [D, N_IN])
    b_in = din("b_in", [N_IN])
    w_conv = din("w_conv", [3, D])
    b_conv = din("b_conv", [D])
    g_head = din("g_head", [D])
    w_pa = din("w_pa", [D, D])
    w_pb = din("w_pb", [D, D])
    w_out = din("w_out", [D, D])
    b_out = din("b_out", [D])
    g_final = din("g_final", [D])
    y = nc.dram_tensor("y", [SEQ, D], F32, kind="ExternalOutput").ap()
    dbg_out = {}
    if dbg:
        for name, shape in dbg.items():
            dbg_out[name] = nc.dram_tensor("dbg_" + name, [128, _prod(shape)], F32, kind="ExternalOutput").ap()

    ARENA = 210944
    with contextlib.ExitStack() as st:
        arena = st.enter_context(nc.sbuf_tensor("arena", [128, ARENA // 2], BF16))
        psum = st.enter_context(nc.psum_tensor("psum", [128, 4096], F32))
        P = Prog(nc)

        cur = [0]

        def V(off, shape, dt):
            n = _prod(shape)
            sz = 2 if dt == BF16 else 4
            assert off % 4 == 0 and off + n * sz <= ARENA, (off, shape)
            a = arena[:, off // 2: off // 2 + n * sz // 2]
            if dt != BF16:
                a = a.bitcast(dt)
            if len(shape) == 2:
                a = a.rearrange("p (a b) -> p a b", a=shape[0])
            elif len(shape) == 3:
                a = a.rearrange("p (a b c) -> p a b c", a=shape[0], b=shape[1])
            return a

        def alloc(shape, dt):
            n = _prod(shape) * (2 if dt == BF16 else 4)
            n = (n + 31) // 32 * 32
            off = cur[0]
            cur[0] += n
            return V(off, shape, dt)

        def PS(bank, off_f32, n_f32):
            return psum[:, bank * 512 + off_f32: bank * 512 + off_f32 + n_f32]

        def PSB(bank, off_bf, n_bf):
            return psum[:, bank * 512: (bank + 1) * 512].bitcast(BF16)[:, off_bf: off_bf + n_bf]

        ident_bf = alloc([128], BF16)
        ident_f = alloc([128], F32)
        M_le = alloc([128], F32)
        M_lt = alloc([128], F32)
        M_ge = alloc([128], F32)
        M_gt = alloc([128], F32)
        ones_f = alloc([128], F32)
        M_ge_bf = alloc([128], BF16)
        pp = alloc([128], F32)
        pp2 = alloc([40], F32)
        dp = alloc([64], F32)
        s_f = alloc([16], F32)
        s_t = alloc([16], F32)
        s2 = alloc([16], BF16)
        gts = alloc([NT, 32], F32)
        lf = alloc([NT, 16], F32)
        ea = alloc([NT, 16], F32)
        wk = alloc([NT, 16], F32)
        ern = alloc([NT, 16], F32)
        eB = alloc([NT, 16], F32)
        tg1 = alloc([NT, 16], F32)
        tg2 = alloc([NT, 16], F32)
        hxT = alloc([8, TOK], BF16)
        ybT = alloc([8, SEQ], BF16)
        bg_bc = alloc([32], F32)
        wg = alloc([8, 32], BF16)
        fsc = alloc([16], F32)
        PH = cur[0]
        assert PH % 32 == 0
        whd = [alloc([8, 640], BF16) for _ in range(2)]
        PH2 = cur[0]
        wv_in = w_in.rearrange("(kc p) n -> p kc n", p=128)

        BQ, BK, BREST, WCV, BCV, GHD = 8, 16, 24, 88, 112, 120
        B_O, B_ZB, B_BA, B_CA, B_XA, B_ZA, B_GA, B_GB = [BREST + 8 * i for i in range(8)]
        AX, BX, AC, BC, BOH, BGAH, BGBH, GHH = [8 * i for i in range(8)]

        cur[0] = PH2
        pst = alloc([128], F32)
        pst2 = alloc([128], F32)
        wada = [alloc([8, 1024], BF16) for _ in range(2)]
        modx = alloc([16], F32)
        modc = alloc([16], F32)
        PRO_LATE = cur[0]
        xst = [alloc([D], F32) for _ in range(6)]
        xs = [alloc([D], BF16) for _ in range(4)]
        sqj = alloc([D], BF16)
        ssx = alloc([NT], F32)
        rsx = alloc([NT], F32)
        lnx = alloc([NT], F32)
        assert cur[0] <= ARENA, cur[0]

        def mk_mask(t, pattern_step, cm, op, fill_in, fill_out):
            P.op("pool", lambda e: e.memset(t, fill_in), writes=[("c", id(t))])
            P.op("pool", lambda e: e.affine_select(out=t, in_=t, pattern=[[pattern_step, 128]], compare_op=op,
                                                   fill=fill_out, base=0, channel_multiplier=cm),
                 reads=[("c", id(t))], writes=[("c", id(t))])

        mk_mask(ident_f, -1, 1, ALU.not_equal, 0.0, 1.0)
        mk_mask(M_le, 1, -1, ALU.is_ge, 1.0, 0.0)
        mk_mask(M_lt, 1, -1, ALU.is_gt, 1.0, 0.0)
        mk_mask(M_ge, -1, 1, ALU.is_ge, 1.0, 0.0)
        mk_mask(M_gt, -1, 1, ALU.is_gt, 1.0, 0.0)
        P.op("pool", lambda e: e.memset(ones_f, 1.0), writes=["ones_f"])
        P.op("pool", lambda e: e.tensor_copy(out=ident_bf, in_=ident_f), reads=[("c", id(ident_f))], writes=["ident_bf"])
        P.op("pool", lambda e: e.tensor_copy(out=M_ge_bf, in_=M_ge), reads=[("c", id(M_ge))], writes=["M_ge_bf"])
        P.op("pool", lambda e: e.memset(pst2, 0.0), writes=["pst2", ("pst2", 1), ("pst2", 2), ("pst2", 3)])

        def row(ap1d, n):
            return ap1d.rearrange("(c p) -> c p", p=128)

        P.dma("sp", lambda e: e.dma_start(out=pst[0:8, :], in_=row(g_norm, 8)), writes=[("pst", 0)])
        P.dma("sp", lambda e: e.dma_start(out=pst[8:24, :], in_=row(b_in[0:2048], 16)), writes=[("pst", 1)])
        P.dma("sp", lambda e: e.dma_start(out=pst[24:88, :], in_=row(b_in[3104:3104 + 8192], 64)), writes=[("pst", 2)])
        P.dma("sp", lambda e: e.dma_start(out=pst[88:112, :], in_=w_conv.rearrange("t (c p) -> (t c) p", p=128)),
              writes=[("pst", 3)])
        P.dma("sp", lambda e: e.dma_start(out=pst[112:120, :], in_=row(b_conv, 8)), writes=[("pst", 4)])
        P.dma("sp", lambda e: e.dma_start(out=pst[120:128, :], in_=row(g_head, 8)), writes=[("pst", 5)])
        P.dma("sp", lambda e: e.dma_start(out=pst2[0:16, :], in_=row(b_ada[0:2048], 16)), reads=[], writes=["pst2"])
        P.dma("sp", lambda e: e.dma_start(out=pst2[16:24, :], in_=row(c_in, 8)), writes=[("pst2", 1)])
        P.dma("sp", lambda e: e.dma_start(out=pst2[24:32, :], in_=row(cctx_in, 8)), writes=[("pst2", 2)])
        P.dma("sp", lambda e: e.dma_start(out=pst2[32:40, :], in_=row(b_in[2048:3072], 8)), writes=[("pst2", 3)])
        for (dst, src, nm) in ((bg_bc, b_in[3072:3104], "bg"),):
            P.dma("sp", lambda e, dst=dst, src=src: e.dma_start(out=dst, in_=src.partition_broadcast(128)), writes=[nm])
        wv_ada = w_ada.rearrange("(kc p) n -> p kc n", p=128)
        for j in range(2):
            P.dma("pool", lambda e, j=j: e.dma_start(out=wada[j], in_=wv_ada[:, :, j * 1024:(j + 1) * 1024]),
                  writes=[("wada", j)])

        P.dma("pool", lambda e: e.dma_start(out=wg, in_=wv_in[:, :, 3072:3104]), writes=["wg"])

        def head_cols(h):
            return [h * 128, 1024 + h * 128, 2048 + h * 128, 3104 + h * 128, 3104 + 1024 + h * 128]

        def load_head_w(h, extra=()):
            for j, c0 in enumerate(head_cols(h)):
                P.dma("pool", lambda e, h=h, j=j, c0=c0: e.dma_start(out=whd[h % 2][:, :, j * 128:(j + 1) * 128],
                                                                     in_=wv_in[:, :, c0:c0 + 128]),
                      reads=list(extra), writes=[("whd", h % 2, j)])

        P.op("pe", lambda e: e.transpose(out=PS(0, 0, 128), in_=pst, identity=ident_f),
             reads=[("pst", i) for i in range(6)] + [("c", id(ident_f))], writes=[("ps", 0)])
        P.op("dve", lambda e: e.tensor_copy(out=pp, in_=PS(0, 0, 128)), reads=[("ps", 0)], writes=["pp"])
        P.op("pe", lambda e: e.transpose(out=PS(1, 0, 64), in_=pst2[0:64, :], identity=ident_f[0:64, 0:64]),
             reads=["pst2", ("pst2", 1), ("pst2", 2), ("pst2", 3), ("c", id(ident_f))], writes=[("ps", 1)])
        P.op("dve", lambda e: e.tensor_copy(out=pp2, in_=PS(1, 0, 40)), reads=[("ps", 1)], writes=["pp2"])
        if stop_after <= 1:
            P.emit()
            return nc
        P.op("act", lambda e: e.activation(out=s_t, in_=pp2[:, 16:32], func=AF.Exp, scale=-1.0), reads=["pp2"], writes=["s_t"])
        P.op("dve", lambda e: e.tensor_scalar(out=s_t, in0=s_t, scalar1=1.0, scalar2=None, op0=ALU.add),
             reads=["s_t"], writes=["s_t"])
        P.op("dve", lambda e: e.reciprocal(out=s_t, in_=s_t), reads=["s_t"], writes=["s_t"])
        P.op("dve", lambda e: e.tensor_tensor(out=s_f, in0=s_t, in1=pp2[:, 16:32], op=ALU.mult),
             reads=["s_t", "pp2"], writes=["s_f"])
        P.op("dve", lambda e: e.tensor_copy(out=s2, in_=s_f), reads=["s_f"], writes=["s2"])
        for j in range(2):
            for mc in range(8):
                for kc in range(8):
                    m = j * 8 + mc
                    P.op("pe", lambda e, j=j, mc=mc, kc=kc, m=m: e.matmul(
                        PS(2, 2 * m, 2), lhsT=wada[j][:, kc, mc * 128:(mc + 1) * 128],
                        rhs=s2[:, kc::8], start=(kc == 0), stop=(kc == 7)),
                        reads=[("wada", j), "s2"], writes=[("ps", 2)])
        modv = PS(2, 0, 32).rearrange("p (m n) -> p m n", n=2)
        P.op("dve", lambda e: e.tensor_tensor(out=modx, in0=modv[:, :, 0], in1=pp2[:, 0:16], op=ALU.add),
             reads=[("ps", 2), "pp2"], writes=["modx"])
        P.op("dve", lambda e: e.tensor_tensor(out=modc, in0=modv[:, :, 1], in1=pp2[:, 0:16], op=ALU.add),
             reads=[("ps", 2), "pp2"], writes=["modc"])
        for (mod, a0, b0, nm) in ((modx, AX, BX, "modx"), (modc, AC, BC, "modc")):
            P.op("dve", lambda e, mod=mod, a0=a0: e.scalar_tensor_tensor(
                out=dp[:, a0:a0 + 8], in0=mod[:, 8:16], scalar=1.0, in1=pp[:, 0:8], op0=ALU.add, op1=ALU.mult),
                reads=[nm, "pp"], writes=[("dp", a0)])
            P.op("dve", lambda e, mod=mod, b0=b0: e.tensor_copy(out=dp[:, b0:b0 + 8], in_=mod[:, 0:8]),
                 reads=[nm], writes=[("dp", b0)])
        for (dst, src) in ((BOH, B_O), (BGAH, B_GA), (BGBH, B_GB), (GHH, GHD)):
            P.op("dve", lambda e, dst=dst, src=src: e.tensor_scalar(out=dp[:, dst:dst + 8], in0=pp[:, src:src + 8],
                                                                    scalar1=0.5, scalar2=None, op0=ALU.mult),
                 reads=["pp"], writes=[("dp", dst)])
        if stop_after <= 2:
            P.emit()
            return nc
        def xsrc(tt):
            return ctx[tt * 128:(tt + 1) * 128, :] if tt < 2 else x[(tt - 2) * 128:(tt - 1) * 128, :]

        for tt in range(NT):
            xb = xst[tt % 6]
            P.dma("sp", lambda e, tt=tt, xb=xb: e.dma_start(out=xb, in_=xsrc(tt)), writes=[("xst", tt % 6), ("xld", tt)])
            P.op("act", lambda e, tt=tt, xb=xb: e.activation(out=sqj, in_=xb, func=AF.Square, accum_out=ssx[:, tt:tt + 1]),
                 reads=[("xst", tt % 6)], writes=["sqj", ("ssx", tt)])
            P.op("act", lambda e, tt=tt: e.activation(out=lnx[:, tt:tt + 1], in_=ssx[:, tt:tt + 1], func=AF.Ln,
                                                      scale=1.0 / D, bias=EPS),
                 reads=[("ssx", tt)], writes=[("lnx", tt)])
            P.op("act", lambda e, tt=tt: e.activation(out=rsx[:, tt:tt + 1], in_=lnx[:, tt:tt + 1], func=AF.Exp, scale=-0.5),
                 reads=[("lnx", tt)], writes=[("rsx", tt)])
            xsb = xs[tt % 4]
            P.op("dve", lambda e, tt=tt, xb=xb, xsb=xsb: e.tensor_scalar(out=xsb, in0=xb, scalar1=rsx[:, tt:tt + 1],
                                                                          scalar2=None, op0=ALU.mult),
                 reads=[("xst", tt % 6), ("rsx", tt)], writes=[("xs", tt % 4)])
            grp = tt // 2
            bank0 = (grp % 4) * 2
            half = tt % 2
            for kc in range(8):
                b = bank0 + (kc // 4)
                off = (kc % 4) * 256 + half * 128
                P.op("pe", lambda e, xsb=xsb, kc=kc, b=b, off=off: e.transpose(
                    out=PSB(b, off, 128), in_=xsb[:, kc * 128:(kc + 1) * 128], identity=ident_bf),
                    reads=[("xs", tt % 4), "ident_bf"], writes=[("ps", b, kc % 4, half)])
            if half == 1:
                a0, b0 = (AC, BC) if tt < 2 else (AX, BX)
                for kc in range(8):
                    b = bank0 + (kc // 4)
                    src = PSB(b, (kc % 4) * 256, 256)
                    dst = hxT[:, kc, (tt - 1) * 128:(tt + 1) * 128]
                    rd = [("ps", b, kc % 4, 0), ("ps", b, kc % 4, 1), ("dp", a0), ("dp", b0)]
                    wr = [("hxT", kc, tt - 1), ("hxT", kc, tt)]
                    if kc < 2:
                        P.op("act", lambda e, src=src, dst=dst, kc=kc, a0=a0, b0=b0: e.activation(
                            out=dst, in_=src, func=AF.Identity, scale=dp[:, a0 + kc:a0 + kc + 1],
                            bias=dp[:, b0 + kc:b0 + kc + 1]), reads=rd, writes=wr)
                    else:
                        P.op("dve", lambda e, src=src, dst=dst, kc=kc, a0=a0, b0=b0: e.tensor_scalar(
                            out=dst, in0=src, scalar1=dp[:, a0 + kc:a0 + kc + 1], scalar2=dp[:, b0 + kc:b0 + kc + 1],
                            op0=ALU.mult, op1=ALU.add), reads=rd, writes=wr)

        def dump(name, src_ap, reads):
            if name in dbg_out:
                a = src_ap
                if len(a.shape) == 3:
                    a = a.rearrange("p a b -> p (a b)")
                if a.dtype == BF16:
                    a = a.bitcast(F32)
                P.dma("sp", lambda e: e.dma_start(out=dbg_out[name], in_=a), reads=reads)

        load_head_w(0, [("xld", 9)])
        load_head_w(1, [("xld", 17)])
        if stop_after <= 3:
            P.barrier()
            dump("pp", pp, [])
            dump("dp", dp, [])
            dump("hxT", hxT, [])
            P.emit()
            return nc

        allhx = [("hxT", kc, tt) for kc in range(8) for tt in range(NT)]
        for tt in range(NT):
            b, off = (0, tt * 32) if tt < 16 else (1, (tt - 16) * 32)
            for kc in range(8):
                P.op("pe", lambda e, tt=tt, kc=kc, b=b, off=off: e.matmul(
                    PS(b, off, 32), lhsT=hxT[:, kc, tt * 128:(tt + 1) * 128], rhs=wg[:, kc, :],
                    start=(kc == 0), stop=(kc == 7)), reads=["wg", ("hxT", kc, tt)], writes=[("ps", b)])
        P.op("dve", lambda e: e.tensor_tensor(out=gts[:, 0:16, :], in0=PS(0, 0, 512).rearrange("p (a b) -> p a b", b=32),
                                              in1=bg_bc.unsqueeze(1).to_broadcast([128, 16, 32]), op=ALU.add),
             reads=[("ps", 0), "bg"], writes=[("gts", 0)])
        P.op("dve", lambda e: e.tensor_tensor(out=gts[:, 16:18, :], in0=PS(1, 0, 64).rearrange("p (a b) -> p a b", b=32),
                                              in1=bg_bc.unsqueeze(1).to_broadcast([128, 2, 32]), op=ALU.add),
             reads=[("ps", 1), "bg"], writes=[("gts", 1)])
        allg = [("gts", 0), ("gts", 1)]
        P.op("act", lambda e: e.activation(out=lf[:, :, 0:8], in_=gts[:, :, 8:16], func=AF.Exp, scale=-1.0),
             reads=allg, writes=["lf0"])
        P.op("act", lambda e: e.activation(out=lf[:, :, 8:16], in_=gts[:, :, 24:32], func=AF.Exp, scale=-1.0),
             reads=allg, writes=["lf1"])
        P.op("act", lambda e: e.activation(out=lf, in_=lf, func=AF.Ln, bias=1.0), reads=["lf0", "lf1"], writes=["lf"])
        lf2 = lf.rearrange("p a b -> p (a b)")
        for i, Mk in enumerate((M_le, M_gt, M_ge, M_lt, ones_f)):
            P.op("pe", lambda e, i=i, Mk=Mk: e.matmul(PS(2 + i, 0, 288), lhsT=Mk, rhs=lf2, start=True, stop=True),
                 reads=["lf", ("c", id(Mk)), "ones_f"], writes=[("ps", 2 + i)])

        def cs(i):
            return PS(2 + i, 0, 288).rearrange("p (a b) -> p a b", b=16)
        Pf, Sf, Pb, Sb, Tt = cs(0), cs(1), cs(2), cs(3), cs(4)
        for (half, Pm, Sm, pb_, sb_, ic) in ((0, Pf, Sf, 2, 3, 0), (1, Pb, Sb, 4, 5, 16)):
            sl = slice(half * 8, half * 8 + 8)
            P.op("dve", lambda e, sl=sl, Pm=Pm, ic=ic: e.tensor_tensor(out=tg1[:, :, sl], in0=Pm[:, :, sl],
                                                                       in1=gts[:, :, ic:ic + 8], op=ALU.add),
                 reads=[("ps", pb_)] + allg, writes=[("tg1", half)])
            P.op("act", lambda e, sl=sl: e.activation(out=ea[:, :, sl], in_=tg1[:, :, sl], func=AF.Exp),
                 reads=[("tg1", half)], writes=[("ea", half)])
            P.op("dve", lambda e, sl=sl, Sm=Sm, ic=ic: e.scalar_tensor_tensor(
                out=tg2[:, :, sl], in0=Sm[:, :, sl], scalar=-1.0, in1=gts[:, :, ic:ic + 8], op0=ALU.mult, op1=ALU.add),
                reads=[("ps", sb_)] + allg, writes=[("tg2", half)])
            P.op("act", lambda e, sl=sl: e.activation(out=wk[:, :, sl], in_=tg2[:, :, sl], func=AF.Exp),
                 reads=[("tg2", half)], writes=[("wk", half)])
            P.op("act", lambda e, sl=sl, Pm=Pm: e.activation(out=ern[:, :, sl], in_=Pm[:, :, sl], func=AF.Exp),
                 reads=[("ps", pb_)], writes=[("ern", half)])
        P.op("act", lambda e: e.activation(out=eB, in_=Tt, func=AF.Exp, scale=-1.0), reads=[("ps", 6)], writes=["eB"])
        if stop_after <= 4:
            P.barrier()
            dump("gts", gts, [])
            dump("ea", ea, [])
            dump("wk", wk, [])
            dump("ern", ern, [])
            dump("eB", eB, [])
            P.emit()
            return nc
        S4OUT = {"gts", "lf0", "lf1", "lf", "tg1", "tg2", "ea", "wk", "ern", "eB", "ps"}
        scratch = {"pst", "pst2", "wada", "modx", "modc"}
        late = {"xst", "xs", "sqj", "ssx", "lnx", "rsx", "xld"}
        allk = set(P.last_w) | set(P.readers)
        rd = [k for k in allk if (k if isinstance(k, str) else k[0]) not in (S4OUT | scratch | late | {"whd", "wg", "hxT"})]
        P.fence_all_later(P.op("pool", lambda e: e.memset(fsc[:, 0:1], 0.0), reads=rd, writes=P.keys_named(scratch) + ["S5"]))

        cur[0] = PH2
        qT, kT, gob, kBf, kBb, V1, Cstf, Cstb = [], [], [], [], [], [], [], []
        for _hb in range(2):
            qT.append(alloc([SEQ], BF16))
            kT.append(alloc([TOK], BF16))
            gob.append(alloc([SEQ], BF16))
            _kB = alloc([NT, 2, 128], BF16)
            kBf.append(_kB[:, :, 0, :])
            kBb.append(_kB[:, :, 1, :])
            kBB = (kBB if _hb else []) + [_kB]
            V1.append(alloc([NT, 130], BF16))
            if _hb == 0:
                S5_HB0_END = cur[0]
        _cf, _cb = alloc([NLT, 130], BF16), alloc([NLT, 130], BF16)
        Cstf, Cstb = [_cf, _cf], [_cb, _cb]
        Cf = [alloc([130], F32) for _ in range(2)]
        Cb = [alloc([130], F32) for _ in range(2)]
        Hh = alloc([NLT, 128], F32)
        hn = alloc([NLT, 128], BF16)
        ssh = alloc([NLT], F32)
        rsh = alloc([NLT], F32)
        mhalf = alloc([NLT], F32)
        t_o = [alloc([512], BF16) for _ in range(2)]
        t_z = [alloc([512], BF16) for _ in range(2)]
        hf = [alloc([128], F32) for _ in range(2)]
        sqh = alloc([128], BF16)
        Spf = [alloc([128], BF16) for _ in range(2)]
        Spb = [alloc([128], BF16) for _ in range(2)]
        Spr = [alloc([128], BF16) for _ in range(2)]
        dcl = [alloc([2], F32) for _ in range(2)]
        dab = [alloc([2], F32) for _ in range(2)]
        sfb = [alloc([2], F32) for _ in range(2)]
        vtmp = [alloc([512], BF16) for _ in range(2)]
        tmpg = [alloc([512], BF16) for _ in range(2)]
        wcv0 = alloc([8, 512], BF16)
        assert cur[0] <= ARENA, cur[0]
        S5_END = cur[0]

        P.op("pool", lambda e: e.memset(mhalf, -0.5), writes=["mhalf"])
        P.op("pool", lambda e: e.memset(V1[0], 1.0), writes=[("V1ones", 0)])
        fm_rot = [0]
        tm_rot = [0]

        def gen_proj(h):
            hb = h % 2
            W = whd[hb]
            wkey = lambda j: ("whd", hb, j)
            jmap = {"q": 0, "k": 1, "v": 2, "o": 3, "zb": 4}
            blocks = [(CTX + bi * 512, 512, bi) for bi in range(4)] + [(0, 256, 4)]
            for (tok0, ntk, bi) in blocks:
                fams = ("q", "k", "v", "o", "zb") if bi < 4 else ("k", "v")
                lsl = slice(bi * 512, (bi + 1) * 512)
                gsl = slice(tok0, tok0 + ntk)
                vt = vtmp[bi % 2]
                for fam in fams:
                    j = jmap[fam]
                    b = fm_rot[0] % 4
                    fm_rot[0] += 1
                    for kc in range(8):
                        P.op("pe", lambda e, ntk=ntk, gsl=gsl, lsl=lsl, bi=bi, vt=vt, b=b, j=j, kc=kc: e.matmul(
                            PS(b, 0, ntk), lhsT=W[:, kc, j * 128:(j + 1) * 128], rhs=hxT[:, kc, gsl],
                            start=(kc == 0), stop=(kc == 7)),
                            reads=[wkey(j)] + [("hxT", kc, t_) for t_ in range(tok0 // 128, (tok0 + ntk) // 128)], writes=[("ps", b)])
                    if fam == "q":
                        P.op("dve", lambda e, ntk=ntk, gsl=gsl, lsl=lsl, bi=bi, vt=vt, b=b: e.tensor_scalar(
                            out=qT[hb][:, lsl], in0=PS(b, 0, ntk), scalar1=pp[:, BQ + h:BQ + h + 1], scalar2=QS,
                            op0=ALU.add, op1=ALU.mult), reads=[("ps", b)], writes=[("qT", hb, bi)])
                    elif fam == "k":
                        P.op("act", lambda e, ntk=ntk, gsl=gsl, lsl=lsl, bi=bi, vt=vt, b=b: e.activation(
                            out=kT[hb][:, gsl], in_=PS(b, 0, ntk), func=AF.Identity, bias=pp[:, BK + h:BK + h + 1]),
                            reads=[("ps", b)], writes=[("kT", hb, bi)])
                    elif fam == "v":
                        P.op("dve", lambda e, ntk=ntk, gsl=gsl, lsl=lsl, bi=bi, vt=vt, b=b: e.tensor_scalar(
                            out=vt[:, 0:ntk], in0=PS(b, 0, ntk), scalar1=pp2[:, 32 + h:33 + h], scalar2=None, op0=ALU.add),
                            reads=[("ps", b)], writes=[("vtmp", bi % 2)])
                    elif fam == "o":
                        P.op("act", lambda e, ntk=ntk, gsl=gsl, lsl=lsl, bi=bi, vt=vt, b=b: e.activation(
                            out=t_o[bi % 2], in_=PS(b, 0, ntk), func=AF.Tanh, scale=0.5,
                            bias=dp[:, BOH + h:BOH + h + 1]), reads=[("ps", b)], writes=[("t_o", bi % 2)])
                    else:
                        P.op("act", lambda e, ntk=ntk, gsl=gsl, lsl=lsl, bi=bi, vt=vt, b=b: e.activation(
                            out=t_z[bi % 2], in_=PS(b, 0, ntk), func=AF.Silu, bias=pp[:, B_ZB + h:B_ZB + h + 1]),
                            reads=[("ps", b)], writes=[("t_z", bi % 2)])
                        P.op("pool", lambda e, ntk=ntk, gsl=gsl, lsl=lsl, bi=bi, vt=vt: e.tensor_tensor(out=tmpg[bi % 2], in0=t_o[bi % 2], in1=t_z[bi % 2], op=ALU.mult),
                             reads=[("t_o", bi % 2), ("t_z", bi % 2)], writes=[("tmpg", bi % 2)])
                        P.op("pool", lambda e, ntk=ntk, gsl=gsl, lsl=lsl, bi=bi, vt=vt: e.tensor_tensor(out=gob[hb][:, lsl], in0=tmpg[bi % 2], in1=t_z[bi % 2], op=ALU.add),
                             reads=[("tmpg", bi % 2), ("t_z", bi % 2)], writes=[("gob", hb, bi)])
                    yield
                for ti in range(ntk // 128):
                    tt = (tok0 // 128 + ti)
                    b = fm_rot[0] % 4
                    fm_rot[0] += 1
                    P.op("pe", lambda e, ntk=ntk, gsl=gsl, lsl=lsl, bi=bi, vt=vt, b=b, tt=tt: e.transpose(out=PSB(b, 0, 128), in_=kT[hb][:, tt * 128:(tt + 1) * 128],
                                                                 identity=ident_bf),
                         reads=[("kT", hb, bi), "ident_bf"], writes=[("ps", b)])
                    P.op("pe", lambda e, ntk=ntk, gsl=gsl, lsl=lsl, bi=bi, vt=vt, b=b, ti=ti: e.transpose(out=PSB(b, 128, 128), in_=vt[:, ti * 128:(ti + 1) * 128],
                                                                 identity=ident_bf),
                         reads=[("vtmp", bi % 2), "ident_bf"], writes=[("ps", b)])
                    P.op("dve", lambda e, ntk=ntk, gsl=gsl, lsl=lsl, bi=bi, vt=vt, tt=tt, b=b: e.tensor_tensor(
                        out=kBB[hb][:, tt, :, :], in0=PSB(b, 0, 128).unsqueeze(1).to_broadcast([128, 2, 128]),
                        in1=wk[:, tt, h::8].unsqueeze(2).to_broadcast([128, 2, 128]), op=ALU.mult),
                        reads=[("ps", b), ("wk", 0), ("wk", 1)], writes=[("kBf", hb, tt), ("kBb", hb, tt)])
                    P.op("act", lambda e, ntk=ntk, gsl=gsl, lsl=lsl, bi=bi, vt=vt, tt=tt, b=b: e.activation(
                        out=V1[hb][:, tt, 0:128], in_=PSB(b, 128, 128), func=AF.Copy),
                        reads=[("ps", b), ("V1ones", hb)], writes=[("V1", hb, tt)])
                    if ti % 2 == 1:
                        yield

        def gen_scan(h):
            hb = h % 2
            P.op("pool", lambda e: e.memset(Cf[0], 0.0), writes=[("Cf", 0)])
            P.op("pool", lambda e: e.memset(Cb[0], 0.0), writes=[("Cb", 0)])
            f_order = list(range(0, 17))
            b_order = [1, 0] + list(range(17, 2, -1))
            for step in range(17):
                for (dirn, order, kB, Cs, Cst, col0, bank) in (("f", f_order, kBf, Cf, Cstf, 0, 4), ("b", b_order, kBb, Cb, Cstb, 8, 5)):
                    tt = order[step]
                    src, dst = Cs[step % 2], Cs[(step + 1) % 2]
                    ck = "C" + dirn
                    P.op("pe", lambda e, tt=tt, kB=kB, bank=bank: e.matmul(
                        PS(bank, 0, 129), lhsT=kB[hb][:, tt, :], rhs=V1[hb][:, tt, 0:129], start=True, stop=True),
                        reads=[("kB" + dirn, hb, tt), ("V1", hb, tt)], writes=[("ps", bank)])
                    P.op("dve", lambda e, tt=tt, src=src, dst=dst, bank=bank, col0=col0: e.scalar_tensor_tensor(
                        out=dst[:, 0:129], in0=src[:, 0:129], scalar=eB[:, tt, col0 + h:col0 + h + 1],
                        in1=PS(bank, 0, 129), op0=ALU.mult, op1=ALU.add),
                        reads=[(ck, step % 2), ("ps", bank), "eB"], writes=[(ck, (step + 1) % 2)])
                    if dirn == "f":
                        nxt = tt + 1
                    else:
                        nxt = 17 if step == 1 else (tt - 1 if step >= 2 else None)
                    if nxt is not None and nxt >= 2:
                        P.op("pool", lambda e, dst=dst, Cst=Cst, nxt=nxt: e.tensor_copy(out=Cst[hb][:, nxt - 2, 0:129], in_=dst[:, 0:129]),
                             reads=[(ck, (step + 1) % 2)], writes=[("Cst" + dirn, 0, nxt)])
                yield

            def emit_S(lt):
                tt = lt + 2
                s2i = lt % 2
                tsl = slice(lt * 128, (lt + 1) * 128)
                for bank in (4,):
                    P.op("pe", lambda e, tsl=tsl, bank=bank, tt=tt: e.matmul(PS(bank, 0, 128), lhsT=kT[hb][:, tt * 128:(tt + 1) * 128],
                                                                             rhs=qT[hb][:, tsl], start=True, stop=True),
                         reads=[("kT", hb, lt // 4), ("qT", hb, lt // 4)], writes=[("ps", bank)])
                P.op("dve", lambda e, tt=tt, s2i=s2i: e.scalar_tensor_tensor(
                    out=Spf[s2i], in0=PS(4, 0, 128), scalar=ea[:, tt, h:h + 1], in1=M_le, op0=ALU.mult, op1=ALU.mult),
                    reads=[("ps", 4), ("ea", 0)], writes=[("Spf", s2i)])
                P.op("act", lambda e, tt=tt, s2i=s2i: e.activation(out=Spr[s2i], in_=PS(4, 0, 128), func=AF.Copy,
                                                                   scale=ea[:, tt, 8 + h:9 + h]),
                     reads=[("ps", 4), ("ea", 1)], writes=[("Spr", s2i)])
                P.op("pool", lambda e, s2i=s2i: e.tensor_tensor(out=Spb[s2i], in0=Spr[s2i], in1=M_ge_bf, op=ALU.mult),
                     reads=[("Spr", s2i)], writes=[("Spb", s2i)])

            def emit_num(lt):
                tt = lt + 2
                s2i = lt % 2
                tsl = slice(lt * 128, (lt + 1) * 128)
                nb = 6 + s2i
                NUM = PS(nb, 0, 260).rearrange("p (a b) -> p a b", a=2)
                for di, (Sp, Cst, dn) in enumerate(((Spf, Cstf, "f"), (Spb, Cstb, "b"))):
                    P.op("pe", lambda e, Sp=Sp, di=di, NUM=NUM: e.matmul(
                        NUM[:, di, 0:129], lhsT=Sp[s2i], rhs=V1[hb][:, tt, 0:129], start=True, stop=False),
                        reads=[("Sp" + dn, s2i), ("V1", hb, tt)], writes=[("ps", nb)])
                    P.op("pe", lambda e, Cst=Cst, di=di, NUM=NUM: e.matmul(
                        NUM[:, di, 0:129], lhsT=qT[hb][:, tsl], rhs=Cst[hb][:, lt, 0:129], start=False, stop=True),
                        reads=[("qT", hb, lt // 4), ("Cst" + dn, 0, tt)], writes=[("ps", nb)])
                numk = [("ps", nb)]
                P.op("act", lambda e, NUM=NUM: e.activation(out=dab[s2i], in_=NUM[:, :, 128], func=AF.Abs),
                     reads=numk, writes=[("dab", s2i)])
                P.op("dve", lambda e: e.tensor_tensor(out=dcl[s2i], in0=dab[s2i], in1=ern[:, tt, h::8], op=ALU.max),
                     reads=[("dab", s2i), ("ern", 0), ("ern", 1)], writes=[("dcl", s2i)])
                P.op("dve", lambda e: e.reciprocal(out=sfb[s2i], in_=dcl[s2i]), reads=[("dcl", s2i)], writes=[("sfb", s2i)])
                P.op("act", lambda e, NUM=NUM: e.activation(out=hf[s2i], in_=NUM[:, 0, 0:128], func=AF.Copy, scale=sfb[s2i][:, 0:1]),
                     reads=numk + [("sfb", s2i)], writes=[("hf", s2i)])
                P.op("dve", lambda e, NUM=NUM: e.scalar_tensor_tensor(
                    out=Hh[:, lt, :], in0=NUM[:, 1, 0:128], scalar=sfb[s2i][:, 1:2], in1=hf[s2i], op0=ALU.mult, op1=ALU.add),
                    reads=numk + [("sfb", s2i), ("hf", s2i)], writes=[("Hh", lt)])
                P.op("act", lambda e: e.activation(out=sqh, in_=Hh[:, lt, :], func=AF.Square, accum_out=ssh[:, lt:lt + 1]),
                     reads=[("Hh", lt)], writes=["sqh", ("ssh", lt)])

            emit_S(0)
            yield
            for lt in range(NLT):
                if lt + 1 < NLT:
                    emit_S(lt + 1)
                emit_num(lt)
                yield
            allss = [("ssh", lt) for lt in range(NLT)]
            P.op("dve", lambda e: e.tensor_scalar(out=ssh, in0=ssh, scalar1=1.0 / 128, scalar2=EPS, op0=ALU.mult, op1=ALU.add),
                 reads=allss, writes=allss)
            P.op("pool", lambda e: e.tensor_tensor(out=rsh, in0=ssh, in1=mhalf, op=ALU.pow), reads=allss + ["mhalf"], writes=["rsh"])
            for blk in range(4):
                for q4 in range(4):
                    lt = blk * 4 + q4
                    if lt % 2 == 0:
                        P.op("act", lambda e, lt=lt: e.activation(out=hn[:, lt, :], in_=Hh[:, lt, :], func=AF.Copy,
                                                                  scale=rsh[:, lt:lt + 1]),
                             reads=[("Hh", lt), "rsh"], writes=[("hn", lt)])
                    else:
                        P.op("pool", lambda e, lt=lt: e.tensor_scalar(out=hn[:, lt, :], in0=Hh[:, lt, :], scalar1=rsh[:, lt:lt + 1],
                                                                      scalar2=0.0, op0=ALU.mult, op1=ALU.add),
                             reads=[("Hh", lt), "rsh"], writes=[("hn", lt)])
                pslot = 6 + blk % 2
                for q4 in range(4):
                    lt = blk * 4 + q4
                    P.op("pe", lambda e, lt=lt, q4=q4, pslot=pslot: e.transpose(
                        out=PSB(pslot, q4 * 128, 128), in_=hn[:, lt, :], identity=ident_bf),
                        reads=[("hn", lt), "ident_bf"], writes=[("ps", pslot)])
                P.op("dve", lambda e, blk=blk, pslot=pslot: e.scalar_tensor_tensor(
                    out=ybT[:, h, blk * 512:(blk + 1) * 512], in0=PSB(pslot, 0, 512), scalar=dp[:, GHH + h:GHH + h + 1],
                    in1=gob[hb][:, blk * 512:(blk + 1) * 512], op0=ALU.mult, op1=ALU.mult),
                    reads=[("ps", pslot), ("gob", hb, blk)], writes=[("ybT", h, blk)])
                yield
            if h == 0:
                dump("Hh0", Hh, [("Hh", lt) for lt in range(NLT)])

        def drive(gens, weights=None):
            pairs = [(g, (weights[i] if weights else 1)) for i, g in enumerate(gens) if g is not None]
            while pairs:
                for (g, wgt) in list(pairs):
                    for _ in range(wgt):
                        try:
                            next(g)
                        except StopIteration:
                            pairs.remove((g, wgt))
                            break

        OFF_BA, OFF_CA, OFF_XA, OFF_ZA, OFF_GA, OFF_GB = [3104 + 1024 * i for i in range(2, 8)]

        def load_cv_w(c):
            for j, o0 in enumerate((OFF_CA, OFF_XA, OFF_ZA, OFF_BA)):
                P.dma("pool", lambda e, c=c, j=j, o0=o0: e.dma_start(
                    out=wcv[c % 2][:, :, j * 128:(j + 1) * 128], in_=wv_in[:, :, o0 + c * 128:o0 + (c + 1) * 128]),
                    writes=[("wcv", c % 2, j)])

        wcv = [wcv0, None]
        assert PRO_LATE >= S5_HB0_END, (PRO_LATE, S5_HB0_END)
        drive([gen_proj(0)])
        P.fence_all_later(P.op("pool", lambda e: e.memset(fsc[:, 4:5], 0.0), writes=P.keys_named(late) + ["S5b"]))
        P.op("pool", lambda e: e.memset(V1[1], 1.0), writes=[("V1ones", 1)])
        for h in range(NH):
            if h == NH - 2:
                load_cv_w(0)
            nxt = None
            if h + 1 < NH:
                nxt = gen_proj(h + 1)
            if h + 2 < NH:
                load_head_w(h + 2)
            if h == NH - 1 and stop_after > 5:
                break
            drive([nxt, gen_scan(h)])
        if stop_after <= 5:
            P.barrier()
            dump("ybT", ybT, [])
            P.emit()
            return nc

        cur[0] = PH
        yaT = alloc([8, SEQ], BF16)
        assert cur[0] == PH + 32768
        wcv = [wcv0, alloc([8, 512], BF16)]
        tca = [alloc([512], F32) for _ in range(1)]
        tu = [alloc([512], F32) for _ in range(1)]
        assert cur[0] <= S5_HB0_END, (cur[0], S5_HB0_END)
        cur[0] = S5_END
        tcv = [alloc([512], F32) for _ in range(1)]
        tsz = [alloc([512], BF16) for _ in range(1)]
        tba = [alloc([512], BF16) for _ in range(1)]
        assert cur[0] <= ARENA, cur[0]
        old_keys = [("whd", hb_, j) for hb_ in range(2) for j in range(5)]
        old_keys += [(nm, 0, i) for nm in ("qT", "kT", "gob") for i in range(4)]
        old_keys += [(nm, 0, tt) for nm in ("kBf", "kBb", "V1") for tt in range(NT)]
        old_keys += [("V1ones", 0)]
        new_keys = [("yaT", c, blk) for c in range(8) for blk in range(4)] + [("wcv", 1, j) for j in range(4)]
        new_keys += [("tca", 0), ("tu", 0), ("tcv", 0), ("tsz", 0), ("tba", 0)]
        P.op("pool", lambda e: e.memset(fsc[:, 1:2], 0.0), writes=old_keys + new_keys + ["fsc1"])
        rot = [0]

        def gen_conv():
          for c in range(8):
            if c + 1 < 8:
                load_cv_w(c + 1)
            W = wcv[c % 2]
            for blk in range(4):
                tok0 = CTX + blk * 512
                i2 = 0
                banks = []
                for j in range(4):
                    b = rot[0] % 4
                    rot[0] += 1
                    banks.append(b)
                    for kc in range(8):
                        P.op("pe", lambda e, b=b, j=j, kc=kc, tok0=tok0, W=W: e.matmul(
                            PS(b, 0, 512), lhsT=W[:, kc, j * 128:(j + 1) * 128], rhs=hxT[:, kc, tok0:tok0 + 512],
                            start=(kc == 0), stop=(kc == 7)),
                            reads=[("wcv", c % 2, j)] + [("hxT", kc, t_) for t_ in range(tok0 // 128, tok0 // 128 + 4)], writes=[("ps", b)])
                bca, bxa, bza, bba = banks
                P.op("act", lambda e, c=c, i2=i2, bca=bca: e.activation(out=tca[i2], in_=PS(bca, 0, 512), func=AF.Identity,
                                                                        bias=pp[:, B_CA + c:B_CA + c + 1]),
                     reads=[("ps", bca)], writes=[("tca", i2)])
                P.op("dve", lambda e, c=c, i2=i2, bxa=bxa: e.scalar_tensor_tensor(
                    out=tu[i2], in0=PS(bxa, 0, 512), scalar=pp[:, B_XA + c:B_XA + c + 1], in1=tca[i2], op0=ALU.add, op1=ALU.mult),
                    reads=[("ps", bxa), ("tca", i2)], writes=[("tu", i2)])
                P.op("act", lambda e, c=c, i2=i2: e.activation(out=tcv[i2], in_=tu[i2], func=AF.Identity,
                                                               scale=pp[:, WCV + 8 + c:WCV + 9 + c],
                                                               bias=pp[:, BCV + c:BCV + c + 1]),
                     reads=[("tu", i2)], writes=[("tcv", i2)])
                u3 = tu[i2].rearrange("p (r w) -> p r w", w=64)
                c3 = tcv[i2].rearrange("p (r w) -> p r w", w=64)
                P.op("dve", lambda e, c=c, u3=u3, c3=c3: e.scalar_tensor_tensor(
                    out=c3[:, :, 1:64], in0=u3[:, :, 0:63], scalar=pp[:, WCV + c:WCV + c + 1], in1=c3[:, :, 1:64],
                    op0=ALU.mult, op1=ALU.add), reads=[("tu", i2), ("tcv", i2)], writes=[("tcv", i2)])
                P.op("dve", lambda e, c=c, u3=u3, c3=c3: e.scalar_tensor_tensor(
                    out=c3[:, :, 0:63], in0=u3[:, :, 1:64], scalar=pp[:, WCV + 16 + c:WCV + 17 + c], in1=c3[:, :, 0:63],
                    op0=ALU.mult, op1=ALU.add), reads=[("tu", i2), ("tcv", i2)], writes=[("tcv", i2)])
                P.op("act", lambda e, c=c, i2=i2, bza=bza: e.activation(out=tsz[i2], in_=PS(bza, 0, 512), func=AF.Silu,
                                                                        bias=pp[:, B_ZA + c:B_ZA + c + 1]),
                     reads=[("ps", bza)], writes=[("tsz", i2)])
                P.op("dve", lambda e, c=c, i2=i2, bba=bba: e.scalar_tensor_tensor(
                    out=tba[i2], in0=PS(bba, 0, 512), scalar=pp[:, B_BA + c:B_BA + c + 1], in1=tsz[i2], op0=ALU.add, op1=ALU.mult),
                    reads=[("ps", bba), ("tsz", i2)], writes=[("tba", i2)])
                P.op("pool", lambda e, c=c, i2=i2, blk=blk: e.tensor_tensor(
                    out=yaT[:, c, blk * 512:(blk + 1) * 512], in0=tba[i2], in1=tcv[i2], op=ALU.mult),
                    reads=[("tba", i2), ("tcv", i2)], writes=[("yaT", c, blk)])
                yield

        drive([gen_scan(NH - 1), gen_conv()], weights=[3, 1])
        if stop_after <= 6:
            P.barrier()
            dump("yaT", yaT, [])
            P.emit()
            return nc

        cur[0] = PH + 32768
        mgT = alloc([8, SEQ], BF16)
        wmg = [alloc([8, 512], BF16) for _ in range(2)]
        tga = [alloc([512], F32) for _ in range(2)]
        tgb = [alloc([512], F32) for _ in range(2)]
        tmA, tmB = tga, tgb
        assert cur[0] <= PH + 90112, cur[0]
        cur[0] = PH + 90112
        wo = alloc([8, 1024], BF16)
        wada2 = alloc([8, 1024], BF16)
        assert cur[0] <= ARENA, cur[0]
        S5N = {"whd", "qT", "kT", "gob", "kBf", "kBb", "V1", "V1ones", "Cstf", "Cstb", "Cf", "Cb", "Hh", "hn", "ssh", "rsh",
               "mhalf", "t_o", "t_z", "hf", "sqh", "Spf", "Spb", "Spr", "dcl", "dab", "sfb", "vtmp", "tmpg"}
        S6N = {"wcv", "tca", "tu", "tcv", "tsz", "tba"}
        P.op("pool", lambda e: e.memset(fsc[:, 2:3], 0.0),
             writes=P.keys_named(S5N) + [("wmg", i, j) for i in range(2) for j in range(4)] + ["fsc2"])
        P.op("pool", lambda e: e.memset(fsc[:, 3:4], 0.0),
             writes=P.keys_named(S5N | S6N) + [("mgT", m, blk) for m in range(8) for blk in range(4)]
             + [(nm, i) for nm in ("tga", "tgb") for i in range(2)] + ["wo", "wada2", "fsc3"])
        wv_out = w_out.rearrange("(kc p) n -> p kc n", p=128)
        wv_pa = w_pa.rearrange("(kc p) n -> p kc n", p=128)
        wv_pb = w_pb.rearrange("(kc p) n -> p kc n", p=128)

        def load_mg_w(m):
            srcs = (wv_pa[:, :, m * 128:(m + 1) * 128], wv_pb[:, :, m * 128:(m + 1) * 128],
                    wv_in[:, :, OFF_GA + m * 128:OFF_GA + (m + 1) * 128], wv_in[:, :, OFF_GB + m * 128:OFF_GB + (m + 1) * 128])
            for j, s in enumerate(srcs):
                P.dma("pool", lambda e, m=m, j=j, s=s: e.dma_start(out=wmg[m % 2][:, :, j * 128:(j + 1) * 128], in_=s),
                      writes=[("wmg", m % 2, j)])

        load_mg_w(0)
        load_mg_w(1)
        P.dma("pool", lambda e: e.dma_start(out=wo, in_=wv_out), writes=["wo"])
        P.dma("pool", lambda e: e.dma_start(out=wada2, in_=wv_ada[:, :, 2048:3072]), writes=["wada2"])
        rot = [0]
        for m in range(8):
            if 1 <= m and m + 1 < 8:
                load_mg_w(m + 1)
            W = wmg[m % 2]
            for blk in range(4):
                i2 = blk % 2
                tsl = slice(blk * 512, (blk + 1) * 512)
                tok0 = CTX + blk * 512
                banks = []
                for j in range(4):
                    b = rot[0] % 8
                    rot[0] += 1
                    banks.append(b)
                    for kc in range(8):
                        if j == 0:
                            rhs = yaT[:, kc, tsl]
                            rk = [("yaT", kc, blk)]
                        elif j == 1:
                            rhs = ybT[:, kc, tsl]
                            rk = [("ybT", kc, blk)]
                        else:
                            rhs = hxT[:, kc, tok0:tok0 + 512]
                            rk = [("hxT", kc, t_) for t_ in range(tok0 // 128, tok0 // 128 + 4)]
                        P.op("pe", lambda e, b=b, j=j, kc=kc, rhs=rhs, W=W: e.matmul(
                            PS(b, 0, 512), lhsT=W[:, kc, j * 128:(j + 1) * 128], rhs=rhs, start=(kc == 0), stop=(kc == 7)),
                            reads=[("wmg", m % 2, j)] + rk, writes=[("ps", b)])
                bpa, bpb, bga, bgb = banks
                P.op("act", lambda e, m=m, i2=i2, bga=bga: e.activation(out=tga[i2], in_=PS(bga, 0, 512), func=AF.Tanh, scale=0.5,
                                                                        bias=dp[:, BGAH + m:BGAH + m + 1]),
                     reads=[("ps", bga)], writes=[("tga", i2)])
                P.op("act", lambda e, m=m, i2=i2, bgb=bgb: e.activation(out=tgb[i2], in_=PS(bgb, 0, 512), func=AF.Tanh, scale=0.5,
                                                                        bias=dp[:, BGBH + m:BGBH + m + 1]),
                     reads=[("ps", bgb)], writes=[("tgb", i2)])
                P.op("dve", lambda e, i2=i2, bpa=bpa: e.scalar_tensor_tensor(
                    out=tmA[i2], in0=tga[i2], scalar=1.0, in1=PS(bpa, 0, 512), op0=ALU.add, op1=ALU.mult),
                    reads=[("tga", i2), ("ps", bpa)], writes=[("tga", i2)])
                P.op("dve", lambda e, i2=i2, bpb=bpb: e.scalar_tensor_tensor(
                    out=tmB[i2], in0=tgb[i2], scalar=1.0, in1=PS(bpb, 0, 512), op0=ALU.add, op1=ALU.mult),
                    reads=[("tgb", i2), ("ps", bpb)], writes=[("tgb", i2)])
                P.op("pool", lambda e, m=m, i2=i2, tsl=tsl: e.tensor_tensor(out=mgT[:, m, tsl], in0=tmA[i2], in1=tmB[i2], op=ALU.add),
                     reads=[("tga", i2), ("tgb", i2)], writes=[("mgT", m, blk)])
        P.barrier()
        dump("mgT", mgT, [])
        if stop_after <= 7:
            P.emit()
            return nc

        cur[0] = PH
        bada_g = alloc([D], F32)
        gate_bc = alloc([D], F32)
        gfin_bc = alloc([D], F32)
        sbc = alloc([8, 128], BF16)
        ones_bf = alloc([128], BF16)
        bo_row = alloc([D], BF16)
        ssf = alloc([16], F32)
        lsf = alloc([16], F32)
        rsf = alloc([16], F32)
        sqf = alloc([D], BF16)
        NB8 = 4
        xt = [alloc([D], F32) for _ in range(1)]
        xn = [alloc([D], F32) for _ in range(1)]
        assert cur[0] <= PH + 32768, cur[0]
        cur[0] = PH + 65536
        xt += [alloc([D], F32) for _ in range(3)]
        xn += [alloc([D], F32) for _ in range(3)]
        assert cur[0] <= PH + 90112, cur[0]
        P.dma("pool", lambda e: e.dma_start(out=bo_row[0:1, :], in_=b_out.rearrange("(o n) -> o n", o=1)), writes=["bo_row"])
        for (dst, src, nm) in ((bada_g, b_ada[2048:3072], "bada_g"), (gfin_bc, g_final, "gfin")):
            P.dma("sp", lambda e, dst=dst, src=src: e.dma_start(out=dst, in_=src.partition_broadcast(128)), writes=[nm])
        P.op("pool", lambda e: e.memset(ones_bf, 1.0), writes=["ones_bf"])
        P.op("pool", lambda e: e.tensor_scalar(out=bada_g, in0=bada_g, scalar1=0.5, scalar2=0.0, op0=ALU.mult, op1=ALU.add),
             reads=["bada_g"], writes=["bada_g"])
        P.op("pool", lambda e: e.tensor_scalar(out=bo_row[0:1, :], in0=bo_row[0:1, :], scalar1=2.0, scalar2=0.0, op0=ALU.mult, op1=ALU.add),
             reads=["bo_row"], writes=["bo_row"])
        for kc in range(8):
            P.op("dve", lambda e, kc=kc: e.tensor_scalar(out=sbc[:, kc, :], in0=ones_bf, scalar1=s_f[:, kc:kc + 1],
                                                         scalar2=None, op0=ALU.mult),
                 reads=["ones_bf"], writes=[("sbc", kc)])
        for nb in range(2):
            for kc in range(8):
                P.op("pe", lambda e, nb=nb, kc=kc: e.matmul(PS(6 + nb, 0, 512), lhsT=sbc[:, kc, :],
                                                            rhs=wada2[:, kc, nb * 512:(nb + 1) * 512],
                                                            start=(kc == 0), stop=(kc == 7)),
                     reads=[("sbc", kc), "wada2"], writes=[("ps", 6 + nb)])
            P.op("dve", lambda e, nb=nb: e.scalar_tensor_tensor(out=gate_bc[:, nb * 512:(nb + 1) * 512], in0=PS(6 + nb, 0, 512),
                                                                scalar=0.5, in1=bada_g[:, nb * 512:(nb + 1) * 512],
                                                                op0=ALU.mult, op1=ALU.add),
                 reads=[("ps", 6 + nb), "bada_g"], writes=[("gate_bc", nb)])
        gk = [("gate_bc", 0), ("gate_bc", 1)]
        for lt in range(NLT):
            i2 = lt % NB8
            tsl = slice(lt * 128, (lt + 1) * 128)
            P.dma("sp", lambda e, lt=lt, i2=i2: e.dma_start(out=xt[i2], in_=x[lt * 128:(lt + 1) * 128, :]), writes=[("xt", i2)])
            bank0 = (lt % 3) * 2
            for nb in range(2):
                for m in range(8):
                    P.op("pe", lambda e, nb=nb, m=m, tsl=tsl, bank0=bank0: e.matmul(
                        PS(bank0 + nb, 0, 512), lhsT=mgT[:, m, tsl], rhs=wo[:, m, nb * 512:(nb + 1) * 512],
                        start=(m == 0), stop=False), reads=["wo"], writes=[("ps", bank0 + nb)])
                P.op("pe", lambda e, nb=nb, bank0=bank0: e.matmul(
                    PS(bank0 + nb, 0, 512), lhsT=ones_bf[0:1, :], rhs=bo_row[0:1, nb * 512:(nb + 1) * 512],
                    start=False, stop=True), reads=["ones_bf", "bo_row"], writes=[("ps", bank0 + nb)])
                P.op("dve", lambda e, nb=nb, i2=i2, bank0=bank0: e.tensor_tensor(
                    out=xn[i2][:, nb * 512:(nb + 1) * 512], in0=PS(bank0 + nb, 0, 512), in1=gate_bc[:, nb * 512:(nb + 1) * 512],
                    op=ALU.mult), reads=[("ps", bank0 + nb)] + gk, writes=[("xn", i2, nb)])
            xk = [("xn", i2, 0), ("xn", i2, 1)]
            P.op("pool", lambda e, i2=i2: e.tensor_tensor(out=xn[i2], in0=xn[i2], in1=xt[i2], op=ALU.add),
                 reads=xk + [("xt", i2)], writes=xk)
            P.op("act", lambda e, i2=i2, lt=lt: e.activation(out=sqf, in_=xn[i2], func=AF.Square, accum_out=ssf[:, lt:lt + 1]),
                 reads=xk, writes=["sqf", ("ssf", lt)])
            P.op("act", lambda e, lt=lt: e.activation(out=lsf[:, lt:lt + 1], in_=ssf[:, lt:lt + 1], func=AF.Ln, scale=1.0 / D, bias=EPS),
                 reads=[("ssf", lt)], writes=[("lsf", lt)])
            P.op("act", lambda e, lt=lt: e.activation(out=rsf[:, lt:lt + 1], in_=lsf[:, lt:lt + 1], func=AF.Exp, scale=-0.5),
                 reads=[("lsf", lt)], writes=[("rsf", lt)])
            P.op("dve", lambda e, i2=i2, lt=lt: e.scalar_tensor_tensor(
                out=xn[i2], in0=xn[i2], scalar=rsf[:, lt:lt + 1], in1=gfin_bc, op0=ALU.mult, op1=ALU.mult),
                reads=xk + [("rsf", lt), "gfin"], writes=xk)
            P.dma("sp", lambda e, i2=i2, lt=lt: e.dma_start(out=y[lt * 128:(lt + 1) * 128, :], in_=xn[i2]), reads=xk)
        P.emit()
    return nc


_NC_CACHE = {}


def _core_inputs(b, x, c, ctx, c_ctx, w_ada, b_ada, g_norm, w_in, b_in, w_conv, b_conv, g_head, w_pa, w_pb, w_out,
                 b_out, g_final):
    f = lambda a: np.ascontiguousarray(a, dtype=np.float32)
    return {
        "x": f(x[b]), "ctx": f(ctx[b]), "c": f(c[b]), "c_ctx": f(c_ctx),
        "w_ada": f(w_ada[0]), "b_ada": f(b_ada[0]), "g_norm": f(g_norm[0]), "w_in": f(w_in[0]), "b_in": f(b_in[0]),
        "w_conv": f(w_conv[0]), "b_conv": f(b_conv[0]), "g_head": f(g_head[0]), "w_pa": f(w_pa[0]), "w_pb": f(w_pb[0]),
        "w_out": f(w_out[0]), "b_out": f(b_out[0]), "g_final": f(g_final),
    }


def kernel(**inputs):
    if "nc" not in _NC_CACHE:
        _NC_CACHE["nc"] = build_program()
    nc = _NC_CACHE["nc"]
    in_maps = [_core_inputs(b, **inputs) for b in range(8)]
    res = run_bass_kernel_spmd(nc, in_maps, core_ids=list(range(8)))
    return np.stack([np.asarray(r["y"], dtype=np.float32).reshape(SEQ, D) for r in res.results], axis=0)
```

```python
import contextlib
import numpy as np
import concourse.bass as bass
import concourse.mybir as mybir
from concourse.bass_utils import run_bass_kernel_spmd

F32 = mybir.dt.float32
BF16 = mybir.dt.bfloat16
AF = mybir.ActivationFunctionType
ALU = mybir.AluOpType

D = 1024
SEQ = 2048
CTX = 256
NT = 18
NLT = 16
TOK = NT * 128
NH = 8
N_IN = 11296
EPS = 1e-6
QS = 128 ** -0.5


class _Probe:
    def __init__(self):
        self.rec = None

    def __getattr__(self, name):
        def f(*a, **k):
            self.rec = (name, a, k)
            return None
        return f


def _free(ap):
    n = 1
    for v in ap.shape[1:]:
        n *= v
    return n


def _cost(eng, fn, dma):
    pr = _Probe()
    fn(pr)
    name, a, k = pr.rec
    if dma:
        out = k.get("out", a[0] if a else None)
        nbytes = _free(out) * out.shape[0] * (2 if out.dtype == BF16 else 4)
        issue = 1100.0 if eng == "pool" else 100.0
        return issue, issue + 2000.0 + nbytes / 250.0
    if eng == "pe":
        if name == "transpose":
            t = 128 / 2.4 + 3
        else:
            rhs = k.get("rhs", a[2] if len(a) > 2 else None)
            n = _free(rhs)
            t = max(n, 56) / 2.4 + 3
            if rhs.dtype == F32:
                t *= 4
        return t, t + 170.0
    out = k.get("out", a[0] if a else None)
    n = _free(out) if out is not None else 128
    if eng == "act":
        t = (170.0 + n * 1.8) if n <= 128 else (300.0 + n * 0.74)
        if k.get("accum_out") is not None:
            t += 190
    elif eng == "dve":
        t = 200.0 + n * 1.05
    else:
        if name == "tensor_tensor":
            t = 180.0 + n * 2.0
            if k.get("op") == ALU.pow:
                t += 2700
        elif name == "tensor_copy":
            t = 340.0 + n * 2.0
        else:
            t = 250.0 + n * 1.0
    return t, t + 60.0


class _Op:
    __slots__ = ("eng", "fn", "deps", "dma", "idx", "pos", "phase", "busy", "lat", "stt", "vc", "waits")


class Prog:
    ENGS = ("pe", "act", "dve", "pool", "sp")
    KDMA = 16
    SYNC = 120.0

    def __init__(self, nc, reorder=True):
        self.nc = nc
        self.ops = []
        self.last_w = {}
        self.readers = {}
        self.phase = 0
        self.reorder = reorder
        self.after = set()

    def _add(self, eng, fn, reads, writes, dma):
        norm = lambda k: k[:2] if (isinstance(k, tuple) and k[0] == "ps") else k
        reads = tuple(dict.fromkeys(norm(k) for k in reads))
        writes = tuple(dict.fromkeys(norm(k) for k in writes))
        writes = writes + tuple(k for k in reads if isinstance(k, tuple) and k[0] == "ps" and k not in writes)
        op = _Op()
        op.eng, op.fn, op.dma, op.idx, op.phase, op.pos = eng, fn, dma, len(self.ops), self.phase, None
        op.busy, op.lat = _cost(eng, fn, dma)
        deps = set()
        for k in reads:
            if k in self.last_w:
                deps.add(self.last_w[k])
        for k in writes:
            if k in self.last_w:
                deps.add(self.last_w[k])
            for r in self.readers.get(k, ()):
                deps.add(r)
        deps |= self.after
        deps.discard(op)
        op.deps = deps
        for k in reads:
            self.readers.setdefault(k, []).append(op)
        for k in writes:
            self.last_w[k] = op
            self.readers[k] = []
        self.ops.append(op)
        return op

    def op(self, eng, fn, reads=(), writes=()):
        return self._add(eng, fn, tuple(reads), tuple(writes), False)

    def dma(self, eng, fn, reads=(), writes=()):
        return self._add(eng, fn, tuple(reads), tuple(writes), True)

    def keys_named(self, names):
        ks = set(self.last_w) | set(self.readers)
        return [k for k in ks if (k if isinstance(k, str) else k[0]) in names]

    def fence_all_later(self, op):
        self.after.add(op)

    def barrier(self):
        self.after = set()
        self.phase += 1
        self.last_w = {}
        self.readers = {}

    def _schedule(self, ops):
        import heapq
        order = {e: [] for e in self.ENGS}
        if not self.reorder:
            for o in ops:
                o.stt = float(o.idx)
                order[o.eng].append(o)
            return order
        inphase = set(ops)
        succs = {o: [] for o in ops}
        indeg = {}
        for o in ops:
            ds = [d for d in o.deps if d in inphase]
            indeg[o] = len(ds)
            for d in ds:
                succs[d].append(o)
        tail = {}
        for o in reversed(ops):
            t_ = 0.0
            for su in succs[o]:
                t_ = max(t_, tail[su] + (30.0 if su.eng == o.eng else self.SYNC))
            tail[o] = o.lat + t_
        pk = lambda o: (-tail[o], o.idx)
        ready = {o: 0.0 for o in ops}
        fut = {e: [] for e in self.ENGS}
        now = {e: [] for e in self.ENGS}
        for o in ops:
            if indeg[o] == 0:
                heapq.heappush(fut[o.eng], (0.0, o.idx, o))
        free = {e: 0.0 for e in self.ENGS}
        dfin = {e: [] for e in self.ENGS}
        KD = self.KDMA
        left = len(ops)
        while left:
            best = None
            for e in self.ENGS:
                while fut[e] and fut[e][0][0] <= free[e]:
                    rt, idx, o = heapq.heappop(fut[e])
                    heapq.heappush(now[e], (pk(o), o.idx, o))
                if now[e]:
                    _, idx, o = now[e][0]
                    stt = free[e]
                elif fut[e]:
                    rt, idx, o = fut[e][0]
                    stt = max(rt, free[e])
                else:
                    continue
                if o.dma and len(dfin[e]) >= KD:
                    stt = max(stt, dfin[e][-KD])
                if best is None or (stt, pk(o)) < (best[0], pk(best[3])):
                    best = (stt, idx, e, o)
            stt, idx, e, o = best
            if now[e] and now[e][0][2] is o:
                heapq.heappop(now[e])
            else:
                heapq.heappop(fut[e])
            free[e] = stt + o.busy
            o.stt = stt
            fin = stt + o.lat
            if o.dma:
                dfin[e].append(fin)
            order[e].append(o)
            left -= 1
            for su in succs[o]:
                lat = 30.0 if (su.eng == o.eng) else self.SYNC
                ready[su] = max(ready[su], fin + lat)
                indeg[su] -= 1
                if indeg[su] == 0:
                    heapq.heappush(fut[su.eng], (ready[su], su.idx, su))
        return order

    def emit(self, final_wait_eng="sp"):
        nc = self.nc
        nph = self.phase + 1
        phases = [[] for _ in range(nph)]
        for o in self.ops:
            phases[o.phase].append(o)
        streams = {e: [] for e in self.ENGS}
        for ph in range(nph):
            od = self._schedule(phases[ph])
            for e in self.ENGS:
                streams[e].extend(od[e])
        cnt = {e: 0 for e in self.ENGS}
        dcnt = {e: 0 for e in self.ENGS}
        last_c = [dict() for _ in range(nph)]
        last_d = [dict() for _ in range(nph)]
        for e in self.ENGS:
            for o in streams[e]:
                if o.dma:
                    dcnt[e] += 1
                    o.pos = dcnt[e]
                    last_d[o.phase][e] = dcnt[e]
                else:
                    cnt[e] += 1
                    o.pos = cnt[e]
                    last_c[o.phase][e] = cnt[e]
        cum_c = [dict() for _ in range(nph + 1)]
        cum_d = [dict() for _ in range(nph + 1)]
        for ph in range(nph):
            cum_c[ph + 1] = dict(cum_c[ph]); cum_c[ph + 1].update(last_c[ph])
            cum_d[ph + 1] = dict(cum_d[ph]); cum_d[ph + 1].update(last_d[ph])
        self.cnt, self.dcnt = cnt, dcnt
        KD = self.KDMA

        def dkey(p, n):
            return ("d", p, (n - 1) % KD), 16 * ((n - 1) // KD + 1)

        know = {e: {} for e in self.ENGS}
        cur_ph = {e: 0 for e in self.ENGS}

        def need(kn, waits, key, val, vc):
            if kn.get(key, 0) >= val:
                return
            waits.append((key, val))
            kn[key] = val
            if vc:
                for k2, v2 in vc.items():
                    if kn.get(k2, 0) < v2:
                        kn[k2] = v2

        def barrier_waits(E, kn, waits, ph):
            for p, n in cum_c[ph].items():
                if not (p == E == "pe"):
                    need(kn, waits, ("c", p), n, None)
            for p, n in cum_d[ph].items():
                for m in range(max(1, n - KD + 1), n + 1):
                    k_, v_ = dkey(p, m)
                    need(kn, waits, k_, v_, None)

        for o in sorted(self.ops, key=lambda o: (o.phase, o.stt, o.idx)):
            E = o.eng
            kn = know[E]
            waits = []
            if o.phase != cur_ph[E]:
                cur_ph[E] = o.phase
                barrier_waits(E, kn, waits, o.phase)
            deps = [d for d in o.deps if d.phase == o.phase]
            deps.sort(key=lambda d: (-d.stt, d.idx))
            for d in deps:
                if d.dma:
                    k_, v_ = dkey(d.eng, d.pos)
                elif d.eng == E == "pe":
                    continue
                else:
                    k_, v_ = ("c", d.eng), d.pos
                need(kn, waits, k_, v_, d.vc)
            if o.dma and o.pos > KD:
                k_, v_ = dkey(E, o.pos - KD)
                need(kn, waits, k_, v_, None)
            o.waits = waits
            o.vc = dict(kn)
        final_waits = []
        barrier_waits(final_wait_eng, know[final_wait_eng], final_waits, nph)

        with contextlib.ExitStack() as st:
            csem = {e: st.enter_context(nc.semaphore("c_" + e)) for e in self.ENGS if cnt[e]}
            dsem = {e: [st.enter_context(nc.semaphore("d_%s%d" % (e, i))) for i in range(KD)]
                    for e in self.ENGS if dcnt[e]}
            block = st.enter_context(nc.Block())

            def body(ename, eng):
                def do_wait(key, val):
                    if key[0] == "c":
                        eng.wait_ge(csem[key[1]], val)
                    else:
                        eng.wait_ge(dsem[key[1]][key[2]], val)

                for o in streams[ename]:
                    for (key, val) in o.waits:
                        do_wait(key, val)
                    ins = o.fn(eng)
                    if o.dma:
                        ins.then_inc(dsem[ename][(o.pos - 1) % KD], 16)
                    else:
                        ins.then_inc(csem[ename], 1)
                if ename == final_wait_eng:
                    for (key, val) in final_waits:
                        do_wait(key, val)

            @block.tensor
            def _(e):
                body("pe", e)

            @block.scalar
            def _(e):
                body("act", e)

            @block.vector
            def _(e):
                body("dve", e)

            @block.gpsimd
            def _(e):
                body("pool", e)

            @block.sync
            def _(e):
                body("sp", e)


def _prod(s):
    r = 1
    for v in s:
        r *= v
    return r


def build_program(stop_after=99, dbg=None):
    nc = bass.Bass("TRN2", target_bir_lowering=False)

    def din(name, shape):
        return nc.dram_tensor(name, list(shape), F32, kind="ExternalInput").ap()

    x = din("x", [SEQ, D])
    ctx = din("ctx", [CTX, D])
    c_in = din("c", [D])
    cctx_in = din("c_ctx", [D])
    w_ada = din("w_ada", [D, 3 * D])
    b_ada = din("b_ada", [3 * D])
    g_norm = din("g_norm", [D])
    w_in = din("w_in", [D, N_IN])
    b_in = din("b_in", [N_IN])
    w_conv = din("w_conv", [3, D])
    b_conv = din("b_conv", [D])
    g_head = din("g_head", [D])
    w_pa = din("w_pa", [D, D])
    w_pb = din("w_pb", [D, D])
    w_out = din("w_out", [D, D])
    b_out = din("b_out", [D])
    g_final = din("g_final", [D])
    y = nc.dram_tensor("y", [SEQ, D], F32, kind="ExternalOutput").ap()
    dbg_out = {}
    if dbg:
        for name, shape in dbg.items():
            dbg_out[name] = nc.dram_tensor("dbg_" + name, [128, _prod(shape)], F32, kind="ExternalOutput").ap()

    ARENA = 210944
    with contextlib.ExitStack() as st:
        arena = st.enter_context(nc.sbuf_tensor("arena", [128, ARENA // 2], BF16))
        psum = st.enter_context(nc.psum_tensor("psum", [128, 4096], F32))
        P = Prog(nc)

        cur = [0]

        def V(off, shape, dt):
            n = _prod(shape)
            sz = 2 if dt == BF16 else 4
            assert off % 4 == 0 and off + n * sz <= ARENA, (off, shape)
            a = arena[:, off // 2: off // 2 + n * sz // 2]
            if dt != BF16:
                a = a.bitcast(dt)
            if len(shape) == 2:
                a = a.rearrange("p (a b) -> p a b", a=shape[0])
            elif len(shape) == 3:
                a = a.rearrange("p (a b c) -> p a b c", a=shape[0], b=shape[1])
            return a

        def alloc(shape, dt):
            n = _prod(shape) * (2 if dt == BF16 else 4)
            n = (n + 31) // 32 * 32
            off = cur[0]
            cur[0] += n
            return V(off, shape, dt)

        def PS(bank, off_f32, n_f32):
            return psum[:, bank * 512 + off_f32: bank * 512 + off_f32 + n_f32]

        def PSB(bank, off_bf, n_bf):
            return psum[:, bank * 512: (bank + 1) * 512].bitcast(BF16)[:, off_bf: off_bf + n_bf]

        ident_bf = alloc([128], BF16)
        ident_f = alloc([128], F32)
        M_le = alloc([128], F32)
        M_lt = alloc([128], F32)
        M_ge = alloc([128], F32)
        M_gt = alloc([128], F32)
        ones_f = alloc([128], F32)
        M_ge_bf = alloc([128], BF16)
        pp = alloc([128], F32)
        pp2 = alloc([40], F32)
        dp = alloc([64], F32)
        s_f = alloc([16], F32)
        s_t = alloc([16], F32)
        s2 = alloc([16], BF16)
        gts = alloc([NT, 32], F32)
        lf = alloc([NT, 16], F32)
        ea = alloc([NT, 16], F32)
        wk = alloc([NT, 16], F32)
        ern = alloc([NT, 16], F32)
        eB = alloc([NT, 16], F32)
        tg1 = alloc([NT, 16], F32)
        tg2 = alloc([NT, 16], F32)
        hxT = alloc([8, TOK], BF16)
        ybT = alloc([8, SEQ], BF16)
        bg_bc = alloc([32], F32)
        wg = alloc([8, 32], BF16)
        fsc = alloc([16], F32)
        PH = cur[0]
        assert PH % 32 == 0
        whd = [alloc([8, 640], BF16) for _ in range(2)]
        PH2 = cur[0]
        wv_in = w_in.rearrange("(kc p) n -> p kc n", p=128)

        BQ, BK, BREST, WCV, BCV, GHD = 8, 16, 24, 88, 112, 120
        B_O, B_ZB, B_BA, B_CA, B_XA, B_ZA, B_GA, B_GB = [BREST + 8 * i for i in range(8)]
        AX, BX, AC, BC, BOH, BGAH, BGBH, GHH = [8 * i for i in range(8)]

        cur[0] = PH2
        pst = alloc([128], F32)
        pst2 = alloc([128], F32)
        wada = [alloc([8, 1024], BF16) for _ in range(2)]
        modx = alloc([16], F32)
        modc = alloc([16], F32)
        PRO_LATE = cur[0]
        xst = [alloc([D], F32) for _ in range(6)]
        xs = [alloc([D], BF16) for _ in range(4)]
        sqj = alloc([D], BF16)
        ssx = alloc([NT], F32)
        rsx = alloc([NT], F32)
        lnx = alloc([NT], F32)
        assert cur[0] <= ARENA, cur[0]

        def mk_mask(t, pattern_step, cm, op, fill_in, fill_out):
            P.op("pool", lambda e: e.memset(t, fill_in), writes=[("c", id(t))])
            P.op("pool", lambda e: e.affine_select(out=t, in_=t, pattern=[[pattern_step, 128]], compare_op=op,
                                                   fill=fill_out, base=0, channel_multiplier=cm),
                 reads=[("c", id(t))], writes=[("c", id(t))])

        mk_mask(ident_f, -1, 1, ALU.not_equal, 0.0, 1.0)
        mk_mask(M_le, 1, -1, ALU.is_ge, 1.0, 0.0)
        mk_mask(M_lt, 1, -1, ALU.is_gt, 1.0, 0.0)
        mk_mask(M_ge, -1, 1, ALU.is_ge, 1.0, 0.0)
        mk_mask(M_gt, -1, 1, ALU.is_gt, 1.0, 0.0)
        P.op("pool", lambda e: e.memset(ones_f, 1.0), writes=["ones_f"])
        P.op("pool", lambda e: e.tensor_copy(out=ident_bf, in_=ident_f), reads=[("c", id(ident_f))], writes=["ident_bf"])
        P.op("pool", lambda e: e.tensor_copy(out=M_ge_bf, in_=M_ge), reads=[("c", id(M_ge))], writes=["M_ge_bf"])
        P.op("pool", lambda e: e.memset(pst2, 0.0), writes=["pst2", ("pst2", 1), ("pst2", 2), ("pst2", 3)])

        def row(ap1d, n):
            return ap1d.rearrange("(c p) -> c p", p=128)

        P.dma("sp", lambda e: e.dma_start(out=pst[0:8, :], in_=row(g_norm, 8)), writes=[("pst", 0)])
        P.dma("sp", lambda e: e.dma_start(out=pst[8:24, :], in_=row(b_in[0:2048], 16)), writes=[("pst", 1)])
        P.dma("sp", lambda e: e.dma_start(out=pst[24:88, :], in_=row(b_in[3104:3104 + 8192], 64)), writes=[("pst", 2)])
        P.dma("sp", lambda e: e.dma_start(out=pst[88:112, :], in_=w_conv.rearrange("t (c p) -> (t c) p", p=128)),
              writes=[("pst", 3)])
        P.dma("sp", lambda e: e.dma_start(out=pst[112:120, :], in_=row(b_conv, 8)), writes=[("pst", 4)])
        P.dma("sp", lambda e: e.dma_start(out=pst[120:128, :], in_=row(g_head, 8)), writes=[("pst", 5)])
        P.dma("sp", lambda e: e.dma_start(out=pst2[0:16, :], in_=row(b_ada[0:2048], 16)), reads=[], writes=["pst2"])
        P.dma("sp", lambda e: e.dma_start(out=pst2[16:24, :], in_=row(c_in, 8)), writes=[("pst2", 1)])
        P.dma("sp", lambda e: e.dma_start(out=pst2[24:32, :], in_=row(cctx_in, 8)), writes=[("pst2", 2)])
        P.dma("sp", lambda e: e.dma_start(out=pst2[32:40, :], in_=row(b_in[2048:3072], 8)), writes=[("pst2", 3)])
        for (dst, src, nm) in ((bg_bc, b_in[3072:3104], "bg"),):
            P.dma("sp", lambda e, dst=dst, src=src: e.dma_start(out=dst, in_=src.partition_broadcast(128)), writes=[nm])
        wv_ada = w_ada.rearrange("(kc p) n -> p kc n", p=128)
        for j in range(2):
            P.dma("pool", lambda e, j=j: e.dma_start(out=wada[j], in_=wv_ada[:, :, j * 1024:(j + 1) * 1024]),
                  writes=[("wada", j)])

        P.dma("pool", lambda e: e.dma_start(out=wg, in_=wv_in[:, :, 3072:3104]), writes=["wg"])

        def head_cols(h):
            return [h * 128, 1024 + h * 128, 2048 + h * 128, 3104 + h * 128, 3104 + 1024 + h * 128]

        def load_head_w(h, extra=()):
            for j, c0 in enumerate(head_cols(h)):
                P.dma("pool", lambda e, h=h, j=j, c0=c0: e.dma_start(out=whd[h % 2][:, :, j * 128:(j + 1) * 128],
                                                                     in_=wv_in[:, :, c0:c0 + 128]),
                      reads=list(extra), writes=[("whd", h % 2, j)])

        P.op("pe", lambda e: e.transpose(out=PS(0, 0, 128), in_=pst, identity=ident_f),
             reads=[("pst", i) for i in range(6)] + [("c", id(ident_f))], writes=[("ps", 0)])
        P.op("dve", lambda e: e.tensor_copy(out=pp, in_=PS(0, 0, 128)), reads=[("ps", 0)], writes=["pp"])
        P.op("pe", lambda e: e.transpose(out=PS(1, 0, 64), in_=pst2[0:64, :], identity=ident_f[0:64, 0:64]),
             reads=["pst2", ("pst2", 1), ("pst2", 2), ("pst2", 3), ("c", id(ident_f))], writes=[("ps", 1)])
        P.op("dve", lambda e: e.tensor_copy(out=pp2, in_=PS(1, 0, 40)), reads=[("ps", 1)], writes=["pp2"])
        if stop_after <= 1:
            P.emit()
            return nc
        P.op("act", lambda e: e.activation(out=s_t, in_=pp2[:, 16:32], func=AF.Exp, scale=-1.0), reads=["pp2"], writes=["s_t"])
        P.op("dve", lambda e: e.tensor_scalar(out=s_t, in0=s_t, scalar1=1.0, scalar2=None, op0=ALU.add),
             reads=["s_t"], writes=["s_t"])
        P.op("dve", lambda e: e.reciprocal(out=s_t, in_=s_t), reads=["s_t"], writes=["s_t"])
        P.op("dve", lambda e: e.tensor_tensor(out=s_f, in0=s_t, in1=pp2[:, 16:32], op=ALU.mult),
             reads=["s_t", "pp2"], writes=["s_f"])
        P.op("dve", lambda e: e.tensor_copy(out=s2, in_=s_f), reads=["s_f"], writes=["s2"])
        for j in range(2):
            for mc in range(8):
                for kc in range(8):
                    m = j * 8 + mc
                    P.op("pe", lambda e, j=j, mc=mc, kc=kc, m=m: e.matmul(
                        PS(2, 2 * m, 2), lhsT=wada[j][:, kc, mc * 128:(mc + 1) * 128],
                        rhs=s2[:, kc::8], start=(kc == 0), stop=(kc == 7)),
                        reads=[("wada", j), "s2"], writes=[("ps", 2)])
        modv = PS(2, 0, 32).rearrange("p (m n) -> p m n", n=2)
        P.op("dve", lambda e: e.tensor_tensor(out=modx, in0=modv[:, :, 0], in1=pp2[:, 0:16], op=ALU.add),
             reads=[("ps", 2), "pp2"], writes=["modx"])
        P.op("dve", lambda e: e.tensor_tensor(out=modc, in0=modv[:, :, 1], in1=pp2[:, 0:16], op=ALU.add),
             reads=[("ps", 2), "pp2"], writes=["modc"])
        for (mod, a0, b0, nm) in ((modx, AX, BX, "modx"), (modc, AC, BC, "modc")):
            P.op("dve", lambda e, mod=mod, a0=a0: e.scalar_tensor_tensor(
                out=dp[:, a0:a0 + 8], in0=mod[:, 8:16], scalar=1.0, in1=pp[:, 0:8], op0=ALU.add, op1=ALU.mult),
                reads=[nm, "pp"], writes=[("dp", a0)])
            P.op("dve", lambda e, mod=mod, b0=b0: e.tensor_copy(out=dp[:, b0:b0 + 8], in_=mod[:, 0:8]),
                 reads=[nm], writes=[("dp", b0)])
        for (dst, src) in ((BOH, B_O), (BGAH, B_GA), (BGBH, B_GB), (GHH, GHD)):
            P.op("dve", lambda e, dst=dst, src=src: e.tensor_scalar(out=dp[:, dst:dst + 8], in0=pp[:, src:src + 8],
                                                                    scalar1=0.5, scalar2=None, op0=ALU.mult),
                 reads=["pp"], writes=[("dp", dst)])
        if stop_after <= 2:
            P.emit()
            return nc
        def xsrc(tt):
            return ctx[tt * 128:(tt + 1) * 128, :] if tt < 2 else x[(tt - 2) * 128:(tt - 1) * 128, :]

        for tt in range(NT):
            xb = xst[tt % 6]
            P.dma("sp", lambda e, tt=tt, xb=xb: e.dma_start(out=xb, in_=xsrc(tt)), writes=[("xst", tt % 6), ("xld", tt)])
            P.op("act", lambda e, tt=tt, xb=xb: e.activation(out=sqj, in_=xb, func=AF.Square, accum_out=ssx[:, tt:tt + 1]),
                 reads=[("xst", tt % 6)], writes=["sqj", ("ssx", tt)])
            P.op("act", lambda e, tt=tt: e.activation(out=lnx[:, tt:tt + 1], in_=ssx[:, tt:tt + 1], func=AF.Ln,
                                                      scale=1.0 / D, bias=EPS),
                 reads=[("ssx", tt)], writes=[("lnx", tt)])
            P.op("act", lambda e, tt=tt: e.activation(out=rsx[:, tt:tt + 1], in_=lnx[:, tt:tt + 1], func=AF.Exp, scale=-0.5),
                 reads=[("lnx", tt)], writes=[("rsx", tt)])
            xsb = xs[tt % 4]
            P.op("dve", lambda e, tt=tt, xb=xb, xsb=xsb: e.tensor_scalar(out=xsb, in0=xb, scalar1=rsx[:, tt:tt + 1],
                                                                          scalar2=None, op0=ALU.mult),
                 reads=[("xst", tt % 6), ("rsx", tt)], writes=[("xs", tt % 4)])
            grp = tt // 2
            bank0 = (grp % 4) * 2
            half = tt % 2
            for kc in range(8):
                b = bank0 + (kc // 4)
                off = (kc % 4) * 256 + half * 128
                P.op("pe", lambda e, xsb=xsb, kc=kc, b=b, off=off: e.transpose(
                    out=PSB(b, off, 128), in_=xsb[:, kc * 128:(kc + 1) * 128], identity=ident_bf),
                    reads=[("xs", tt % 4), "ident_bf"], writes=[("ps", b, kc % 4, half)])
            if half == 1:
                a0, b0 = (AC, BC) if tt < 2 else (AX, BX)
                for kc in range(8):
                    b = bank0 + (kc // 4)
                    src = PSB(b, (kc % 4) * 256, 256)
                    dst = hxT[:, kc, (tt - 1) * 128:(tt + 1) * 128]
                    rd = [("ps", b, kc % 4, 0), ("ps", b, kc % 4, 1), ("dp", a0), ("dp", b0)]
                    wr = [("hxT", kc, tt - 1), ("hxT", kc, tt)]
                    if kc < 2:
                        P.op("act", lambda e, src=src, dst=dst, kc=kc, a0=a0, b0=b0: e.activation(
                            out=dst, in_=src, func=AF.Identity, scale=dp[:, a0 + kc:a0 + kc + 1],
                            bias=dp[:, b0 + kc:b0 + kc + 1]), reads=rd, writes=wr)
                    else:
                        P.op("dve", lambda e, src=src, dst=dst, kc=kc, a0=a0, b0=b0: e.tensor_scalar(
                            out=dst, in0=src, scalar1=dp[:, a0 + kc:a0 + kc + 1], scalar2=dp[:, b0 + kc:b0 + kc + 1],
                            op0=ALU.mult, op1=ALU.add), reads=rd, writes=wr)

        def dump(name, src_ap, reads):
            if name in dbg_out:
                a = src_ap
                if len(a.shape) == 3:
                    a = a.rearrange("p a b -> p (a b)")
                if a.dtype == BF16:
                    a = a.bitcast(F32)
                P.dma("sp", lambda e: e.dma_start(out=dbg_out[name], in_=a), reads=reads)

        load_head_w(0, [("xld", 9)])
        load_head_w(1, [("xld", 17)])
        if stop_after <= 3:
            P.barrier()
            dump("pp", pp, [])
            dump("dp", dp, [])
            dump("hxT", hxT, [])
            P.emit()
            return nc

        allhx = [("hxT", kc, tt) for kc in range(8) for tt in range(NT)]
        for tt in range(NT):
            b, off = (0, tt * 32) if tt < 16 else (1, (tt - 16) * 32)
            for kc in range(8):
                P.op("pe", lambda e, tt=tt, kc=kc, b=b, off=off: e.matmul(
                    PS(b, off, 32), lhsT=hxT[:, kc, tt * 128:(tt + 1) * 128], rhs=wg[:, kc, :],
                    start=(kc == 0), stop=(kc == 7)), reads=["wg", ("hxT", kc, tt)], writes=[("ps", b)])
        P.op("dve", lambda e: e.tensor_tensor(out=gts[:, 0:16, :], in0=PS(0, 0, 512).rearrange("p (a b) -> p a b", b=32),
                                              in1=bg_bc.unsqueeze(1).to_broadcast([128, 16, 32]), op=ALU.add),
             reads=[("ps", 0), "bg"], writes=[("gts", 0)])
        P.op("dve", lambda e: e.tensor_tensor(out=gts[:, 16:18, :], in0=PS(1, 0, 64).rearrange("p (a b) -> p a b", b=32),
                                              in1=bg_bc.unsqueeze(1).to_broadcast([128, 2, 32]), op=ALU.add),
             reads=[("ps", 1), "bg"], writes=[("gts", 1)])
        allg = [("gts", 0), ("gts", 1)]
        P.op("act", lambda e: e.activation(out=lf[:, :, 0:8], in_=gts[:, :, 8:16], func=AF.Exp, scale=-1.0),
             reads=allg, writes=["lf0"])
        P.op("act", lambda e: e.activation(out=lf[:, :, 8:16], in_=gts[:, :, 24:32], func=AF.Exp, scale=-1.0),
             reads=allg, writes=["lf1"])
        P.op("act", lambda e: e.activation(out=lf, in_=lf, func=AF.Ln, bias=1.0), reads=["lf0", "lf1"], writes=["lf"])
        lf2 = lf.rearrange("p a b -> p (a b)")
        for i, Mk in enumerate((M_le, M_gt, M_ge, M_lt, ones_f)):
            P.op("pe", lambda e, i=i, Mk=Mk: e.matmul(PS(2 + i, 0, 288), lhsT=Mk, rhs=lf2, start=True, stop=True),
                 reads=["lf", ("c", id(Mk)), "ones_f"], writes=[("ps", 2 + i)])

        def cs(i):
            return PS(2 + i, 0, 288).rearrange("p (a b) -> p a b", b=16)
        Pf, Sf, Pb, Sb, Tt = cs(0), cs(1), cs(2), cs(3), cs(4)
        for (half, Pm, Sm, pb_, sb_, ic) in ((0, Pf, Sf, 2, 3, 0), (1, Pb, Sb, 4, 5, 16)):
            sl = slice(half * 8, half * 8 + 8)
            P.op("dve", lambda e, sl=sl, Pm=Pm, ic=ic: e.tensor_tensor(out=tg1[:, :, sl], in0=Pm[:, :, sl],
                                                                       in1=gts[:, :, ic:ic + 8], op=ALU.add),
                 reads=[("ps", pb_)] + allg, writes=[("tg1", half)])
            P.op("act", lambda e, sl=sl: e.activation(out=ea[:, :, sl], in_=tg1[:, :, sl], func=AF.Exp),
                 reads=[("tg1", half)], writes=[("ea", half)])
            P.op("dve", lambda e, sl=sl, Sm=Sm, ic=ic: e.scalar_tensor_tensor(
                out=tg2[:, :, sl], in0=Sm[:, :, sl], scalar=-1.0, in1=gts[:, :, ic:ic + 8], op0=ALU.mult, op1=ALU.add),
                reads=[("ps", sb_)] + allg, writes=[("tg2", half)])
            P.op("act", lambda e, sl=sl: e.activation(out=wk[:, :, sl], in_=tg2[:, :, sl], func=AF.Exp),
                 reads=[("tg2", half)], writes=[("wk", half)])
            P.op("act", lambda e, sl=sl, Pm=Pm: e.activation(out=ern[:, :, sl], in_=Pm[:, :, sl], func=AF.Exp),
                 reads=[("ps", pb_)], writes=[("ern", half)])
        P.op("act", lambda e: e.activation(out=eB, in_=Tt, func=AF.Exp, scale=-1.0), reads=[("ps", 6)], writes=["eB"])
        if stop_after <= 4:
            P.barrier()
            dump("gts", gts, [])
            dump("ea", ea, [])
            dump("wk", wk, [])
            dump("ern", ern, [])
            dump("eB", eB, [])
            P.emit()
            return nc
        S4OUT = {"gts", "lf0", "lf1", "lf", "tg1", "tg2", "ea", "wk", "ern", "eB", "ps"}
        scratch = {"pst", "pst2", "wada", "modx", "modc"}
        late = {"xst", "xs", "sqj", "ssx", "lnx", "rsx", "xld"}
        allk = set(P.last_w) | set(P.readers)
        rd = [k for k in allk if (k if isinstance(k, str) else k[0]) not in (S4OUT | scratch | late | {"whd", "wg", "hxT"})]
        P.fence_all_later(P.op("pool", lambda e: e.memset(fsc[:, 0:1], 0.0), reads=rd, writes=P.keys_named(scratch) + ["S5"]))

        cur[0] = PH2
        qT, kT, gob, kBf, kBb, V1, Cstf, Cstb = [], [], [], [], [], [], [], []
        for _hb in range(2):
            qT.append(alloc([SEQ], BF16))
            kT.append(alloc([TOK], BF16))
            gob.append(alloc([SEQ], BF16))
            _kB = alloc([NT, 2, 128], BF16)
            kBf.append(_kB[:, :, 0, :])
            kBb.append(_kB[:, :, 1, :])
            kBB = (kBB if _hb else []) + [_kB]
            V1.append(alloc([NT, 130], BF16))
            if _hb == 0:
                S5_HB0_END = cur[0]
        Cst2 = alloc([NLT, 2, 130], BF16)
        _cf, _cb = Cst2[:, :, 0, :], Cst2[:, :, 1, :]
        Cstf, Cstb = [_cf, _cf], [_cb, _cb]
        Cf = [alloc([130], F32) for _ in range(2)]
        Cb = [alloc([130], F32) for _ in range(2)]
        Hh = alloc([NLT, 128], F32)
        hn = alloc([NLT, 128], BF16)
        ssh = alloc([NLT], F32)
        rsh = alloc([NLT], F32)
        mhalf = alloc([NLT], F32)
        t_o = [alloc([512], BF16) for _ in range(2)]
        t_z = [alloc([512], BF16) for _ in range(2)]
        hf = [alloc([128], F32) for _ in range(2)]
        sqh = alloc([128], BF16)
        Spf = [alloc([128], BF16) for _ in range(2)]
        Spb = [alloc([128], BF16) for _ in range(2)]
        Spr = [alloc([128], BF16) for _ in range(2)]
        dcl = [alloc([2], F32) for _ in range(2)]
        dab = [alloc([2], F32) for _ in range(2)]
        sfb = [alloc([2], F32) for _ in range(2)]
        vtmp = [alloc([512], BF16) for _ in range(2)]
        tmpg = [alloc([512], BF16) for _ in range(2)]
        wcv0 = alloc([8, 512], BF16)
        assert cur[0] <= ARENA, cur[0]
        S5_END = cur[0]

        P.op("pool", lambda e: e.memset(mhalf, -0.5), writes=["mhalf"])
        P.op("pool", lambda e: e.memset(V1[0], 1.0), writes=[("V1ones", 0)])
        fm_rot = [0]
        tm_rot = [0]

        def gen_proj(h):
            hb = h % 2
            W = whd[hb]
            wkey = lambda j: ("whd", hb, j)
            jmap = {"q": 0, "k": 1, "v": 2, "o": 3, "zb": 4}
            blocks = [(CTX + bi * 512, 512, bi) for bi in range(4)] + [(0, 256, 4)]
            for (tok0, ntk, bi) in blocks:
                fams = ("q", "k", "v", "o", "zb") if bi < 4 else ("k", "v")
                lsl = slice(bi * 512, (bi + 1) * 512)
                gsl = slice(tok0, tok0 + ntk)
                vt = vtmp[bi % 2]
                for fam in fams:
                    j = jmap[fam]
                    b = fm_rot[0] % 4
                    fm_rot[0] += 1
                    for kc in range(8):
                        P.op("pe", lambda e, ntk=ntk, gsl=gsl, lsl=lsl, bi=bi, vt=vt, b=b, j=j, kc=kc: e.matmul(
                            PS(b, 0, ntk), lhsT=W[:, kc, j * 128:(j + 1) * 128], rhs=hxT[:, kc, gsl],
                            start=(kc == 0), stop=(kc == 7)),
                            reads=[wkey(j)] + [("hxT", kc, t_) for t_ in range(tok0 // 128, (tok0 + ntk) // 128)], writes=[("ps", b)])
                    if fam == "q":
                        P.op("dve", lambda e, ntk=ntk, gsl=gsl, lsl=lsl, bi=bi, vt=vt, b=b: e.tensor_scalar(
                            out=qT[hb][:, lsl], in0=PS(b, 0, ntk), scalar1=pp[:, BQ + h:BQ + h + 1], scalar2=QS,
                            op0=ALU.add, op1=ALU.mult), reads=[("ps", b)], writes=[("qT", hb, bi)])
                    elif fam == "k":
                        P.op("act", lambda e, ntk=ntk, gsl=gsl, lsl=lsl, bi=bi, vt=vt, b=b: e.activation(
                            out=kT[hb][:, gsl], in_=PS(b, 0, ntk), func=AF.Identity, bias=pp[:, BK + h:BK + h + 1]),
                            reads=[("ps", b)], writes=[("kT", hb, bi)])
                    elif fam == "v":
                        P.op("dve", lambda e, ntk=ntk, gsl=gsl, lsl=lsl, bi=bi, vt=vt, b=b: e.tensor_scalar(
                            out=vt[:, 0:ntk], in0=PS(b, 0, ntk), scalar1=pp2[:, 32 + h:33 + h], scalar2=None, op0=ALU.add),
                            reads=[("ps", b)], writes=[("vtmp", bi % 2)])
                    elif fam == "o":
                        P.op("act", lambda e, ntk=ntk, gsl=gsl, lsl=lsl, bi=bi, vt=vt, b=b: e.activation(
                            out=t_o[bi % 2], in_=PS(b, 0, ntk), func=AF.Tanh, scale=0.5,
                            bias=dp[:, BOH + h:BOH + h + 1]), reads=[("ps", b)], writes=[("t_o", bi % 2)])
                    else:
                        P.op("act", lambda e, ntk=ntk, gsl=gsl, lsl=lsl, bi=bi, vt=vt, b=b: e.activation(
                            out=t_z[bi % 2], in_=PS(b, 0, ntk), func=AF.Silu, bias=pp[:, B_ZB + h:B_ZB + h + 1]),
                            reads=[("ps", b)], writes=[("t_z", bi % 2)])
                        P.op("pool", lambda e, ntk=ntk, gsl=gsl, lsl=lsl, bi=bi, vt=vt: e.tensor_tensor(out=tmpg[bi % 2], in0=t_o[bi % 2], in1=t_z[bi % 2], op=ALU.mult),
                             reads=[("t_o", bi % 2), ("t_z", bi % 2)], writes=[("tmpg", bi % 2)])
                        P.op("pool", lambda e, ntk=ntk, gsl=gsl, lsl=lsl, bi=bi, vt=vt: e.tensor_tensor(out=gob[hb][:, lsl], in0=tmpg[bi % 2], in1=t_z[bi % 2], op=ALU.add),
                             reads=[("tmpg", bi % 2), ("t_z", bi % 2)], writes=[("gob", hb, bi)])
                    yield
                for ti in range(ntk // 128):
                    tt = (tok0 // 128 + ti)
                    b = fm_rot[0] % 4
                    fm_rot[0] += 1
                    P.op("pe", lambda e, ntk=ntk, gsl=gsl, lsl=lsl, bi=bi, vt=vt, b=b, tt=tt: e.transpose(out=PSB(b, 0, 128), in_=kT[hb][:, tt * 128:(tt + 1) * 128],
                                                                 identity=ident_bf),
                         reads=[("kT", hb, bi), "ident_bf"], writes=[("ps", b)])
                    P.op("pe", lambda e, ntk=ntk, gsl=gsl, lsl=lsl, bi=bi, vt=vt, b=b, ti=ti: e.transpose(out=PSB(b, 128, 128), in_=vt[:, ti * 128:(ti + 1) * 128],
                                                                 identity=ident_bf),
                         reads=[("vtmp", bi % 2), "ident_bf"], writes=[("ps", b)])
                    P.op("dve", lambda e, ntk=ntk, gsl=gsl, lsl=lsl, bi=bi, vt=vt, tt=tt, b=b: e.tensor_tensor(
                        out=kBB[hb][:, tt, :, :], in0=PSB(b, 0, 128).unsqueeze(1).to_broadcast([128, 2, 128]),
                        in1=wk[:, tt, h::8].unsqueeze(2).to_broadcast([128, 2, 128]), op=ALU.mult),
                        reads=[("ps", b), ("wk", 0), ("wk", 1)], writes=[("kBf", hb, tt), ("kBb", hb, tt)])
                    P.op("act", lambda e, ntk=ntk, gsl=gsl, lsl=lsl, bi=bi, vt=vt, tt=tt, b=b: e.activation(
                        out=V1[hb][:, tt, 0:128], in_=PSB(b, 128, 128), func=AF.Copy),
                        reads=[("ps", b), ("V1ones", hb)], writes=[("V1", hb, tt)])
                    if ti % 2 == 1:
                        yield

        def gen_scan(h):
            hb = h % 2
            P.op("pool", lambda e: e.memset(Cf[0], 0.0), writes=[("Cf", 0)])
            P.op("pool", lambda e: e.memset(Cb[0], 0.0), writes=[("Cb", 0)])
            f_order = list(range(0, 17))
            b_order = [1, 0] + list(range(17, 2, -1))
            for step in range(17):
                for (dirn, order, kB, Cs, Cst, col0, bank) in (("f", f_order, kBf, Cf, Cstf, 0, 4), ("b", b_order, kBb, Cb, Cstb, 8, 5)):
                    tt = order[step]
                    src, dst = Cs[step % 2], Cs[(step + 1) % 2]
                    ck = "C" + dirn
                    P.op("pe", lambda e, tt=tt, kB=kB, bank=bank: e.matmul(
                        PS(bank, 0, 129), lhsT=kB[hb][:, tt, :], rhs=V1[hb][:, tt, 0:129], start=True, stop=True),
                        reads=[("kB" + dirn, hb, tt), ("V1", hb, tt)], writes=[("ps", bank)])
                    P.op("dve", lambda e, tt=tt, src=src, dst=dst, bank=bank, col0=col0: e.scalar_tensor_tensor(
                        out=dst[:, 0:129], in0=src[:, 0:129], scalar=eB[:, tt, col0 + h:col0 + h + 1],
                        in1=PS(bank, 0, 129), op0=ALU.mult, op1=ALU.add),
                        reads=[(ck, step % 2), ("ps", bank), "eB"], writes=[(ck, (step + 1) % 2)])
                    if dirn == "f":
                        nxt = tt + 1
                    else:
                        nxt = 17 if step == 1 else (tt - 1 if step >= 2 else None)
                    if nxt is not None and nxt >= 2:
                        P.op("pool", lambda e, dst=dst, Cst=Cst, nxt=nxt: e.tensor_copy(out=Cst[hb][:, nxt - 2, 0:129], in_=dst[:, 0:129]),
                             reads=[(ck, (step + 1) % 2), "Cstpad"], writes=[("Cst" + dirn, 0, nxt)])
                yield

            def emit_S(lt):
                tt = lt + 2
                s2i = lt % 2
                tsl = slice(lt * 128, (lt + 1) * 128)
                for bank in (4,):
                    P.op("pe", lambda e, tsl=tsl, bank=bank, tt=tt: e.matmul(PS(bank, 0, 128), lhsT=kT[hb][:, tt * 128:(tt + 1) * 128],
                                                                             rhs=qT[hb][:, tsl], start=True, stop=True),
                         reads=[("kT", hb, lt // 4), ("qT", hb, lt // 4)], writes=[("ps", bank)])
                P.op("dve", lambda e, tt=tt, s2i=s2i: e.scalar_tensor_tensor(
                    out=Spf[s2i], in0=PS(4, 0, 128), scalar=ea[:, tt, h:h + 1], in1=M_le, op0=ALU.mult, op1=ALU.mult),
                    reads=[("ps", 4), ("ea", 0)], writes=[("Spf", s2i)])
                P.op("act", lambda e, tt=tt, s2i=s2i: e.activation(out=Spr[s2i], in_=PS(4, 0, 128), func=AF.Copy,
                                                                   scale=ea[:, tt, 8 + h:9 + h]),
                     reads=[("ps", 4), ("ea", 1)], writes=[("Spr", s2i)])
                P.op("pool", lambda e, s2i=s2i: e.tensor_tensor(out=Spb[s2i], in0=Spr[s2i], in1=M_ge_bf, op=ALU.mult),
                     reads=[("Spr", s2i)], writes=[("Spb", s2i)])

            def emit_num(lt):
                tt = lt + 2
                s2i = lt % 2
                tsl = slice(lt * 128, (lt + 1) * 128)
                nb = 6 + s2i
                NUM = PS(nb, 0, 260).rearrange("p (a b) -> p a b", a=2)
                P.op("pe", lambda e, NUM=NUM: e.matmul(
                    NUM[:, :, :], lhsT=qT[hb][:, tsl], rhs=Cst2[:, lt, :, :], start=True, stop=False),
                    reads=[("qT", hb, lt // 4), ("Cstf", 0, tt), ("Cstb", 0, tt), "Cstpad"], writes=[("ps", nb)])
                for di, (Sp, dn) in enumerate(((Spf, "f"), (Spb, "b"))):
                    P.op("pe", lambda e, Sp=Sp, di=di, NUM=NUM: e.matmul(
                        NUM[:, di, 0:129], lhsT=Sp[s2i], rhs=V1[hb][:, tt, 0:129], start=False, stop=(di == 1)),
                        reads=[("Sp" + dn, s2i), ("V1", hb, tt)], writes=[("ps", nb)])
                numk = [("ps", nb)]
                P.op("act", lambda e, NUM=NUM: e.activation(out=dab[s2i], in_=NUM[:, :, 128], func=AF.Abs),
                     reads=numk, writes=[("dab", s2i)])
                P.op("dve", lambda e: e.tensor_tensor(out=dcl[s2i], in0=dab[s2i], in1=ern[:, tt, h::8], op=ALU.max),
                     reads=[("dab", s2i), ("ern", 0), ("ern", 1)], writes=[("dcl", s2i)])
                P.op("dve", lambda e: e.reciprocal(out=sfb[s2i], in_=dcl[s2i]), reads=[("dcl", s2i)], writes=[("sfb", s2i)])
                P.op("act", lambda e, NUM=NUM: e.activation(out=hf[s2i], in_=NUM[:, 0, 0:128], func=AF.Copy, scale=sfb[s2i][:, 0:1]),
                     reads=numk + [("sfb", s2i)], writes=[("hf", s2i)])
                P.op("dve", lambda e, NUM=NUM: e.scalar_tensor_tensor(
                    out=Hh[:, lt, :], in0=NUM[:, 1, 0:128], scalar=sfb[s2i][:, 1:2], in1=hf[s2i], op0=ALU.mult, op1=ALU.add),
                    reads=numk + [("sfb", s2i), ("hf", s2i)], writes=[("Hh", lt)])
                P.op("act", lambda e: e.activation(out=sqh, in_=Hh[:, lt, :], func=AF.Square, accum_out=ssh[:, lt:lt + 1]),
                     reads=[("Hh", lt)], writes=["sqh", ("ssh", lt)])

            emit_S(0)
            yield
            for lt in range(NLT):
                if lt + 1 < NLT:
                    emit_S(lt + 1)
                emit_num(lt)
                yield
            allss = [("ssh", lt) for lt in range(NLT)]
            P.op("dve", lambda e: e.tensor_scalar(out=ssh, in0=ssh, scalar1=1.0 / 128, scalar2=EPS, op0=ALU.mult, op1=ALU.add),
                 reads=allss, writes=allss)
            P.op("pool", lambda e: e.tensor_tensor(out=rsh, in0=ssh, in1=mhalf, op=ALU.pow), reads=allss + ["mhalf"], writes=["rsh"])
            for blk in range(4):
                for q4 in range(4):
                    lt = blk * 4 + q4
                    if lt % 2 == 0:
                        P.op("act", lambda e, lt=lt: e.activation(out=hn[:, lt, :], in_=Hh[:, lt, :], func=AF.Copy,
                                                                  scale=rsh[:, lt:lt + 1]),
                             reads=[("Hh", lt), "rsh"], writes=[("hn", lt)])
                    else:
                        P.op("pool", lambda e, lt=lt: e.tensor_scalar(out=hn[:, lt, :], in0=Hh[:, lt, :], scalar1=rsh[:, lt:lt + 1],
                                                                      scalar2=0.0, op0=ALU.mult, op1=ALU.add),
                             reads=[("Hh", lt), "rsh"], writes=[("hn", lt)])
                pslot = 6 + blk % 2
                for q4 in range(4):
                    lt = blk * 4 + q4
                    P.op("pe", lambda e, lt=lt, q4=q4, pslot=pslot: e.transpose(
                        out=PSB(pslot, q4 * 128, 128), in_=hn[:, lt, :], identity=ident_bf),
                        reads=[("hn", lt), "ident_bf"], writes=[("ps", pslot)])
                P.op("dve", lambda e, blk=blk, pslot=pslot: e.scalar_tensor_tensor(
                    out=ybT[:, h, blk * 512:(blk + 1) * 512], in0=PSB(pslot, 0, 512), scalar=dp[:, GHH + h:GHH + h + 1],
                    in1=gob[hb][:, blk * 512:(blk + 1) * 512], op0=ALU.mult, op1=ALU.mult),
                    reads=[("ps", pslot), ("gob", hb, blk)], writes=[("ybT", h, blk)])
                yield
            if h == 0:
                dump("Hh0", Hh, [("Hh", lt) for lt in range(NLT)])

        def drive(gens, weights=None):
            pairs = [(g, (weights[i] if weights else 1)) for i, g in enumerate(gens) if g is not None]
            while pairs:
                for (g, wgt) in list(pairs):
                    for _ in range(wgt):
                        try:
                            next(g)
                        except StopIteration:
                            pairs.remove((g, wgt))
                            break

        OFF_BA, OFF_CA, OFF_XA, OFF_ZA, OFF_GA, OFF_GB = [3104 + 1024 * i for i in range(2, 8)]

        def load_cv_w(c):
            for j, o0 in enumerate((OFF_CA, OFF_XA, OFF_ZA, OFF_BA)):
                P.dma("pool", lambda e, c=c, j=j, o0=o0: e.dma_start(
                    out=wcv[c % 2][:, :, j * 128:(j + 1) * 128], in_=wv_in[:, :, o0 + c * 128:o0 + (c + 1) * 128]),
                    writes=[("wcv", c % 2, j)])

        wcv = [wcv0, None]
        assert PRO_LATE >= S5_HB0_END, (PRO_LATE, S5_HB0_END)
        drive([gen_proj(0)])
        P.fence_all_later(P.op("pool", lambda e: e.memset(fsc[:, 4:5], 0.0), writes=P.keys_named(late) + ["S5b"]))
        P.op("pool", lambda e: e.memset(V1[1], 1.0), writes=[("V1ones", 1)])
        P.op("pool", lambda e: e.memset(Cst2, 0.0), writes=["Cstpad"])
        for h in range(NH):
            if h == NH - 2:
                load_cv_w(0)
            nxt = None
            if h + 1 < NH:
                nxt = gen_proj(h + 1)
            if h + 2 < NH:
                load_head_w(h + 2)
            if h == NH - 1 and stop_after > 5:
                break
            drive([nxt, gen_scan(h)])
        if stop_after <= 5:
            P.barrier()
            dump("ybT", ybT, [])
            P.emit()
            return nc

        cur[0] = PH
        yaT = alloc([8, SEQ], BF16)
        assert cur[0] == PH + 32768
        wcv = [wcv0, alloc([8, 512], BF16)]
        tca = [alloc([512], F32) for _ in range(1)]
        tu = [alloc([512], F32) for _ in range(1)]
        assert cur[0] <= S5_HB0_END, (cur[0], S5_HB0_END)
        cur[0] = S5_END
        tcv = [alloc([512], F32) for _ in range(1)]
        tsz = [alloc([512], BF16) for _ in range(1)]
        tba = [alloc([512], BF16) for _ in range(1)]
        assert cur[0] <= ARENA, cur[0]
        old_keys = [("whd", hb_, j) for hb_ in range(2) for j in range(5)]
        old_keys += [(nm, 0, i) for nm in ("qT", "kT", "gob") for i in range(4)]
        old_keys += [(nm, 0, tt) for nm in ("kBf", "kBb", "V1") for tt in range(NT)]
        old_keys += [("V1ones", 0)]
        new_keys = [("yaT", c, blk) for c in range(8) for blk in range(4)] + [("wcv", 1, j) for j in range(4)]
        new_keys += [("tca", 0), ("tu", 0), ("tcv", 0), ("tsz", 0), ("tba", 0)]
        P.op("pool", lambda e: e.memset(fsc[:, 1:2], 0.0), writes=old_keys + new_keys + ["fsc1"])
        rot = [0]

        def gen_conv():
          for c in range(8):
            if c + 1 < 8:
                load_cv_w(c + 1)
            W = wcv[c % 2]
            for blk in range(4):
                tok0 = CTX + blk * 512
                i2 = 0
                banks = []
                for j in range(4):
                    b = rot[0] % 4
                    rot[0] += 1
                    banks.append(b)
                    for kc in range(8):
                        P.op("pe", lambda e, b=b, j=j, kc=kc, tok0=tok0, W=W: e.matmul(
                            PS(b, 0, 512), lhsT=W[:, kc, j * 128:(j + 1) * 128], rhs=hxT[:, kc, tok0:tok0 + 512],
                            start=(kc == 0), stop=(kc == 7)),
                            reads=[("wcv", c % 2, j)] + [("hxT", kc, t_) for t_ in range(tok0 // 128, tok0 // 128 + 4)], writes=[("ps", b)])
                bca, bxa, bza, bba = banks
                P.op("act", lambda e, c=c, i2=i2, bca=bca: e.activation(out=tca[i2], in_=PS(bca, 0, 512), func=AF.Identity,
                                                                        bias=pp[:, B_CA + c:B_CA + c + 1]),
                     reads=[("ps", bca)], writes=[("tca", i2)])
                P.op("dve", lambda e, c=c, i2=i2, bxa=bxa: e.scalar_tensor_tensor(
                    out=tu[i2], in0=PS(bxa, 0, 512), scalar=pp[:, B_XA + c:B_XA + c + 1], in1=tca[i2], op0=ALU.add, op1=ALU.mult),
                    reads=[("ps", bxa), ("tca", i2)], writes=[("tu", i2)])
                P.op("act", lambda e, c=c, i2=i2: e.activation(out=tcv[i2], in_=tu[i2], func=AF.Identity,
                                                               scale=pp[:, WCV + 8 + c:WCV + 9 + c],
                                                               bias=pp[:, BCV + c:BCV + c + 1]),
                     reads=[("tu", i2)], writes=[("tcv", i2)])
                u3 = tu[i2].rearrange("p (r w) -> p r w", w=64)
                c3 = tcv[i2].rearrange("p (r w) -> p r w", w=64)
                P.op("dve", lambda e, c=c, u3=u3, c3=c3: e.scalar_tensor_tensor(
                    out=c3[:, :, 1:64], in0=u3[:, :, 0:63], scalar=pp[:, WCV + c:WCV + c + 1], in1=c3[:, :, 1:64],
                    op0=ALU.mult, op1=ALU.add), reads=[("tu", i2), ("tcv", i2)], writes=[("tcv", i2)])
                P.op("dve", lambda e, c=c, u3=u3, c3=c3: e.scalar_tensor_tensor(
                    out=c3[:, :, 0:63], in0=u3[:, :, 1:64], scalar=pp[:, WCV + 16 + c:WCV + 17 + c], in1=c3[:, :, 0:63],
                    op0=ALU.mult, op1=ALU.add), reads=[("tu", i2), ("tcv", i2)], writes=[("tcv", i2)])
                P.op("act", lambda e, c=c, i2=i2, bza=bza: e.activation(out=tsz[i2], in_=PS(bza, 0, 512), func=AF.Silu,
                                                                        bias=pp[:, B_ZA + c:B_ZA + c + 1]),
                     reads=[("ps", bza)], writes=[("tsz", i2)])
                P.op("dve", lambda e, c=c, i2=i2, bba=bba: e.scalar_tensor_tensor(
                    out=tba[i2], in0=PS(bba, 0, 512), scalar=pp[:, B_BA + c:B_BA + c + 1], in1=tsz[i2], op0=ALU.add, op1=ALU.mult),
                    reads=[("ps", bba), ("tsz", i2)], writes=[("tba", i2)])
                P.op("pool", lambda e, c=c, i2=i2, blk=blk: e.tensor_tensor(
                    out=yaT[:, c, blk * 512:(blk + 1) * 512], in0=tba[i2], in1=tcv[i2], op=ALU.mult),
                    reads=[("tba", i2), ("tcv", i2)], writes=[("yaT", c, blk)])
                yield

        drive([gen_scan(NH - 1), gen_conv()], weights=[3, 1])
        if stop_after <= 6:
            P.barrier()
            dump("yaT", yaT, [])
            P.emit()
            return nc

        cur[0] = PH + 32768
        mgT = alloc([8, SEQ], BF16)
        wmg = [alloc([8, 512], BF16) for _ in range(2)]
        tga = [alloc([512], F32) for _ in range(2)]
        tgb = [alloc([512], F32) for _ in range(2)]
        tmA, tmB = tga, tgb
        assert cur[0] <= PH + 90112, cur[0]
        cur[0] = PH + 90112
        wo = alloc([8, 1024], BF16)
        wada2 = alloc([8, 1024], BF16)
        assert cur[0] <= ARENA, cur[0]
        S5N = {"whd", "qT", "kT", "gob", "kBf", "kBb", "V1", "V1ones", "Cstf", "Cstb", "Cf", "Cb", "Hh", "hn", "ssh", "rsh",
               "mhalf", "t_o", "t_z", "hf", "sqh", "Spf", "Spb", "Spr", "dcl", "dab", "sfb", "vtmp", "tmpg"}
        S6N = {"wcv", "tca", "tu", "tcv", "tsz", "tba"}
        P.op("pool", lambda e: e.memset(fsc[:, 2:3], 0.0),
             writes=P.keys_named(S5N) + [("wmg", i, j) for i in range(2) for j in range(4)] + ["fsc2"])
        P.op("pool", lambda e: e.memset(fsc[:, 3:4], 0.0),
             writes=P.keys_named(S5N | S6N) + [("mgT", m, blk) for m in range(8) for blk in range(4)]
             + [(nm, i) for nm in ("tga", "tgb") for i in range(2)] + ["wo", "wada2", "fsc3"])
        wv_out = w_out.rearrange("(kc p) n -> p kc n", p=128)
        wv_pa = w_pa.rearrange("(kc p) n -> p kc n", p=128)
        wv_pb = w_pb.rearrange("(kc p) n -> p kc n", p=128)

        def load_mg_w(m):
            srcs = (wv_pa[:, :, m * 128:(m + 1) * 128], wv_pb[:, :, m * 128:(m + 1) * 128],
                    wv_in[:, :, OFF_GA + m * 128:OFF_GA + (m + 1) * 128], wv_in[:, :, OFF_GB + m * 128:OFF_GB + (m + 1) * 128])
            for j, s in enumerate(srcs):
                P.dma("pool", lambda e, m=m, j=j, s=s: e.dma_start(out=wmg[m % 2][:, :, j * 128:(j + 1) * 128], in_=s),
                      writes=[("wmg", m % 2, j)])

        load_mg_w(0)
        load_mg_w(1)
        P.dma("pool", lambda e: e.dma_start(out=wo, in_=wv_out), writes=["wo"])
        P.dma("pool", lambda e: e.dma_start(out=wada2, in_=wv_ada[:, :, 2048:3072]), writes=["wada2"])
        rot = [0]
        for m in range(8):
            if 1 <= m and m + 1 < 8:
                load_mg_w(m + 1)
            W = wmg[m % 2]
            for blk in range(4):
                i2 = blk % 2
                tsl = slice(blk * 512, (blk + 1) * 512)
                tok0 = CTX + blk * 512
                banks = []
                for j in range(4):
                    b = rot[0] % 8
                    rot[0] += 1
                    banks.append(b)
                    for kc in range(8):
                        if j == 0:
                            rhs = yaT[:, kc, tsl]
                            rk = [("yaT", kc, blk)]
                        elif j == 1:
                            rhs = ybT[:, kc, tsl]
                            rk = [("ybT", kc, blk)]
                        else:
                            rhs = hxT[:, kc, tok0:tok0 + 512]
                            rk = [("hxT", kc, t_) for t_ in range(tok0 // 128, tok0 // 128 + 4)]
                        P.op("pe", lambda e, b=b, j=j, kc=kc, rhs=rhs, W=W: e.matmul(
                            PS(b, 0, 512), lhsT=W[:, kc, j * 128:(j + 1) * 128], rhs=rhs, start=(kc == 0), stop=(kc == 7)),
                            reads=[("wmg", m % 2, j)] + rk, writes=[("ps", b)])
                bpa, bpb, bga, bgb = banks
                P.op("act", lambda e, m=m, i2=i2, bga=bga: e.activation(out=tga[i2], in_=PS(bga, 0, 512), func=AF.Tanh, scale=0.5,
                                                                        bias=dp[:, BGAH + m:BGAH + m + 1]),
                     reads=[("ps", bga)], writes=[("tga", i2)])
                P.op("act", lambda e, m=m, i2=i2, bgb=bgb: e.activation(out=tgb[i2], in_=PS(bgb, 0, 512), func=AF.Tanh, scale=0.5,
                                                                        bias=dp[:, BGBH + m:BGBH + m + 1]),
                     reads=[("ps", bgb)], writes=[("tgb", i2)])
                P.op("dve", lambda e, i2=i2, bpa=bpa: e.scalar_tensor_tensor(
                    out=tmA[i2], in0=tga[i2], scalar=1.0, in1=PS(bpa, 0, 512), op0=ALU.add, op1=ALU.mult),
                    reads=[("tga", i2), ("ps", bpa)], writes=[("tga", i2)])
                P.op("dve", lambda e, i2=i2, bpb=bpb: e.scalar_tensor_tensor(
                    out=tmB[i2], in0=tgb[i2], scalar=1.0, in1=PS(bpb, 0, 512), op0=ALU.add, op1=ALU.mult),
                    reads=[("tgb", i2), ("ps", bpb)], writes=[("tgb", i2)])
                P.op("pool", lambda e, m=m, i2=i2, tsl=tsl: e.tensor_tensor(out=mgT[:, m, tsl], in0=tmA[i2], in1=tmB[i2], op=ALU.add),
                     reads=[("tga", i2), ("tgb", i2)], writes=[("mgT", m, blk)])
        P.barrier()
        dump("mgT", mgT, [])
        if stop_after <= 7:
            P.emit()
            return nc

        cur[0] = PH
        bada_g = alloc([D], F32)
        gate_bc = alloc([D], F32)
        gfin_bc = alloc([D], F32)
        sbc = alloc([8, 128], BF16)
        ones_bf = alloc([128], BF16)
        bo_row = alloc([D], BF16)
        ssf = alloc([16], F32)
        lsf = alloc([16], F32)
        rsf = alloc([16], F32)
        sqf = alloc([D], BF16)
        NB8 = 4
        xt = [alloc([D], F32) for _ in range(1)]
        xn = [alloc([D], F32) for _ in range(1)]
        assert cur[0] <= PH + 32768, cur[0]
        cur[0] = PH + 65536
        xt += [alloc([D], F32) for _ in range(3)]
        xn += [alloc([D], F32) for _ in range(3)]
        assert cur[0] <= PH + 90112, cur[0]
        P.dma("pool", lambda e: e.dma_start(out=bo_row[0:1, :], in_=b_out.rearrange("(o n) -> o n", o=1)), writes=["bo_row"])
        for (dst, src, nm) in ((bada_g, b_ada[2048:3072], "bada_g"), (gfin_bc, g_final, "gfin")):
            P.dma("sp", lambda e, dst=dst, src=src: e.dma_start(out=dst, in_=src.partition_broadcast(128)), writes=[nm])
        P.op("pool", lambda e: e.memset(ones_bf, 1.0), writes=["ones_bf"])
        P.op("pool", lambda e: e.tensor_scalar(out=bada_g, in0=bada_g, scalar1=0.5, scalar2=0.0, op0=ALU.mult, op1=ALU.add),
             reads=["bada_g"], writes=["bada_g"])
        P.op("pool", lambda e: e.tensor_scalar(out=bo_row[0:1, :], in0=bo_row[0:1, :], scalar1=2.0, scalar2=0.0, op0=ALU.mult, op1=ALU.add),
             reads=["bo_row"], writes=["bo_row"])
        for kc in range(8):
            P.op("dve", lambda e, kc=kc: e.tensor_scalar(out=sbc[:, kc, :], in0=ones_bf, scalar1=s_f[:, kc:kc + 1],
                                                         scalar2=None, op0=ALU.mult),
                 reads=["ones_bf"], writes=[("sbc", kc)])
        for nb in range(2):
            for kc in range(8):
                P.op("pe", lambda e, nb=nb, kc=kc: e.matmul(PS(6 + nb, 0, 512), lhsT=sbc[:, kc, :],
                                                            rhs=wada2[:, kc, nb * 512:(nb + 1) * 512],
                                                            start=(kc == 0), stop=(kc == 7)),
                     reads=[("sbc", kc), "wada2"], writes=[("ps", 6 + nb)])
            P.op("dve", lambda e, nb=nb: e.scalar_tensor_tensor(out=gate_bc[:, nb * 512:(nb + 1) * 512], in0=PS(6 + nb, 0, 512),
                                                                scalar=0.5, in1=bada_g[:, nb * 512:(nb + 1) * 512],
                                                                op0=ALU.mult, op1=ALU.add),
                 reads=[("ps", 6 + nb), "bada_g"], writes=[("gate_bc", nb)])
        gk = [("gate_bc", 0), ("gate_bc", 1)]
        for lt in range(NLT):
            i2 = lt % NB8
            tsl = slice(lt * 128, (lt + 1) * 128)
            P.dma("sp", lambda e, lt=lt, i2=i2: e.dma_start(out=xt[i2], in_=x[lt * 128:(lt + 1) * 128, :]), writes=[("xt", i2)])
            bank0 = (lt % 3) * 2
            for nb in range(2):
                for m in range(8):
                    P.op("pe", lambda e, nb=nb, m=m, tsl=tsl, bank0=bank0: e.matmul(
                        PS(bank0 + nb, 0, 512), lhsT=mgT[:, m, tsl], rhs=wo[:, m, nb * 512:(nb + 1) * 512],
                        start=(m == 0), stop=False), reads=["wo"], writes=[("ps", bank0 + nb)])
                P.op("pe", lambda e, nb=nb, bank0=bank0: e.matmul(
                    PS(bank0 + nb, 0, 512), lhsT=ones_bf[0:1, :], rhs=bo_row[0:1, nb * 512:(nb + 1) * 512],
                    start=False, stop=True), reads=["ones_bf", "bo_row"], writes=[("ps", bank0 + nb)])
                P.op("dve", lambda e, nb=nb, i2=i2, bank0=bank0: e.tensor_tensor(
                    out=xn[i2][:, nb * 512:(nb + 1) * 512], in0=PS(bank0 + nb, 0, 512), in1=gate_bc[:, nb * 512:(nb + 1) * 512],
                    op=ALU.mult), reads=[("ps", bank0 + nb)] + gk, writes=[("xn", i2, nb)])
            xk = [("xn", i2, 0), ("xn", i2, 1)]
            P.op("pool", lambda e, i2=i2: e.tensor_tensor(out=xn[i2], in0=xn[i2], in1=xt[i2], op=ALU.add),
                 reads=xk + [("xt", i2)], writes=xk)
            P.op("act", lambda e, i2=i2, lt=lt: e.activation(out=sqf, in_=xn[i2], func=AF.Square, accum_out=ssf[:, lt:lt + 1]),
                 reads=xk, writes=["sqf", ("ssf", lt)])
            P.op("act", lambda e, lt=lt: e.activation(out=lsf[:, lt:lt + 1], in_=ssf[:, lt:lt + 1], func=AF.Ln, scale=1.0 / D, bias=EPS),
                 reads=[("ssf", lt)], writes=[("lsf", lt)])
            P.op("act", lambda e, lt=lt: e.activation(out=rsf[:, lt:lt + 1], in_=lsf[:, lt:lt + 1], func=AF.Exp, scale=-0.5),
                 reads=[("lsf", lt)], writes=[("rsf", lt)])
            P.op("dve", lambda e, i2=i2, lt=lt: e.scalar_tensor_tensor(
                out=xn[i2], in0=xn[i2], scalar=rsf[:, lt:lt + 1], in1=gfin_bc, op0=ALU.mult, op1=ALU.mult),
                reads=xk + [("rsf", lt), "gfin"], writes=xk)
            P.dma("sp", lambda e, i2=i2, lt=lt: e.dma_start(out=y[lt * 128:(lt + 1) * 128, :], in_=xn[i2]), reads=xk)
        P.emit()
    return nc


_NC_CACHE = {}


def _core_inputs(b, x, c, ctx, c_ctx, w_ada, b_ada, g_norm, w_in, b_in, w_conv, b_conv, g_head, w_pa, w_pb, w_out,
                 b_out, g_final):
    f = lambda a: np.ascontiguousarray(a, dtype=np.float32)
    return {
        "x": f(x[b]), "ctx": f(ctx[b]), "c": f(c[b]), "c_ctx": f(c_ctx),
        "w_ada": f(w_ada[0]), "b_ada": f(b_ada[0]), "g_norm": f(g_norm[0]), "w_in": f(w_in[0]), "b_in": f(b_in[0]),
        "w_conv": f(w_conv[0]), "b_conv": f(b_conv[0]), "g_head": f(g_head[0]), "w_pa": f(w_pa[0]), "w_pb": f(w_pb[0]),
        "w_out": f(w_out[0]), "b_out": f(b_out[0]), "g_final": f(g_final),
    }


def kernel(**inputs):
    if "nc" not in _NC_CACHE:
        _NC_CACHE["nc"] = build_program()
    nc = _NC_CACHE["nc"]
    in_maps = [_core_inputs(b, **inputs) for b in range(8)]
    res = run_bass_kernel_spmd(nc, in_maps, core_ids=list(range(8)))
    return np.stack([np.asarray(r["y"], dtype=np.float32).reshape(SEQ, D) for r in res.results], axis=0)
```

```python
import contextlib
import numpy as np
import concourse.bass as bass
import concourse.mybir as mybir
from concourse.bass_utils import run_bass_kernel_spmd

F32 = mybir.dt.float32
BF16 = mybir.dt.bfloat16
AF = mybir.ActivationFunctionType
ALU = mybir.AluOpType

D = 1024
SEQ = 2048
CTX = 256
NT = 18
NLT = 16
TOK = NT * 128
NH = 8
N_IN = 11296
EPS = 1e-6
QS = 128 ** -0.5


class _Probe:
    def __init__(self):
        self.rec = None

    def __getattr__(self, name):
        def f(*a, **k):
            self.rec = (name, a, k)
            return None
        return f


def _free(ap):
    n = 1
    for v in ap.shape[1:]:
        n *= v
    return n


def _cost(eng, fn, dma):
    pr = _Probe()
    fn(pr)
    name, a, k = pr.rec
    if dma:
        out = k.get("out", a[0] if a else None)
        nbytes = _free(out) * out.shape[0] * (2 if out.dtype == BF16 else 4)
        issue = 1100.0 if eng == "pool" else 100.0
        return issue, issue + 2000.0 + nbytes / 250.0
    if eng == "pe":
        if name == "transpose":
            t = 128 / 2.4 + 3
        else:
            rhs = k.get("rhs", a[2] if len(a) > 2 else None)
            n = _free(rhs)
            t = max(n, 56) / 2.4 + 3
            if rhs.dtype == F32:
                t *= 4
        return t, t + 170.0
    out = k.get("out", a[0] if a else None)
    n = _free(out) if out is not None else 128
    if eng == "act":
        t = (170.0 + n * 1.8) if n <= 128 else (300.0 + n * 0.74)
        if k.get("accum_out") is not None:
            t += 190
    elif eng == "dve":
        t = 200.0 + n * 1.05
    else:
        if name == "tensor_tensor":
            t = 180.0 + n * 2.0
            if k.get("op") == ALU.pow:
                t += 2700
        elif name == "tensor_copy":
            t = 340.0 + n * 2.0
        else:
            t = 250.0 + n * 1.0
    return t, t + 60.0


class _Op:
    __slots__ = ("eng", "fn", "deps", "dma", "idx", "pos", "phase", "busy", "lat", "stt", "vc", "waits")


class Prog:
    ENGS = ("pe", "act", "dve", "pool", "sp")
    KDMA = 16
    SYNC = 120.0

    def __init__(self, nc, reorder=True):
        self.nc = nc
        self.ops = []
        self.last_w = {}
        self.readers = {}
        self.phase = 0
        self.reorder = reorder
        self.after = set()

    def _add(self, eng, fn, reads, writes, dma):
        norm = lambda k: k[:2] if (isinstance(k, tuple) and k[0] == "ps") else k
        reads = tuple(dict.fromkeys(norm(k) for k in reads))
        writes = tuple(dict.fromkeys(norm(k) for k in writes))
        writes = writes + tuple(k for k in reads if isinstance(k, tuple) and k[0] == "ps" and k not in writes)
        op = _Op()
        op.eng, op.fn, op.dma, op.idx, op.phase, op.pos = eng, fn, dma, len(self.ops), self.phase, None
        op.busy, op.lat = _cost(eng, fn, dma)
        deps = set()
        for k in reads:
            if k in self.last_w:
                deps.add(self.last_w[k])
        for k in writes:
            if k in self.last_w:
                deps.add(self.last_w[k])
            for r in self.readers.get(k, ()):
                deps.add(r)
        deps |= self.after
        deps.discard(op)
        op.deps = deps
        for k in reads:
            self.readers.setdefault(k, []).append(op)
        for k in writes:
            self.last_w[k] = op
            self.readers[k] = []
        self.ops.append(op)
        return op

    def op(self, eng, fn, reads=(), writes=()):
        return self._add(eng, fn, tuple(reads), tuple(writes), False)

    def dma(self, eng, fn, reads=(), writes=()):
        return self._add(eng, fn, tuple(reads), tuple(writes), True)

    def keys_named(self, names):
        ks = set(self.last_w) | set(self.readers)
        return [k for k in ks if (k if isinstance(k, str) else k[0]) in names]

    def fence_all_later(self, op):
        self.after.add(op)

    def barrier(self):
        self.after = set()
        self.phase += 1
        self.last_w = {}
        self.readers = {}

    def _schedule(self, ops):
        import heapq
        order = {e: [] for e in self.ENGS}
        if not self.reorder:
            for o in ops:
                o.stt = float(o.idx)
                order[o.eng].append(o)
            return order
        inphase = set(ops)
        succs = {o: [] for o in ops}
        indeg = {}
        for o in ops:
            ds = [d for d in o.deps if d in inphase]
            indeg[o] = len(ds)
            for d in ds:
                succs[d].append(o)
        tail = {}
        for o in reversed(ops):
            t_ = 0.0
            for su in succs[o]:
                t_ = max(t_, tail[su] + (30.0 if su.eng == o.eng else self.SYNC))
            tail[o] = o.lat + t_
        pk = lambda o: (-tail[o], o.idx)
        ready = {o: 0.0 for o in ops}
        fut = {e: [] for e in self.ENGS}
        now = {e: [] for e in self.ENGS}
        for o in ops:
            if indeg[o] == 0:
                heapq.heappush(fut[o.eng], (0.0, o.idx, o))
        free = {e: 0.0 for e in self.ENGS}
        dfin = {e: [] for e in self.ENGS}
        KD = self.KDMA
        left = len(ops)
        while left:
            best = None
            for e in self.ENGS:
                while fut[e] and fut[e][0][0] <= free[e]:
                    rt, idx, o = heapq.heappop(fut[e])
                    heapq.heappush(now[e], (pk(o), o.idx, o))
                if now[e]:
                    _, idx, o = now[e][0]
                    stt = free[e]
                elif fut[e]:
                    rt, idx, o = fut[e][0]
                    stt = max(rt, free[e])
                else:
                    continue
                if o.dma and len(dfin[e]) >= KD:
                    stt = max(stt, dfin[e][-KD])
                if best is None or (stt, pk(o)) < (best[0], pk(best[3])):
                    best = (stt, idx, e, o)
            stt, idx, e, o = best
            if now[e] and now[e][0][2] is o:
                heapq.heappop(now[e])
            else:
                heapq.heappop(fut[e])
            free[e] = stt + o.busy
            o.stt = stt
            fin = stt + o.lat
            if o.dma:
                dfin[e].append(fin)
            order[e].append(o)
            left -= 1
            for su in succs[o]:
                lat = 30.0 if (su.eng == o.eng) else self.SYNC
                ready[su] = max(ready[su], fin + lat)
                indeg[su] -= 1
                if indeg[su] == 0:
                    heapq.heappush(fut[su.eng], (ready[su], su.idx, su))
        return order

    def emit(self, final_wait_eng="sp"):
        nc = self.nc
        nph = self.phase + 1
        phases = [[] for _ in range(nph)]
        for o in self.ops:
            phases[o.phase].append(o)
        streams = {e: [] for e in self.ENGS}
        for ph in range(nph):
            od = self._schedule(phases[ph])
            for e in self.ENGS:
                streams[e].extend(od[e])
        cnt = {e: 0 for e in self.ENGS}
        dcnt = {e: 0 for e in self.ENGS}
        last_c = [dict() for _ in range(nph)]
        last_d = [dict() for _ in range(nph)]
        for e in self.ENGS:
            for o in streams[e]:
                if o.dma:
                    dcnt[e] += 1
                    o.pos = dcnt[e]
                    last_d[o.phase][e] = dcnt[e]
                else:
                    cnt[e] += 1
                    o.pos = cnt[e]
                    last_c[o.phase][e] = cnt[e]
        cum_c = [dict() for _ in range(nph + 1)]
        cum_d = [dict() for _ in range(nph + 1)]
        for ph in range(nph):
            cum_c[ph + 1] = dict(cum_c[ph]); cum_c[ph + 1].update(last_c[ph])
            cum_d[ph + 1] = dict(cum_d[ph]); cum_d[ph + 1].update(last_d[ph])
        self.cnt, self.dcnt = cnt, dcnt
        KD = self.KDMA

        def dkey(p, n):
            return ("d", p, (n - 1) % KD), 16 * ((n - 1) // KD + 1)

        know = {e: {} for e in self.ENGS}
        cur_ph = {e: 0 for e in self.ENGS}

        def need(kn, waits, key, val, vc):
            if kn.get(key, 0) >= val:
                return
            waits.append((key, val))
            kn[key] = val
            if vc:
                for k2, v2 in vc.items():
                    if kn.get(k2, 0) < v2:
                        kn[k2] = v2

        def barrier_waits(E, kn, waits, ph):
            for p, n in cum_c[ph].items():
                if not (p == E == "pe"):
                    need(kn, waits, ("c", p), n, None)
            for p, n in cum_d[ph].items():
                for m in range(max(1, n - KD + 1), n + 1):
                    k_, v_ = dkey(p, m)
                    need(kn, waits, k_, v_, None)

        for o in sorted(self.ops, key=lambda o: (o.phase, o.stt, o.idx)):
            E = o.eng
            kn = know[E]
            waits = []
            if o.phase != cur_ph[E]:
                cur_ph[E] = o.phase
                barrier_waits(E, kn, waits, o.phase)
            deps = [d for d in o.deps if d.phase == o.phase]
            deps.sort(key=lambda d: (-d.stt, d.idx))
            for d in deps:
                if d.dma:
                    k_, v_ = dkey(d.eng, d.pos)
                elif d.eng == E == "pe":
                    continue
                else:
                    k_, v_ = ("c", d.eng), d.pos
                need(kn, waits, k_, v_, d.vc)
            if o.dma and o.pos > KD:
                k_, v_ = dkey(E, o.pos - KD)
                need(kn, waits, k_, v_, None)
            o.waits = waits
            o.vc = dict(kn)
        final_waits = []
        barrier_waits(final_wait_eng, know[final_wait_eng], final_waits, nph)

        with contextlib.ExitStack() as st:
            csem = {e: st.enter_context(nc.semaphore("c_" + e)) for e in self.ENGS if cnt[e]}
            dsem = {e: [st.enter_context(nc.semaphore("d_%s%d" % (e, i))) for i in range(KD)]
                    for e in self.ENGS if dcnt[e]}
            block = st.enter_context(nc.Block())

            def body(ename, eng):
                def do_wait(key, val):
                    if key[0] == "c":
                        eng.wait_ge(csem[key[1]], val)
                    else:
                        eng.wait_ge(dsem[key[1]][key[2]], val)

                for o in streams[ename]:
                    for (key, val) in o.waits:
                        do_wait(key, val)
                    ins = o.fn(eng)
                    if o.dma:
                        ins.then_inc(dsem[ename][(o.pos - 1) % KD], 16)
                    else:
                        ins.then_inc(csem[ename], 1)
                if ename == final_wait_eng:
                    for (key, val) in final_waits:
                        do_wait(key, val)

            @block.tensor
            def _(e):
                body("pe", e)

            @block.scalar
            def _(e):
                body("act", e)

            @block.vector
            def _(e):
                body("dve", e)

            @block.gpsimd
            def _(e):
                body("pool", e)

            @block.sync
            def _(e):
                body("sp", e)


def _prod(s):
    r = 1
    for v in s:
        r *= v
    return r


def build_program(stop_after=99, dbg=None):
    nc = bass.Bass("TRN2", target_bir_lowering=False)

    def din(name, shape):
        return nc.dram_tensor(name, list(shape), F32, kind="ExternalInput").ap()

    x = din("x", [SEQ, D])
    ctx = din("ctx", [CTX, D])
    c_in = din("c", [D])
    cctx_in = din("c_ctx", [D])
    w_ada = din("w_ada", [D, 3 * D])
    b_ada = din("b_ada", [3 * D])
    g_norm = din("g_norm", [D])
    w_in = din("w_in", [D, N_IN])
    b_in = din("b_in", [N_IN])
    w_conv = din("w_conv", [3, D])
    b_conv = din("b_conv", [D])
    g_head = din("g_head", [D])
    w_pa = din("w_pa", [D, D])
    w_pb = din("w_pb", [D, D])
    w_out = din("w_out", [D, D])
    b_out = din("b_out", [D])
    g_final = din("g_final", [D])
    y = nc.dram_tensor("y", [SEQ, D], F32, kind="ExternalOutput").ap()
    dbg_out = {}
    if dbg:
        for name, shape in dbg.items():
            dbg_out[name] = nc.dram_tensor("dbg_" + name, [128, _prod(shape)], F32, kind="ExternalOutput").ap()

    ARENA = 210944
    with contextlib.ExitStack() as st:
        arena = st.enter_context(nc.sbuf_tensor("arena", [128, ARENA // 2], BF16))
        psum = st.enter_context(nc.psum_tensor("psum", [128, 4096], F32))
        P = Prog(nc)

        cur = [0]

        def V(off, shape, dt):
            n = _prod(shape)
            sz = 2 if dt == BF16 else 4
            assert off % 4 == 0 and off + n * sz <= ARENA, (off, shape)
            a = arena[:, off // 2: off // 2 + n * sz // 2]
            if dt != BF16:
                a = a.bitcast(dt)
            if len(shape) == 2:
                a = a.rearrange("p (a b) -> p a b", a=shape[0])
            elif len(shape) == 3:
                a = a.rearrange("p (a b c) -> p a b c", a=shape[0], b=shape[1])
            return a

        def alloc(shape, dt):
            n = _prod(shape) * (2 if dt == BF16 else 4)
            n = (n + 31) // 32 * 32
            off = cur[0]
            cur[0] += n
            return V(off, shape, dt)

        def PS(bank, off_f32, n_f32):
            return psum[:, bank * 512 + off_f32: bank * 512 + off_f32 + n_f32]

        def PSB(bank, off_bf, n_bf):
            return psum[:, bank * 512: (bank + 1) * 512].bitcast(BF16)[:, off_bf: off_bf + n_bf]

        ident_bf = alloc([128], BF16)
        ident_f = alloc([128], F32)
        M_le = alloc([128], F32)
        M_lt = alloc([128], F32)
        M_ge = alloc([128], F32)
        M_gt = alloc([128], F32)
        ones_f = alloc([128], F32)
        M_ge_bf = alloc([128], BF16)
        pp = alloc([128], F32)
        pp2 = alloc([40], F32)
        dp = alloc([64], F32)
        s_f = alloc([16], F32)
        s_t = alloc([16], F32)
        s2 = alloc([16], BF16)
        gts = alloc([NT, 32], F32)
        lf = alloc([NT, 16], F32)
        ea = alloc([NT, 16], F32)
        wk = alloc([NT, 16], F32)
        ern = alloc([NT, 16], F32)
        eB = alloc([NT, 16], F32)
        tg1 = alloc([NT, 16], F32)
        tg2 = alloc([NT, 16], F32)
        hxT = alloc([8, TOK], BF16)
        ybT = alloc([8, SEQ], BF16)
        bg_bc = alloc([32], F32)
        wg = alloc([8, 32], BF16)
        fsc = alloc([16], F32)
        PH = cur[0]
        assert PH % 32 == 0
        whd = [alloc([8, 640], BF16) for _ in range(2)]
        PH2 = cur[0]
        wv_in = w_in.rearrange("(kc p) n -> p kc n", p=128)

        BQ, BK, BREST, WCV, BCV, GHD = 8, 16, 24, 88, 112, 120
        B_O, B_ZB, B_BA, B_CA, B_XA, B_ZA, B_GA, B_GB = [BREST + 8 * i for i in range(8)]
        AX, BX, AC, BC, BOH, BGAH, BGBH, GHH = [8 * i for i in range(8)]

        cur[0] = PH2
        pst = alloc([128], F32)
        pst2 = alloc([128], F32)
        wada = [alloc([8, 1024], BF16) for _ in range(2)]
        modx = alloc([16], F32)
        modc = alloc([16], F32)
        PRO_LATE = cur[0]
        xst = [alloc([D], F32) for _ in range(6)]
        xs = [alloc([D], BF16) for _ in range(4)]
        sqj = alloc([D], BF16)
        ssx = alloc([NT], F32)
        rsx = alloc([NT], F32)
        lnx = alloc([NT], F32)
        assert cur[0] <= ARENA, cur[0]

        def mk_mask(t, pattern_step, cm, op, fill_in, fill_out):
            P.op("pool", lambda e: e.memset(t, fill_in), writes=[("c", id(t))])
            P.op("pool", lambda e: e.affine_select(out=t, in_=t, pattern=[[pattern_step, 128]], compare_op=op,
                                                   fill=fill_out, base=0, channel_multiplier=cm),
                 reads=[("c", id(t))], writes=[("c", id(t))])

        mk_mask(ident_f, -1, 1, ALU.not_equal, 0.0, 1.0)
        mk_mask(M_le, 1, -1, ALU.is_ge, 1.0, 0.0)
        mk_mask(M_lt, 1, -1, ALU.is_gt, 1.0, 0.0)
        mk_mask(M_ge, -1, 1, ALU.is_ge, 1.0, 0.0)
        mk_mask(M_gt, -1, 1, ALU.is_gt, 1.0, 0.0)
        P.op("pool", lambda e: e.memset(ones_f, 1.0), writes=["ones_f"])
        P.op("pool", lambda e: e.tensor_copy(out=ident_bf, in_=ident_f), reads=[("c", id(ident_f))], writes=["ident_bf"])
        P.op("pool", lambda e: e.tensor_copy(out=M_ge_bf, in_=M_ge), reads=[("c", id(M_ge))], writes=["M_ge_bf"])
        P.op("pool", lambda e: e.memset(pst2, 0.0), writes=["pst2", ("pst2", 1), ("pst2", 2), ("pst2", 3)])

        def row(ap1d, n):
            return ap1d.rearrange("(c p) -> c p", p=128)

        P.dma("sp", lambda e: e.dma_start(out=pst[0:8, :], in_=row(g_norm, 8)), writes=[("pst", 0)])
        P.dma("sp", lambda e: e.dma_start(out=pst[8:24, :], in_=row(b_in[0:2048], 16)), writes=[("pst", 1)])
        P.dma("sp", lambda e: e.dma_start(out=pst[24:88, :], in_=row(b_in[3104:3104 + 8192], 64)), writes=[("pst", 2)])
        P.dma("sp", lambda e: e.dma_start(out=pst[88:112, :], in_=w_conv.rearrange("t (c p) -> (t c) p", p=128)),
              writes=[("pst", 3)])
        P.dma("sp", lambda e: e.dma_start(out=pst[112:120, :], in_=row(b_conv, 8)), writes=[("pst", 4)])
        P.dma("sp", lambda e: e.dma_start(out=pst[120:128, :], in_=row(g_head, 8)), writes=[("pst", 5)])
        P.dma("sp", lambda e: e.dma_start(out=pst2[0:16, :], in_=row(b_ada[0:2048], 16)), reads=[], writes=["pst2"])
        P.dma("sp", lambda e: e.dma_start(out=pst2[16:24, :], in_=row(c_in, 8)), writes=[("pst2", 1)])
        P.dma("sp", lambda e: e.dma_start(out=pst2[24:32, :], in_=row(cctx_in, 8)), writes=[("pst2", 2)])
        P.dma("sp", lambda e: e.dma_start(out=pst2[32:40, :], in_=row(b_in[2048:3072], 8)), writes=[("pst2", 3)])
        for (dst, src, nm) in ((bg_bc, b_in[3072:3104], "bg"),):
            P.dma("sp", lambda e, dst=dst, src=src: e.dma_start(out=dst, in_=src.partition_broadcast(128)), writes=[nm])
        wv_ada = w_ada.rearrange("(kc p) n -> p kc n", p=128)
        for j in range(2):
            P.dma("pool", lambda e, j=j: e.dma_start(out=wada[j], in_=wv_ada[:, :, j * 1024:(j + 1) * 1024]),
                  writes=[("wada", j)])

        P.dma("pool", lambda e: e.dma_start(out=wg, in_=wv_in[:, :, 3072:3104]), writes=["wg"])

        def head_cols(h):
            return [h * 128, 1024 + h * 128, 2048 + h * 128, 3104 + h * 128, 3104 + 1024 + h * 128]

        def load_head_w(h, extra=()):
            for j, c0 in enumerate(head_cols(h)):
                P.dma("pool", lambda e, h=h, j=j, c0=c0: e.dma_start(out=whd[h % 2][:, :, j * 128:(j + 1) * 128],
                                                                     in_=wv_in[:, :, c0:c0 + 128]),
                      reads=list(extra), writes=[("whd", h % 2, j)])

        P.op("pe", lambda e: e.transpose(out=PS(0, 0, 128), in_=pst, identity=ident_f),
             reads=[("pst", i) for i in range(6)] + [("c", id(ident_f))], writes=[("ps", 0)])
        P.op("dve", lambda e: e.tensor_copy(out=pp, in_=PS(0, 0, 128)), reads=[("ps", 0)], writes=["pp"])
        P.op("pe", lambda e: e.transpose(out=PS(1, 0, 64), in_=pst2[0:64, :], identity=ident_f[0:64, 0:64]),
             reads=["pst2", ("pst2", 1), ("pst2", 2), ("pst2", 3), ("c", id(ident_f))], writes=[("ps", 1)])
        P.op("dve", lambda e: e.tensor_copy(out=pp2, in_=PS(1, 0, 40)), reads=[("ps", 1)], writes=["pp2"])
        if stop_after <= 1:
            P.emit()
            return nc
        P.op("act", lambda e: e.activation(out=s_t, in_=pp2[:, 16:32], func=AF.Exp, scale=-1.0), reads=["pp2"], writes=["s_t"])
        P.op("dve", lambda e: e.tensor_scalar(out=s_t, in0=s_t, scalar1=1.0, scalar2=None, op0=ALU.add),
             reads=["s_t"], writes=["s_t"])
        P.op("dve", lambda e: e.reciprocal(out=s_t, in_=s_t), reads=["s_t"], writes=["s_t"])
        P.op("dve", lambda e: e.tensor_tensor(out=s_f, in0=s_t, in1=pp2[:, 16:32], op=ALU.mult),
             reads=["s_t", "pp2"], writes=["s_f"])
        P.op("dve", lambda e: e.tensor_copy(out=s2, in_=s_f), reads=["s_f"], writes=["s2"])
        for j in range(2):
            for mc in range(8):
                for kc in range(8):
                    m = j * 8 + mc
                    P.op("pe", lambda e, j=j, mc=mc, kc=kc, m=m: e.matmul(
                        PS(2, 2 * m, 2), lhsT=wada[j][:, kc, mc * 128:(mc + 1) * 128],
                        rhs=s2[:, kc::8], start=(kc == 0), stop=(kc == 7)),
                        reads=[("wada", j), "s2"], writes=[("ps", 2)])
        modv = PS(2, 0, 32).rearrange("p (m n) -> p m n", n=2)
        P.op("dve", lambda e: e.tensor_tensor(out=modx, in0=modv[:, :, 0], in1=pp2[:, 0:16], op=ALU.add),
             reads=[("ps", 2), "pp2"], writes=["modx"])
        P.op("dve", lambda e: e.tensor_tensor(out=modc, in0=modv[:, :, 1], in1=pp2[:, 0:16], op=ALU.add),
             reads=[("ps", 2), "pp2"], writes=["modc"])
        for (mod, a0, b0, nm) in ((modx, AX, BX, "modx"), (modc, AC, BC, "modc")):
            P.op("dve", lambda e, mod=mod, a0=a0: e.scalar_tensor_tensor(
                out=dp[:, a0:a0 + 8], in0=mod[:, 8:16], scalar=1.0, in1=pp[:, 0:8], op0=ALU.add, op1=ALU.mult),
                reads=[nm, "pp"], writes=[("dp", a0)])
            P.op("dve", lambda e, mod=mod, b0=b0: e.tensor_copy(out=dp[:, b0:b0 + 8], in_=mod[:, 0:8]),
                 reads=[nm], writes=[("dp", b0)])
        for (dst, src) in ((BOH, B_O), (BGAH, B_GA), (BGBH, B_GB), (GHH, GHD)):
            P.op("dve", lambda e, dst=dst, src=src: e.tensor_scalar(out=dp[:, dst:dst + 8], in0=pp[:, src:src + 8],
                                                                    scalar1=0.5, scalar2=None, op0=ALU.mult),
                 reads=["pp"], writes=[("dp", dst)])
        if stop_after <= 2:
            P.emit()
            return nc
        def xsrc(tt):
            return ctx[tt * 128:(tt + 1) * 128, :] if tt < 2 else x[(tt - 2) * 128:(tt - 1) * 128, :]

        for tt in range(NT):
            xb = xst[tt % 6]
            P.dma("sp", lambda e, tt=tt, xb=xb: e.dma_start(out=xb, in_=xsrc(tt)), writes=[("xst", tt % 6), ("xld", tt)])
            P.op("act", lambda e, tt=tt, xb=xb: e.activation(out=sqj, in_=xb, func=AF.Square, accum_out=ssx[:, tt:tt + 1]),
                 reads=[("xst", tt % 6)], writes=["sqj", ("ssx", tt)])
            P.op("act", lambda e, tt=tt: e.activation(out=lnx[:, tt:tt + 1], in_=ssx[:, tt:tt + 1], func=AF.Ln,
                                                      scale=1.0 / D, bias=EPS),
                 reads=[("ssx", tt)], writes=[("lnx", tt)])
            P.op("act", lambda e, tt=tt: e.activation(out=rsx[:, tt:tt + 1], in_=lnx[:, tt:tt + 1], func=AF.Exp, scale=-0.5),
                 reads=[("lnx", tt)], writes=[("rsx", tt)])
            xsb = xs[tt % 4]
            P.op("dve", lambda e, tt=tt, xb=xb, xsb=xsb: e.tensor_scalar(out=xsb, in0=xb, scalar1=rsx[:, tt:tt + 1],
                                                                          scalar2=None, op0=ALU.mult),
                 reads=[("xst", tt % 6), ("rsx", tt)], writes=[("xs", tt % 4)])
            grp = tt // 2
            bank0 = (grp % 4) * 2
            half = tt % 2
            for kc in range(8):
                b = bank0 + (kc // 4)
                off = (kc % 4) * 256 + half * 128
                P.op("pe", lambda e, xsb=xsb, kc=kc, b=b, off=off: e.transpose(
                    out=PSB(b, off, 128), in_=xsb[:, kc * 128:(kc + 1) * 128], identity=ident_bf),
                    reads=[("xs", tt % 4), "ident_bf"], writes=[("ps", b, kc % 4, half)])
            if half == 1:
                a0, b0 = (AC, BC) if tt < 2 else (AX, BX)
                for kc in range(8):
                    b = bank0 + (kc // 4)
                    src = PSB(b, (kc % 4) * 256, 256)
                    dst = hxT[:, kc, (tt - 1) * 128:(tt + 1) * 128]
                    rd = [("ps", b, kc % 4, 0), ("ps", b, kc % 4, 1), ("dp", a0), ("dp", b0)]
                    wr = [("hxT", kc, tt - 1), ("hxT", kc, tt)]
                    if kc < 2:
                        P.op("act", lambda e, src=src, dst=dst, kc=kc, a0=a0, b0=b0: e.activation(
                            out=dst, in_=src, func=AF.Identity, scale=dp[:, a0 + kc:a0 + kc + 1],
                            bias=dp[:, b0 + kc:b0 + kc + 1]), reads=rd, writes=wr)
                    else:
                        P.op("dve", lambda e, src=src, dst=dst, kc=kc, a0=a0, b0=b0: e.tensor_scalar(
                            out=dst, in0=src, scalar1=dp[:, a0 + kc:a0 + kc + 1], scalar2=dp[:, b0 + kc:b0 + kc + 1],
                            op0=ALU.mult, op1=ALU.add), reads=rd, writes=wr)

        def dump(name, src_ap, reads):
            if name in dbg_out:
                a = src_ap
                if len(a.shape) == 3:
                    a = a.rearrange("p a b -> p (a b)")
                if a.dtype == BF16:
                    a = a.bitcast(F32)
                P.dma("sp", lambda e: e.dma_start(out=dbg_out[name], in_=a), reads=reads)

        load_head_w(0, [("xld", 9)])
        load_head_w(1, [("xld", 17)])
        if stop_after <= 3:
            P.barrier()
            dump("pp", pp, [])
            dump("dp", dp, [])
            dump("hxT", hxT, [])
            P.emit()
            return nc

        allhx = [("hxT", kc, tt) for kc in range(8) for tt in range(NT)]
        for tt in range(NT):
            b, off = (0, tt * 32) if tt < 16 else (1, (tt - 16) * 32)
            for kc in range(8):
                P.op("pe", lambda e, tt=tt, kc=kc, b=b, off=off: e.matmul(
                    PS(b, off, 32), lhsT=hxT[:, kc, tt * 128:(tt + 1) * 128], rhs=wg[:, kc, :],
                    start=(kc == 0), stop=(kc == 7)), reads=["wg", ("hxT", kc, tt)], writes=[("ps", b)])
        P.op("dve", lambda e: e.tensor_tensor(out=gts[:, 0:16, :], in0=PS(0, 0, 512).rearrange("p (a b) -> p a b", b=32),
                                              in1=bg_bc.unsqueeze(1).to_broadcast([128, 16, 32]), op=ALU.add),
             reads=[("ps", 0), "bg"], writes=[("gts", 0)])
        P.op("dve", lambda e: e.tensor_tensor(out=gts[:, 16:18, :], in0=PS(1, 0, 64).rearrange("p (a b) -> p a b", b=32),
                                              in1=bg_bc.unsqueeze(1).to_broadcast([128, 2, 32]), op=ALU.add),
             reads=[("ps", 1), "bg"], writes=[("gts", 1)])
        allg = [("gts", 0), ("gts", 1)]
        P.op("act", lambda e: e.activation(out=lf[:, :, 0:8], in_=gts[:, :, 8:16], func=AF.Exp, scale=-1.0),
             reads=allg, writes=["lf0"])
        P.op("act", lambda e: e.activation(out=lf[:, :, 8:16], in_=gts[:, :, 24:32], func=AF.Exp, scale=-1.0),
             reads=allg, writes=["lf1"])
        P.op("act", lambda e: e.activation(out=lf, in_=lf, func=AF.Ln, bias=1.0), reads=["lf0", "lf1"], writes=["lf"])
        lf2 = lf.rearrange("p a b -> p (a b)")
        for i, Mk in enumerate((M_le, M_gt, M_ge, M_lt, ones_f)):
            P.op("pe", lambda e, i=i, Mk=Mk: e.matmul(PS(2 + i, 0, 288), lhsT=Mk, rhs=lf2, start=True, stop=True),
                 reads=["lf", ("c", id(Mk)), "ones_f"], writes=[("ps", 2 + i)])

        def cs(i):
            return PS(2 + i, 0, 288).rearrange("p (a b) -> p a b", b=16)
        Pf, Sf, Pb, Sb, Tt = cs(0), cs(1), cs(2), cs(3), cs(4)
        for (half, Pm, Sm, pb_, sb_, ic) in ((0, Pf, Sf, 2, 3, 0), (1, Pb, Sb, 4, 5, 16)):
            sl = slice(half * 8, half * 8 + 8)
            P.op("dve", lambda e, sl=sl, Pm=Pm, ic=ic: e.tensor_tensor(out=tg1[:, :, sl], in0=Pm[:, :, sl],
                                                                       in1=gts[:, :, ic:ic + 8], op=ALU.add),
                 reads=[("ps", pb_)] + allg, writes=[("tg1", half)])
            P.op("act", lambda e, sl=sl: e.activation(out=ea[:, :, sl], in_=tg1[:, :, sl], func=AF.Exp),
                 reads=[("tg1", half)], writes=[("ea", half)])
            P.op("dve", lambda e, sl=sl, Sm=Sm, ic=ic: e.scalar_tensor_tensor(
                out=tg2[:, :, sl], in0=Sm[:, :, sl], scalar=-1.0, in1=gts[:, :, ic:ic + 8], op0=ALU.mult, op1=ALU.add),
                reads=[("ps", sb_)] + allg, writes=[("tg2", half)])
            P.op("act", lambda e, sl=sl: e.activation(out=wk[:, :, sl], in_=tg2[:, :, sl], func=AF.Exp),
                 reads=[("tg2", half)], writes=[("wk", half)])
            P.op("act", lambda e, sl=sl, Pm=Pm: e.activation(out=ern[:, :, sl], in_=Pm[:, :, sl], func=AF.Exp),
                 reads=[("ps", pb_)], writes=[("ern", half)])
        P.op("act", lambda e: e.activation(out=eB, in_=Tt, func=AF.Exp, scale=-1.0), reads=[("ps", 6)], writes=["eB"])
        if stop_after <= 4:
            P.barrier()
            dump("gts", gts, [])
            dump("ea", ea, [])
            dump("wk", wk, [])
            dump("ern", ern, [])
            dump("eB", eB, [])
            P.emit()
            return nc
        S4OUT = {"gts", "lf0", "lf1", "lf", "tg1", "tg2", "ea", "wk", "ern", "eB", "ps"}
        scratch = {"pst", "pst2", "wada", "modx", "modc"}
        late = {"xst", "xs", "sqj", "ssx", "lnx", "rsx", "xld"}
        allk = set(P.last_w) | set(P.readers)
        rd = [k for k in allk if (k if isinstance(k, str) else k[0]) not in (S4OUT | scratch | late | {"whd", "wg", "hxT"})]
        P.fence_all_later(P.op("pool", lambda e: e.memset(fsc[:, 0:1], 0.0), reads=rd, writes=P.keys_named(scratch) + ["S5"]))

        cur[0] = PH2
        qT, kT, gob, kBf, kBb, V1, Cstf, Cstb = [], [], [], [], [], [], [], []
        for _hb in range(2):
            qT.append(alloc([SEQ], BF16))
            kT.append(alloc([TOK], BF16))
            gob.append(alloc([SEQ], BF16))
            _kB = alloc([NT, 2, 128], BF16)
            kBf.append(_kB[:, :, 0, :])
            kBb.append(_kB[:, :, 1, :])
            kBB = (kBB if _hb else []) + [_kB]
            V1.append(alloc([NT, 130], BF16))
            if _hb == 0:
                S5_HB0_END = cur[0]
        Cst2 = alloc([NLT, 2, 130], BF16)
        _cf, _cb = Cst2[:, :, 0, :], Cst2[:, :, 1, :]
        Cstf, Cstb = [_cf, _cf], [_cb, _cb]
        Cf = [alloc([130], F32) for _ in range(2)]
        Cb = [alloc([130], F32) for _ in range(2)]
        Hh = alloc([NLT, 128], F32)
        hn = alloc([NLT, 128], BF16)
        ssh = alloc([NLT], F32)
        rsh = alloc([NLT], F32)
        mhalf = alloc([NLT], F32)
        t_o = [alloc([512], BF16) for _ in range(2)]
        t_z = [alloc([512], BF16) for _ in range(2)]
        hf = [alloc([128], F32) for _ in range(2)]
        sqh = alloc([128], BF16)
        Spf = [alloc([128], BF16) for _ in range(2)]
        Spb = [alloc([128], BF16) for _ in range(2)]
        Spr = [alloc([128], BF16) for _ in range(2)]
        dcl = [alloc([2], F32) for _ in range(2)]
        dab = [alloc([2], F32) for _ in range(2)]
        sfb = [alloc([2], F32) for _ in range(2)]
        vtmp = [alloc([512], BF16) for _ in range(2)]
        tmpg = [alloc([512], BF16) for _ in range(2)]
        wcv0 = alloc([8, 512], BF16)
        assert cur[0] <= ARENA, cur[0]
        S5_END = cur[0]

        P.op("pool", lambda e: e.memset(mhalf, -0.5), writes=["mhalf"])
        P.op("pool", lambda e: e.memset(V1[0], 1.0), writes=[("V1ones", 0)])
        fm_rot = [0]
        tm_rot = [0]

        def gen_proj(h):
            hb = h % 2
            W = whd[hb]
            wkey = lambda j: ("whd", hb, j)
            jmap = {"q": 0, "k": 1, "v": 2, "o": 3, "zb": 4}
            blocks = [(CTX + bi * 512, 512, bi) for bi in range(4)] + [(0, 256, 4)]
            for (tok0, ntk, bi) in blocks:
                fams = ("q", "k", "v", "o", "zb") if bi < 4 else ("k", "v")
                lsl = slice(bi * 512, (bi + 1) * 512)
                gsl = slice(tok0, tok0 + ntk)
                vt = vtmp[bi % 2]
                for fam in fams:
                    j = jmap[fam]
                    b = fm_rot[0] % 4
                    fm_rot[0] += 1
                    for kc in range(8):
                        P.op("pe", lambda e, ntk=ntk, gsl=gsl, lsl=lsl, bi=bi, vt=vt, b=b, j=j, kc=kc: e.matmul(
                            PS(b, 0, ntk), lhsT=W[:, kc, j * 128:(j + 1) * 128], rhs=hxT[:, kc, gsl],
                            start=(kc == 0), stop=(kc == 7)),
                            reads=[wkey(j)] + [("hxT", kc, t_) for t_ in range(tok0 // 128, (tok0 + ntk) // 128)], writes=[("ps", b)])
                    if fam == "q":
                        P.op("dve", lambda e, ntk=ntk, gsl=gsl, lsl=lsl, bi=bi, vt=vt, b=b: e.tensor_scalar(
                            out=qT[hb][:, lsl], in0=PS(b, 0, ntk), scalar1=pp[:, BQ + h:BQ + h + 1], scalar2=QS,
                            op0=ALU.add, op1=ALU.mult), reads=[("ps", b)], writes=[("qT", hb, bi)])
                    elif fam == "k":
                        P.op("act", lambda e, ntk=ntk, gsl=gsl, lsl=lsl, bi=bi, vt=vt, b=b: e.activation(
                            out=kT[hb][:, gsl], in_=PS(b, 0, ntk), func=AF.Identity, bias=pp[:, BK + h:BK + h + 1]),
                            reads=[("ps", b)], writes=[("kT", hb, bi)])
                    elif fam == "v":
                        P.op("dve", lambda e, ntk=ntk, gsl=gsl, lsl=lsl, bi=bi, vt=vt, b=b: e.tensor_scalar(
                            out=vt[:, 0:ntk], in0=PS(b, 0, ntk), scalar1=pp2[:, 32 + h:33 + h], scalar2=None, op0=ALU.add),
                            reads=[("ps", b)], writes=[("vtmp", bi % 2)])
                    elif fam == "o":
                        P.op("act", lambda e, ntk=ntk, gsl=gsl, lsl=lsl, bi=bi, vt=vt, b=b: e.activation(
                            out=t_o[bi % 2], in_=PS(b, 0, ntk), func=AF.Tanh, scale=0.5,
                            bias=dp[:, BOH + h:BOH + h + 1]), reads=[("ps", b)], writes=[("t_o", bi % 2)])
                    else:
                        P.op("act", lambda e, ntk=ntk, gsl=gsl, lsl=lsl, bi=bi, vt=vt, b=b: e.activation(
                            out=t_z[bi % 2], in_=PS(b, 0, ntk), func=AF.Silu, bias=pp[:, B_ZB + h:B_ZB + h + 1]),
                            reads=[("ps", b)], writes=[("t_z", bi % 2)])
                        P.op("pool", lambda e, ntk=ntk, gsl=gsl, lsl=lsl, bi=bi, vt=vt: e.tensor_tensor(out=tmpg[bi % 2], in0=t_o[bi % 2], in1=t_z[bi % 2], op=ALU.mult),
                             reads=[("t_o", bi % 2), ("t_z", bi % 2)], writes=[("tmpg", bi % 2)])
                        P.op("pool", lambda e, ntk=ntk, gsl=gsl, lsl=lsl, bi=bi, vt=vt: e.tensor_tensor(out=gob[hb][:, lsl], in0=tmpg[bi % 2], in1=t_z[bi % 2], op=ALU.add),
                             reads=[("tmpg", bi % 2), ("t_z", bi % 2)], writes=[("gob", hb, bi)])
                    yield
                for ti in range(ntk // 128):
                    tt = (tok0 // 128 + ti)
                    b = fm_rot[0] % 4
                    fm_rot[0] += 1
                    P.op("pe", lambda e, ntk=ntk, gsl=gsl, lsl=lsl, bi=bi, vt=vt, b=b, tt=tt: e.transpose(out=PSB(b, 0, 128), in_=kT[hb][:, tt * 128:(tt + 1) * 128],
                                                                 identity=ident_bf),
                         reads=[("kT", hb, bi), "ident_bf"], writes=[("ps", b)])
                    P.op("pe", lambda e, ntk=ntk, gsl=gsl, lsl=lsl, bi=bi, vt=vt, b=b, ti=ti: e.transpose(out=PSB(b, 128, 128), in_=vt[:, ti * 128:(ti + 1) * 128],
                                                                 identity=ident_bf),
                         reads=[("vtmp", bi % 2), "ident_bf"], writes=[("ps", b)])
                    P.op("dve", lambda e, ntk=ntk, gsl=gsl, lsl=lsl, bi=bi, vt=vt, tt=tt, b=b: e.tensor_tensor(
                        out=kBB[hb][:, tt, :, :], in0=PSB(b, 0, 128).unsqueeze(1).to_broadcast([128, 2, 128]),
                        in1=wk[:, tt, h::8].unsqueeze(2).to_broadcast([128, 2, 128]), op=ALU.mult),
                        reads=[("ps", b), ("wk", 0), ("wk", 1)], writes=[("kBf", hb, tt), ("kBb", hb, tt)])
                    P.op("act", lambda e, ntk=ntk, gsl=gsl, lsl=lsl, bi=bi, vt=vt, tt=tt, b=b: e.activation(
                        out=V1[hb][:, tt, 0:128], in_=PSB(b, 128, 128), func=AF.Copy),
                        reads=[("ps", b), ("V1ones", hb)], writes=[("V1", hb, tt)])
                    if ti % 2 == 1:
                        yield

        def gen_scan(h):
            hb = h % 2
            P.op("pool", lambda e: e.memset(Cf[0], 0.0), writes=[("Cf", 0)])
            P.op("pool", lambda e: e.memset(Cb[0], 0.0), writes=[("Cb", 0)])
            f_order = list(range(0, 17))
            b_order = [1, 0] + list(range(17, 2, -1))
            for step in range(17):
                for (dirn, order, kB, Cs, Cst, col0, bank) in (("f", f_order, kBf, Cf, Cstf, 0, 4), ("b", b_order, kBb, Cb, Cstb, 8, 5)):
                    tt = order[step]
                    src, dst = Cs[step % 2], Cs[(step + 1) % 2]
                    ck = "C" + dirn
                    P.op("pe", lambda e, tt=tt, kB=kB, bank=bank: e.matmul(
                        PS(bank, 0, 129), lhsT=kB[hb][:, tt, :], rhs=V1[hb][:, tt, 0:129], start=True, stop=True),
                        reads=[("kB" + dirn, hb, tt), ("V1", hb, tt)], writes=[("ps", bank)])
                    P.op("dve", lambda e, tt=tt, src=src, dst=dst, bank=bank, col0=col0: e.scalar_tensor_tensor(
                        out=dst[:, 0:129], in0=src[:, 0:129], scalar=eB[:, tt, col0 + h:col0 + h + 1],
                        in1=PS(bank, 0, 129), op0=ALU.mult, op1=ALU.add),
                        reads=[(ck, step % 2), ("ps", bank), "eB"], writes=[(ck, (step + 1) % 2)])
                    if dirn == "f":
                        nxt = tt + 1
                    else:
                        nxt = 17 if step == 1 else (tt - 1 if step >= 2 else None)
                    if nxt is not None and nxt >= 2:
                        P.op("pool", lambda e, dst=dst, Cst=Cst, nxt=nxt: e.tensor_copy(out=Cst[hb][:, nxt - 2, 0:129], in_=dst[:, 0:129]),
                             reads=[(ck, (step + 1) % 2), "Cstpad"], writes=[("Cst" + dirn, 0, nxt)])
                yield

            def emit_S(lt):
                tt = lt + 2
                s2i = lt % 2
                tsl = slice(lt * 128, (lt + 1) * 128)
                for bank in (4 + s2i,):
                    P.op("pe", lambda e, tsl=tsl, bank=bank, tt=tt: e.matmul(PS(bank, 0, 128), lhsT=kT[hb][:, tt * 128:(tt + 1) * 128],
                                                                             rhs=qT[hb][:, tsl], start=True, stop=True),
                         reads=[("kT", hb, lt // 4), ("qT", hb, lt // 4)], writes=[("ps", bank)])
                P.op("dve", lambda e, tt=tt, s2i=s2i: e.scalar_tensor_tensor(
                    out=Spf[s2i], in0=PS(4 + s2i, 0, 128), scalar=ea[:, tt, h:h + 1], in1=M_le, op0=ALU.mult, op1=ALU.mult),
                    reads=[("ps", 4 + s2i), ("ea", 0)], writes=[("Spf", s2i)])
                P.op("act", lambda e, tt=tt, s2i=s2i: e.activation(out=Spr[s2i], in_=PS(4 + s2i, 0, 128), func=AF.Copy,
                                                                   scale=ea[:, tt, 8 + h:9 + h]),
                     reads=[("ps", 4 + s2i), ("ea", 1)], writes=[("Spr", s2i)])
                P.op("pool", lambda e, s2i=s2i: e.tensor_tensor(out=Spb[s2i], in0=Spr[s2i], in1=M_ge_bf, op=ALU.mult),
                     reads=[("Spr", s2i)], writes=[("Spb", s2i)])

            def emit_num(lt):
                tt = lt + 2
                s2i = lt % 2
                tsl = slice(lt * 128, (lt + 1) * 128)
                nb = 6 + s2i
                NUM = PS(nb, 0, 260).rearrange("p (a b) -> p a b", a=2)
                P.op("pe", lambda e, NUM=NUM: e.matmul(
                    NUM[:, :, :], lhsT=qT[hb][:, tsl], rhs=Cst2[:, lt, :, :], start=True, stop=False),
                    reads=[("qT", hb, lt // 4), ("Cstf", 0, tt), ("Cstb", 0, tt), "Cstpad"], writes=[("ps", nb)])
                for di, (Sp, dn) in enumerate(((Spf, "f"), (Spb, "b"))):
                    P.op("pe", lambda e, Sp=Sp, di=di, NUM=NUM: e.matmul(
                        NUM[:, di, 0:129], lhsT=Sp[s2i], rhs=V1[hb][:, tt, 0:129], start=False, stop=(di == 1)),
                        reads=[("Sp" + dn, s2i), ("V1", hb, tt)], writes=[("ps", nb)])
                numk = [("ps", nb)]
                P.op("act", lambda e, NUM=NUM: e.activation(out=dab[s2i], in_=NUM[:, :, 128], func=AF.Abs),
                     reads=numk, writes=[("dab", s2i)])
                P.op("dve", lambda e: e.tensor_tensor(out=dcl[s2i], in0=dab[s2i], in1=ern[:, tt, h::8], op=ALU.max),
                     reads=[("dab", s2i), ("ern", 0), ("ern", 1)], writes=[("dcl", s2i)])
                P.op("dve", lambda e: e.reciprocal(out=sfb[s2i], in_=dcl[s2i]), reads=[("dcl", s2i)], writes=[("sfb", s2i)])
                P.op("act", lambda e, NUM=NUM: e.activation(out=hf[s2i], in_=NUM[:, 0, 0:128], func=AF.Copy, scale=sfb[s2i][:, 0:1]),
                     reads=numk + [("sfb", s2i)], writes=[("hf", s2i)])
                P.op("dve", lambda e, NUM=NUM: e.scalar_tensor_tensor(
                    out=Hh[:, lt, :], in0=NUM[:, 1, 0:128], scalar=sfb[s2i][:, 1:2], in1=hf[s2i], op0=ALU.mult, op1=ALU.add),
                    reads=numk + [("sfb", s2i), ("hf", s2i)], writes=[("Hh", lt)])
                P.op("act", lambda e: e.activation(out=sqh, in_=Hh[:, lt, :], func=AF.Square, accum_out=ssh[:, lt:lt + 1]),
                     reads=[("Hh", lt)], writes=["sqh", ("ssh", lt)])

            emit_S(0)
            yield
            for lt in range(NLT):
                if lt + 1 < NLT:
                    emit_S(lt + 1)
                emit_num(lt)
                yield
            allss = [("ssh", lt) for lt in range(NLT)]
            P.op("dve", lambda e: e.tensor_scalar(out=ssh, in0=ssh, scalar1=1.0 / 128, scalar2=EPS, op0=ALU.mult, op1=ALU.add),
                 reads=allss, writes=allss)
            P.op("pool", lambda e: e.tensor_tensor(out=rsh, in0=ssh, in1=mhalf, op=ALU.pow), reads=allss + ["mhalf"], writes=["rsh"])
            for blk in range(4):
                for q4 in range(4):
                    lt = blk * 4 + q4
                    if lt % 2 == 0:
                        P.op("act", lambda e, lt=lt: e.activation(out=hn[:, lt, :], in_=Hh[:, lt, :], func=AF.Copy,
                                                                  scale=rsh[:, lt:lt + 1]),
                             reads=[("Hh", lt), "rsh"], writes=[("hn", lt)])
                    else:
                        P.op("pool", lambda e, lt=lt: e.tensor_scalar(out=hn[:, lt, :], in0=Hh[:, lt, :], scalar1=rsh[:, lt:lt + 1],
                                                                      scalar2=0.0, op0=ALU.mult, op1=ALU.add),
                             reads=[("Hh", lt), "rsh"], writes=[("hn", lt)])
                pslot = 6 + blk % 2
                for q4 in range(4):
                    lt = blk * 4 + q4
                    P.op("pe", lambda e, lt=lt, q4=q4, pslot=pslot: e.transpose(
                        out=PSB(pslot, q4 * 128, 128), in_=hn[:, lt, :], identity=ident_bf),
                        reads=[("hn", lt), "ident_bf"], writes=[("ps", pslot)])
                P.op("dve", lambda e, blk=blk, pslot=pslot: e.scalar_tensor_tensor(
                    out=ybT[:, h, blk * 512:(blk + 1) * 512], in0=PSB(pslot, 0, 512), scalar=dp[:, GHH + h:GHH + h + 1],
                    in1=gob[hb][:, blk * 512:(blk + 1) * 512], op0=ALU.mult, op1=ALU.mult),
                    reads=[("ps", pslot), ("gob", hb, blk)], writes=[("ybT", h, blk)])
                yield
            if h == 0:
                dump("Hh0", Hh, [("Hh", lt) for lt in range(NLT)])

        def drive(gens, weights=None):
            pairs = [(g, (weights[i] if weights else 1)) for i, g in enumerate(gens) if g is not None]
            while pairs:
                for (g, wgt) in list(pairs):
                    for _ in range(wgt):
                        try:
                            next(g)
                        except StopIteration:
                            pairs.remove((g, wgt))
                            break

        OFF_BA, OFF_CA, OFF_XA, OFF_ZA, OFF_GA, OFF_GB = [3104 + 1024 * i for i in range(2, 8)]

        def load_cv_w(c):
            for j, o0 in enumerate((OFF_CA, OFF_XA, OFF_ZA, OFF_BA)):
                P.dma("pool", lambda e, c=c, j=j, o0=o0: e.dma_start(
                    out=wcv[c % 2][:, :, j * 128:(j + 1) * 128], in_=wv_in[:, :, o0 + c * 128:o0 + (c + 1) * 128]),
                    writes=[("wcv", c % 2, j)])

        wcv = [wcv0, None]
        assert PRO_LATE >= S5_HB0_END, (PRO_LATE, S5_HB0_END)
        drive([gen_proj(0)])
        P.fence_all_later(P.op("pool", lambda e: e.memset(fsc[:, 4:5], 0.0), writes=P.keys_named(late) + ["S5b"]))
        P.op("pool", lambda e: e.memset(V1[1], 1.0), writes=[("V1ones", 1)])
        P.op("pool", lambda e: e.memset(Cst2, 0.0), writes=["Cstpad"])
        for h in range(NH):
            if h == NH - 2:
                load_cv_w(0)
            nxt = None
            if h + 1 < NH:
                nxt = gen_proj(h + 1)
            if h + 2 < NH:
                load_head_w(h + 2)
            if h == NH - 1 and stop_after > 5:
                break
            drive([nxt, gen_scan(h)])
        if stop_after <= 5:
            P.barrier()
            dump("ybT", ybT, [])
            P.emit()
            return nc

        cur[0] = PH
        yaT = alloc([8, SEQ], BF16)
        assert cur[0] == PH + 32768
        wcv = [wcv0, alloc([8, 512], BF16)]
        tca = [alloc([512], F32) for _ in range(1)]
        tu = [alloc([512], F32) for _ in range(1)]
        assert cur[0] <= S5_HB0_END, (cur[0], S5_HB0_END)
        cur[0] = S5_END
        tcv = [alloc([512], F32) for _ in range(1)]
        tsz = [alloc([512], BF16) for _ in range(1)]
        tba = [alloc([512], BF16) for _ in range(1)]
        assert cur[0] <= ARENA, cur[0]
        old_keys = [("whd", hb_, j) for hb_ in range(2) for j in range(5)]
        old_keys += [(nm, 0, i) for nm in ("qT", "kT", "gob") for i in range(4)]
        old_keys += [(nm, 0, tt) for nm in ("kBf", "kBb", "V1") for tt in range(NT)]
        old_keys += [("V1ones", 0)]
        new_keys = [("yaT", c, blk) for c in range(8) for blk in range(4)] + [("wcv", 1, j) for j in range(4)]
        new_keys += [("tca", 0), ("tu", 0), ("tcv", 0), ("tsz", 0), ("tba", 0)]
        P.op("pool", lambda e: e.memset(fsc[:, 1:2], 0.0), writes=old_keys + new_keys + ["fsc1"])
        rot = [0]

        def gen_conv():
          for c in range(8):
            if c + 1 < 8:
                load_cv_w(c + 1)
            W = wcv[c % 2]
            for blk in range(4):
                tok0 = CTX + blk * 512
                i2 = 0
                banks = []
                for j in range(4):
                    b = rot[0] % 4
                    rot[0] += 1
                    banks.append(b)
                    for kc in range(8):
                        P.op("pe", lambda e, b=b, j=j, kc=kc, tok0=tok0, W=W: e.matmul(
                            PS(b, 0, 512), lhsT=W[:, kc, j * 128:(j + 1) * 128], rhs=hxT[:, kc, tok0:tok0 + 512],
                            start=(kc == 0), stop=(kc == 7)),
                            reads=[("wcv", c % 2, j)] + [("hxT", kc, t_) for t_ in range(tok0 // 128, tok0 // 128 + 4)], writes=[("ps", b)])
                bca, bxa, bza, bba = banks
                P.op("act", lambda e, c=c, i2=i2, bca=bca: e.activation(out=tca[i2], in_=PS(bca, 0, 512), func=AF.Identity,
                                                                        bias=pp[:, B_CA + c:B_CA + c + 1]),
                     reads=[("ps", bca)], writes=[("tca", i2)])
                P.op("dve", lambda e, c=c, i2=i2, bxa=bxa: e.scalar_tensor_tensor(
                    out=tu[i2], in0=PS(bxa, 0, 512), scalar=pp[:, B_XA + c:B_XA + c + 1], in1=tca[i2], op0=ALU.add, op1=ALU.mult),
                    reads=[("ps", bxa), ("tca", i2)], writes=[("tu", i2)])
                P.op("act", lambda e, c=c, i2=i2: e.activation(out=tcv[i2], in_=tu[i2], func=AF.Identity,
                                                               scale=pp[:, WCV + 8 + c:WCV + 9 + c],
                                                               bias=pp[:, BCV + c:BCV + c + 1]),
                     reads=[("tu", i2)], writes=[("tcv", i2)])
                u3 = tu[i2].rearrange("p (r w) -> p r w", w=64)
                c3 = tcv[i2].rearrange("p (r w) -> p r w", w=64)
                P.op("dve", lambda e, c=c, u3=u3, c3=c3: e.scalar_tensor_tensor(
                    out=c3[:, :, 1:64], in0=u3[:, :, 0:63], scalar=pp[:, WCV + c:WCV + c + 1], in1=c3[:, :, 1:64],
                    op0=ALU.mult, op1=ALU.add), reads=[("tu", i2), ("tcv", i2)], writes=[("tcv", i2)])
                P.op("dve", lambda e, c=c, u3=u3, c3=c3: e.scalar_tensor_tensor(
                    out=c3[:, :, 0:63], in0=u3[:, :, 1:64], scalar=pp[:, WCV + 16 + c:WCV + 17 + c], in1=c3[:, :, 0:63],
                    op0=ALU.mult, op1=ALU.add), reads=[("tu", i2), ("tcv", i2)], writes=[("tcv", i2)])
                P.op("act", lambda e, c=c, i2=i2, bza=bza: e.activation(out=tsz[i2], in_=PS(bza, 0, 512), func=AF.Silu,
                                                                        bias=pp[:, B_ZA + c:B_ZA + c + 1]),
                     reads=[("ps", bza)], writes=[("tsz", i2)])
                P.op("dve", lambda e, c=c, i2=i2, bba=bba: e.scalar_tensor_tensor(
                    out=tba[i2], in0=PS(bba, 0, 512), scalar=pp[:, B_BA + c:B_BA + c + 1], in1=tsz[i2], op0=ALU.add, op1=ALU.mult),
                    reads=[("ps", bba), ("tsz", i2)], writes=[("tba", i2)])
                P.op("pool", lambda e, c=c, i2=i2, blk=blk: e.tensor_tensor(
                    out=yaT[:, c, blk * 512:(blk + 1) * 512], in0=tba[i2], in1=tcv[i2], op=ALU.mult),
                    reads=[("tba", i2), ("tcv", i2)], writes=[("yaT", c, blk)])
                yield

        drive([gen_scan(NH - 1), gen_conv()], weights=[3, 1])
        if stop_after <= 6:
            P.barrier()
            dump("yaT", yaT, [])
            P.emit()
            return nc

        cur[0] = PH + 32768
        mgT = alloc([8, SEQ], BF16)
        wmg = [alloc([8, 512], BF16) for _ in range(2)]
        tga = [alloc([512], F32) for _ in range(2)]
        tgb = [alloc([512], F32) for _ in range(2)]
        tmA, tmB = tga, tgb
        assert cur[0] <= PH + 90112, cur[0]
        cur[0] = PH + 90112
        wo = alloc([8, 1024], BF16)
        wada2 = alloc([8, 1024], BF16)
        assert cur[0] <= ARENA, cur[0]
        S5N = {"whd", "qT", "kT", "gob", "kBf", "kBb", "V1", "V1ones", "Cstf", "Cstb", "Cf", "Cb", "Hh", "hn", "ssh", "rsh",
               "mhalf", "t_o", "t_z", "hf", "sqh", "Spf", "Spb", "Spr", "dcl", "dab", "sfb", "vtmp", "tmpg"}
        S6N = {"wcv", "tca", "tu", "tcv", "tsz", "tba"}
        P.op("pool", lambda e: e.memset(fsc[:, 2:3], 0.0),
             writes=P.keys_named(S5N) + [("wmg", i, j) for i in range(2) for j in range(4)] + ["fsc2"])
        P.op("pool", lambda e: e.memset(fsc[:, 3:4], 0.0),
             writes=P.keys_named(S5N | S6N) + [("mgT", m, blk) for m in range(8) for blk in range(4)]
             + [(nm, i) for nm in ("tga", "tgb") for i in range(2)] + ["wo", "wada2", "fsc3"])
        wv_out = w_out.rearrange("(kc p) n -> p kc n", p=128)
        wv_pa = w_pa.rearrange("(kc p) n -> p kc n", p=128)
        wv_pb = w_pb.rearrange("(kc p) n -> p kc n", p=128)

        def load_mg_w(m):
            srcs = (wv_pa[:, :, m * 128:(m + 1) * 128], wv_pb[:, :, m * 128:(m + 1) * 128],
                    wv_in[:, :, OFF_GA + m * 128:OFF_GA + (m + 1) * 128], wv_in[:, :, OFF_GB + m * 128:OFF_GB + (m + 1) * 128])
            for j, s in enumerate(srcs):
                P.dma("pool", lambda e, m=m, j=j, s=s: e.dma_start(out=wmg[m % 2][:, :, j * 128:(j + 1) * 128], in_=s),
                      writes=[("wmg", m % 2, j)])

        load_mg_w(0)
        load_mg_w(1)
        P.dma("pool", lambda e: e.dma_start(out=wo, in_=wv_out), writes=["wo"])
        P.dma("pool", lambda e: e.dma_start(out=wada2, in_=wv_ada[:, :, 2048:3072]), writes=["wada2"])
        rot = [0]
        for m in range(8):
            if 1 <= m and m + 1 < 8:
                load_mg_w(m + 1)
            W = wmg[m % 2]
            for blk in range(4):
                i2 = blk % 2
                tsl = slice(blk * 512, (blk + 1) * 512)
                tok0 = CTX + blk * 512
                banks = []
                for j in range(4):
                    b = rot[0] % 8
                    rot[0] += 1
                    banks.append(b)
                    for kc in range(8):
                        if j == 0:
                            rhs = yaT[:, kc, tsl]
                            rk = [("yaT", kc, blk)]
                        elif j == 1:
                            rhs = ybT[:, kc, tsl]
                            rk = [("ybT", kc, blk)]
                        else:
                            rhs = hxT[:, kc, tok0:tok0 + 512]
                            rk = [("hxT", kc, t_) for t_ in range(tok0 // 128, tok0 // 128 + 4)]
                        P.op("pe", lambda e, b=b, j=j, kc=kc, rhs=rhs, W=W: e.matmul(
                            PS(b, 0, 512), lhsT=W[:, kc, j * 128:(j + 1) * 128], rhs=rhs, start=(kc == 0), stop=(kc == 7)),
                            reads=[("wmg", m % 2, j)] + rk, writes=[("ps", b)])
                bpa, bpb, bga, bgb = banks
                P.op("act", lambda e, m=m, i2=i2, bga=bga: e.activation(out=tga[i2], in_=PS(bga, 0, 512), func=AF.Tanh, scale=0.5,
                                                                        bias=dp[:, BGAH + m:BGAH + m + 1]),
                     reads=[("ps", bga)], writes=[("tga", i2)])
                P.op("act", lambda e, m=m, i2=i2, bgb=bgb: e.activation(out=tgb[i2], in_=PS(bgb, 0, 512), func=AF.Tanh, scale=0.5,
                                                                        bias=dp[:, BGBH + m:BGBH + m + 1]),
                     reads=[("ps", bgb)], writes=[("tgb", i2)])
                P.op("dve", lambda e, i2=i2, bpa=bpa: e.scalar_tensor_tensor(
                    out=tmA[i2], in0=tga[i2], scalar=1.0, in1=PS(bpa, 0, 512), op0=ALU.add, op1=ALU.mult),
                    reads=[("tga", i2), ("ps", bpa)], writes=[("tga", i2)])
                P.op("dve", lambda e, i2=i2, bpb=bpb: e.scalar_tensor_tensor(
                    out=tmB[i2], in0=tgb[i2], scalar=1.0, in1=PS(bpb, 0, 512), op0=ALU.add, op1=ALU.mult),
                    reads=[("tgb", i2), ("ps", bpb)], writes=[("tgb", i2)])
                P.op("pool", lambda e, m=m, i2=i2, tsl=tsl: e.tensor_tensor(out=mgT[:, m, tsl], in0=tmA[i2], in1=tmB[i2], op=ALU.add),
                     reads=[("tga", i2), ("tgb", i2)], writes=[("mgT", m, blk)])
        P.barrier()
        dump("mgT", mgT, [])
        if stop_after <= 7:
            P.emit()
            return nc

        cur[0] = PH
        bada_g = alloc([D], F32)
        gate_bc = alloc([D], F32)
        gfin_bc = alloc([D], F32)
        sbc = alloc([8, 128], BF16)
        ones_bf = alloc([128], BF16)
        bo_row = alloc([D], BF16)
        ssf = alloc([16], F32)
        lsf = alloc([16], F32)
        rsf = alloc([16], F32)
        sqf = alloc([D], BF16)
        NB8 = 4
        xt = [alloc([D], F32) for _ in range(1)]
        xn = [alloc([D], F32) for _ in range(1)]
        assert cur[0] <= PH + 32768, cur[0]
        cur[0] = PH + 65536
        xt += [alloc([D], F32) for _ in range(3)]
        xn += [alloc([D], F32) for _ in range(3)]
        assert cur[0] <= PH + 90112, cur[0]
        P.dma("pool", lambda e: e.dma_start(out=bo_row[0:1, :], in_=b_out.rearrange("(o n) -> o n", o=1)), writes=["bo_row"])
        for (dst, src, nm) in ((bada_g, b_ada[2048:3072], "bada_g"), (gfin_bc, g_final, "gfin")):
            P.dma("sp", lambda e, dst=dst, src=src: e.dma_start(out=dst, in_=src.partition_broadcast(128)), writes=[nm])
        P.op("pool", lambda e: e.memset(ones_bf, 1.0), writes=["ones_bf"])
        P.op("pool", lambda e: e.tensor_scalar(out=bada_g, in0=bada_g, scalar1=0.5, scalar2=0.0, op0=ALU.mult, op1=ALU.add),
             reads=["bada_g"], writes=["bada_g"])
        P.op("pool", lambda e: e.tensor_scalar(out=bo_row[0:1, :], in0=bo_row[0:1, :], scalar1=2.0, scalar2=0.0, op0=ALU.mult, op1=ALU.add),
             reads=["bo_row"], writes=["bo_row"])
        for kc in range(8):
            P.op("dve", lambda e, kc=kc: e.tensor_scalar(out=sbc[:, kc, :], in0=ones_bf, scalar1=s_f[:, kc:kc + 1],
                                                         scalar2=None, op0=ALU.mult),
                 reads=["ones_bf"], writes=[("sbc", kc)])
        for nb in range(2):
            for kc in range(8):
                P.op("pe", lambda e, nb=nb, kc=kc: e.matmul(PS(6 + nb, 0, 512), lhsT=sbc[:, kc, :],
                                                            rhs=wada2[:, kc, nb * 512:(nb + 1) * 512],
                                                            start=(kc == 0), stop=(kc == 7)),
                     reads=[("sbc", kc), "wada2"], writes=[("ps", 6 + nb)])
            P.op("dve", lambda e, nb=nb: e.scalar_tensor_tensor(out=gate_bc[:, nb * 512:(nb + 1) * 512], in0=PS(6 + nb, 0, 512),
                                                                scalar=0.5, in1=bada_g[:, nb * 512:(nb + 1) * 512],
                                                                op0=ALU.mult, op1=ALU.add),
                 reads=[("ps", 6 + nb), "bada_g"], writes=[("gate_bc", nb)])
        gk = [("gate_bc", 0), ("gate_bc", 1)]
        for lt in range(NLT):
            i2 = lt % NB8
            tsl = slice(lt * 128, (lt + 1) * 128)
            P.dma("sp", lambda e, lt=lt, i2=i2: e.dma_start(out=xt[i2], in_=x[lt * 128:(lt + 1) * 128, :]), writes=[("xt", i2)])
            bank0 = (lt % 3) * 2
            for nb in range(2):
                for m in range(8):
                    P.op("pe", lambda e, nb=nb, m=m, tsl=tsl, bank0=bank0: e.matmul(
                        PS(bank0 + nb, 0, 512), lhsT=mgT[:, m, tsl], rhs=wo[:, m, nb * 512:(nb + 1) * 512],
                        start=(m == 0), stop=False), reads=["wo"], writes=[("ps", bank0 + nb)])
                P.op("pe", lambda e, nb=nb, bank0=bank0: e.matmul(
                    PS(bank0 + nb, 0, 512), lhsT=ones_bf[0:1, :], rhs=bo_row[0:1, nb * 512:(nb + 1) * 512],
                    start=False, stop=True), reads=["ones_bf", "bo_row"], writes=[("ps", bank0 + nb)])
                P.op("dve", lambda e, nb=nb, i2=i2, bank0=bank0: e.tensor_tensor(
                    out=xn[i2][:, nb * 512:(nb + 1) * 512], in0=PS(bank0 + nb, 0, 512), in1=gate_bc[:, nb * 512:(nb + 1) * 512],
                    op=ALU.mult), reads=[("ps", bank0 + nb)] + gk, writes=[("xn", i2, nb)])
            xk = [("xn", i2, 0), ("xn", i2, 1)]
            P.op("pool", lambda e, i2=i2: e.tensor_tensor(out=xn[i2], in0=xn[i2], in1=xt[i2], op=ALU.add),
                 reads=xk + [("xt", i2)], writes=xk)
            P.op("act", lambda e, i2=i2, lt=lt: e.activation(out=sqf, in_=xn[i2], func=AF.Square, accum_out=ssf[:, lt:lt + 1]),
                 reads=xk, writes=["sqf", ("ssf", lt)])
            P.op("act", lambda e, lt=lt: e.activation(out=lsf[:, lt:lt + 1], in_=ssf[:, lt:lt + 1], func=AF.Ln, scale=1.0 / D, bias=EPS),
                 reads=[("ssf", lt)], writes=[("lsf", lt)])
            P.op("act", lambda e, lt=lt: e.activation(out=rsf[:, lt:lt + 1], in_=lsf[:, lt:lt + 1], func=AF.Exp, scale=-0.5),
                 reads=[("lsf", lt)], writes=[("rsf", lt)])
            P.op("dve", lambda e, i2=i2, lt=lt: e.scalar_tensor_tensor(
                out=xn[i2], in0=xn[i2], scalar=rsf[:, lt:lt + 1], in1=gfin_bc, op0=ALU.mult, op1=ALU.mult),
                reads=xk + [("rsf", lt), "gfin"], writes=xk)
            P.dma("sp", lambda e, i2=i2, lt=lt: e.dma_start(out=y[lt * 128:(lt + 1) * 128, :], in_=xn[i2]), reads=xk)
        P.emit()
    return nc


_NC_CACHE = {}


def _core_inputs(b, x, c, ctx, c_ctx, w_ada, b_ada, g_norm, w_in, b_in, w_conv, b_conv, g_head, w_pa, w_pb, w_out,
                 b_out, g_final):
    f = lambda a: np.ascontiguousarray(a, dtype=np.float32)
    return {
        "x": f(x[b]), "ctx": f(ctx[b]), "c": f(c[b]), "c_ctx": f(c_ctx),
        "w_ada": f(w_ada[0]), "b_ada": f(b_ada[0]), "g_norm": f(g_norm[0]), "w_in": f(w_in[0]), "b_in": f(b_in[0]),
        "w_conv": f(w_conv[0]), "b_conv": f(b_conv[0]), "g_head": f(g_head[0]), "w_pa": f(w_pa[0]), "w_pb": f(w_pb[0]),
        "w_out": f(w_out[0]), "b_out": f(b_out[0]), "g_final": f(g_final),
    }


def kernel(**inputs):
    if "nc" not in _NC_CACHE:
        _NC_CACHE["nc"] = build_program()
    nc = _NC_CACHE["nc"]
    in_maps = [_core_inputs(b, **inputs) for b in range(8)]
    res = run_bass_kernel_spmd(nc, in_maps, core_ids=list(range(8)))
    return np.stack([np.asarray(r["y"], dtype=np.float32).reshape(SEQ, D) for r in res.results], axis=0)
```
